# Optimizing a Trainium2 kernel written in Bass

```python
import jax, jax.numpy as jnp
from jax import lax
import numpy as np

D_MODEL = 2048
BATCH = 2
SEQ = 8192
DEPTH = 4
DEC_BATCH = 4
DEC_SEQ = 4096
PAST_LEN = 128

N_EVEN = (DEPTH + 1) // 2
N_ODD = DEPTH // 2
MIX_WIDTH = D_MODEL
A_WIDTH = MIX_WIDTH // 2
A_HEADS = 4
A_HEAD_DIM = A_WIDTH // A_HEADS
CHUNK = 128
B_WIDTH = MIX_WIDTH - A_WIDTH
POOL_WINDOWS = (2, 4, 8, 16)
B_GROUPS = len(POOL_WINDOWS)
B_GROUP_DIM = B_WIDTH // B_GROUPS
AB_IN = 2 * A_WIDTH + B_WIDTH
C_WIDTH = MIX_WIDTH // 2
C_KERNEL = 31
D_WIDTH = MIX_WIDTH - C_WIDTH
D_GROUPS = 4
D_GROUP_DIM = D_WIDTH // D_GROUPS
CD_IN = 2 * C_WIDTH + D_WIDTH
D_FF = 5632
FFN_KERNEL = 3
EPS = 1e-6

kernel_name = "hybrid_sgu_pool_conformer_fnet_encoder"


def rms_norm(x, g):
    xf = x.astype(jnp.float32)
    y = xf * lax.rsqrt(jnp.mean(xf * xf, axis=-1, keepdims=True) + EPS)
    return (y * g.astype(jnp.float32)).astype(x.dtype)


def layer_norm(x, g, b):
    xf = x.astype(jnp.float32)
    mu = jnp.mean(xf, axis=-1, keepdims=True)
    xc = xf - mu
    y = xc * lax.rsqrt(jnp.mean(xc * xc, axis=-1, keepdims=True) + EPS)
    return (y * g.astype(jnp.float32) + b.astype(jnp.float32)).astype(x.dtype)


def depthwise_conv(x, w, b):
    k = w.shape[0]
    y = lax.conv_general_dilated(
        x, w[:, None, :].astype(x.dtype), window_strides=(1,),
        padding=[(k // 2, k // 2)], dimension_numbers=("NWC", "WIO", "NWC"),
        feature_group_count=x.shape[-1])
    return y + b.astype(x.dtype)


def mixer_a(u, v, v_gain, w_s, b_s):
    u = jax.nn.gelu(u)
    v = rms_norm(jax.nn.gelu(v), v_gain)
    bsz, s, _ = v.shape
    vh = v.reshape(bsz, s // CHUNK, CHUNK, A_HEADS, A_HEAD_DIM)
    mixed = jnp.einsum("hpq,bcqhd->bcphd", w_s.astype(v.dtype), vh)
    mixed = mixed + b_s.T.astype(v.dtype)[None, None, :, :, None]
    return u * mixed.reshape(bsz, s, A_WIDTH)


def mixer_b(xb, w_g, scale):
    bsz, s, _ = xb.shape
    xf = xb.astype(jnp.float32)
    csum = jnp.concatenate(
        [jnp.zeros((bsz, 1, B_WIDTH), jnp.float32), jnp.cumsum(xf, axis=1)], axis=1)
    t = np.arange(s)
    outs = []
    for g, w in enumerate(POOL_WINDOWS):
        half = w // 2
        lo = np.maximum(t - half, 0)
        hi = np.minimum(t + half, s)
        cnt = (hi - lo).astype(np.float32)[None, :, None]
        sl = slice(g * B_GROUP_DIM, (g + 1) * B_GROUP_DIM)
        cs = csum[:, :, sl]
        outs.append((cs[:, hi] - cs[:, lo]) / cnt - xf[:, :, sl])
    pooled = jnp.stack(outs, axis=2).astype(xb.dtype)
    mixed = jnp.einsum("bsgd,gde->bsge", pooled, w_g.astype(xb.dtype)).reshape(bsz, s, B_WIDTH)
    return mixed * scale.astype(xb.dtype)


def mixer_c(h, conv_w, conv_b, ln_g, ln_b):
    a, gate = h[..., :C_WIDTH], h[..., C_WIDTH:]
    y = a * jax.nn.sigmoid(gate)
    y = depthwise_conv(y, conv_w, conv_b)
    return jax.nn.silu(layer_norm(y, ln_g, ln_b))


def mixer_d(xd):
    bsz, s, _ = xd.shape
    xf = xd.astype(jnp.float32).reshape(bsz, s, D_GROUPS, D_GROUP_DIM)
    y = jnp.real(jnp.fft.fft2(xf, axes=(1, 3), norm="ortho"))
    return y.astype(xd.dtype).reshape(bsz, s, D_WIDTH)


def conv_ffn(h, w_in, conv_w, conv_b, w_out):
    p = h @ w_in.astype(h.dtype)
    a, g = p[..., :D_FF], p[..., D_FF:]
    a = depthwise_conv(a, conv_w, conv_b)
    return (jax.nn.gelu(a) * g) @ w_out.astype(h.dtype)


def setup_inputs(seed: int = 0) -> dict:
    key = jax.random.key(seed)
    ks = jax.random.split(key, 32)
    f32 = jnp.float32
    nrm = lambda k, shape, scale: jax.random.normal(k, shape, f32) * scale
    res_scale = (2.0 * DEPTH) ** -0.5
    return {
        "x_prompt": nrm(ks[0], (BATCH, SEQ, D_MODEL), 1.0),
        "x_sample": nrm(ks[1], (DEC_BATCH, DEC_SEQ, D_MODEL), 1.0),
        "norm_mix": 1.0 + nrm(ks[2], (DEPTH, D_MODEL), 0.02),
        "norm_ffn": 1.0 + nrm(ks[3], (DEPTH, D_MODEL), 0.02),
        "norm_final": 1.0 + nrm(ks[4], (D_MODEL,), 0.02),
        "ab_w_in": nrm(ks[5], (N_EVEN, D_MODEL, AB_IN), D_MODEL ** -0.5),
        "ab_w_out": nrm(ks[6], (N_EVEN, MIX_WIDTH, D_MODEL), MIX_WIDTH ** -0.5 * res_scale),
        "a_v_gain": 1.0 + nrm(ks[7], (N_EVEN, A_WIDTH), 0.02),
        "a_w_s": nrm(ks[8], (N_EVEN, A_HEADS, CHUNK, CHUNK), CHUNK ** -0.5),
        "a_b_s": 1.0 + nrm(ks[9], (N_EVEN, A_HEADS, CHUNK), 0.02),
        "b_w_g": nrm(ks[10], (N_EVEN, B_GROUPS, B_GROUP_DIM, B_GROUP_DIM), B_GROUP_DIM ** -0.5),
        "b_scale": 1.0 + nrm(ks[11], (N_EVEN, B_WIDTH), 0.02),
        "cd_w_in": nrm(ks[12], (N_ODD, D_MODEL, CD_IN), D_MODEL ** -0.5),
        "cd_w_out": nrm(ks[13], (N_ODD, MIX_WIDTH, D_MODEL), MIX_WIDTH ** -0.5 * res_scale),
        "c_conv_w": nrm(ks[14], (N_ODD, C_KERNEL, C_WIDTH), C_KERNEL ** -0.5),
        "c_conv_b": nrm(ks[15], (N_ODD, C_WIDTH), 0.02),
        "c_ln_g": 1.0 + nrm(ks[16], (N_ODD, C_WIDTH), 0.02),
        "c_ln_b": nrm(ks[17], (N_ODD, C_WIDTH), 0.02),
        "f_w_in": nrm(ks[18], (DEPTH, D_MODEL, 2 * D_FF), D_MODEL ** -0.5),
        "f_conv_w": nrm(ks[19], (DEPTH, FFN_KERNEL, D_FF), FFN_KERNEL ** -0.5),
        "f_conv_b": nrm(ks[20], (DEPTH, D_FF), 0.02),
        "f_w_out": nrm(ks[21], (DEPTH, D_FF, D_MODEL), D_FF ** -0.5 * res_scale),
    }


def reference(x_prompt, x_sample, norm_mix, norm_ffn, norm_final,
              ab_w_in, ab_w_out, a_v_gain, a_w_s, a_b_s, b_w_g, b_scale,
              cd_w_in, cd_w_out, c_conv_w, c_conv_b, c_ln_g, c_ln_b,
              f_w_in, f_conv_w, f_conv_b, f_w_out):
    def trunk(x):
        for l in range(DEPTH):
            i = l // 2
            h = rms_norm(x, norm_mix[l])
            if l % 2 == 0:
                p = h @ ab_w_in[i].astype(h.dtype)
                u = p[..., :A_WIDTH]
                v = p[..., A_WIDTH:2 * A_WIDTH]
                xb = p[..., 2 * A_WIDTH:]
                ya = mixer_a(u, v, a_v_gain[i], a_w_s[i], a_b_s[i])
                yb = mixer_b(xb, b_w_g[i], b_scale[i])
                y = jnp.concatenate([ya, yb], axis=-1) @ ab_w_out[i].astype(h.dtype)
            else:
                p = h @ cd_w_in[i].astype(h.dtype)
                yc = mixer_c(p[..., :2 * C_WIDTH], c_conv_w[i], c_conv_b[i], c_ln_g[i], c_ln_b[i])
                yd = mixer_d(p[..., 2 * C_WIDTH:])
                y = jnp.concatenate([yc, yd], axis=-1) @ cd_w_out[i].astype(h.dtype)
            x = x + y
            h = rms_norm(x, norm_ffn[l])
            x = x + conv_ffn(h, f_w_in[l], f_conv_w[l], f_conv_b[l], f_w_out[l])
        return rms_norm(x, norm_final)

    y_prompt = trunk(x_prompt)
    y_sample = trunk(x_sample)
    return (y_prompt, y_sample)
```

```python
import contextlib
import numpy as np
import ml_dtypes
import concourse.bass as bass
import concourse.mybir as mybir
from concourse.bass_utils import run_bass_kernel_spmd

F32 = mybir.dt.float32
BF16 = mybir.dt.bfloat16
ALU = mybir.AluOpType
AF = mybir.ActivationFunctionType
AX = mybir.AxisListType

EPOCH = 30000
DMA_SEM_MAX = 30000


class Buf:
    def __init__(self, name, ap=None):
        self.name = name
        self.ap = ap
        self.writer = None
        self.readers = {}


class Sched:
    ENGS = ("pe", "act", "dve", "pool", "sp")

    def __init__(self, nc):
        self.nc = nc
        self.prog = {e: [] for e in self.ENGS}
        self.seq = {e: 0 for e in self.ENGS}
        self.known = {e: {} for e in self.ENGS}
        self.dma_cnt = {}
        self.dma_gen = {}
        self.semnames = set()
        self.stack = contextlib.ExitStack()
        self.nalloc = 0
        self.pending = {e: [] for e in self.ENGS}

    def sbuf(self, name, free, dtype, parts=128):
        t = self.stack.enter_context(self.nc.sbuf_tensor(name, [parts] + list(free), dtype))
        return t

    def psum(self, name, free, dtype=F32):
        t = self.stack.enter_context(self.nc.psum_tensor(name, [128] + list(free), dtype))
        return t

    def _deps(self, reads, writes, skip_dsem=None):
        deps = []
        for b in reads:
            if b.writer is not None:
                deps.append(b.writer)
        for b in writes:
            if b.writer is not None:
                if not (skip_dsem is not None and b.writer[0] == skip_dsem):
                    deps.append(b.writer)
            for k, v in b.readers.items():
                deps.append((k, v))
        return deps

    def _filter(self, e, deps, n, is_dma=False):
        waits = []
        kn = self.known[e]
        for (k, v) in deps:
            if k[0] == "E" and k[1] == e and not is_dma:
                if e in ("pe", "sp"):
                    continue
                if n - v > 2:
                    continue
            if kn.get(k, 0) >= v:
                continue
            kn[k] = v
            waits.append((k, v))
        return waits

    def op(self, e, fn, reads=(), writes=()):
        n = self.seq[e] + 1
        self.seq[e] = n
        deps = self._deps(reads, writes)
        waits = self.pending[e] + self._filter(e, deps, n)
        self.pending[e] = []
        ev = (("E", e), n)
        self.prog[e].append((waits, fn, ev))
        for b in reads:
            if b.readers.get(ev[0], 0) < n:
                b.readers[ev[0]] = n
        for b in writes:
            b.writer = ev
            b.readers = {}
        return ev

    def dma(self, q, fn, reads, writes, semkey):
        gen = self.dma_gen.get(semkey, 0)
        cnt = self.dma_cnt.get((semkey, gen), 0)
        if cnt + 16 > DMA_SEM_MAX:
            gen += 1
            self.dma_gen[semkey] = gen
            cnt = 0
        cnt += 16
        self.dma_cnt[(semkey, gen)] = cnt
        k = ("D", semkey, gen)
        deps = self._deps(reads, writes, skip_dsem=k)
        waits = self.pending[q] + self._filter(q, deps, 0, is_dma=True)
        self.pending[q] = []
        ev = (k, cnt)
        self.prog[q].append((waits, fn, ev))
        for b in reads:
            if b.readers.get(k, 0) < cnt:
                b.readers[k] = cnt
        for b in writes:
            b.writer = ev
            b.readers = {}
        return ev

    def barrier(self):
        evs = []
        for e in self.ENGS:
            if self.seq[e] > 0:
                evs.append((("E", e), self.seq[e]))
        for (semkey, gen), cnt in self.dma_cnt.items():
            evs.append((("D", semkey, gen), cnt))
        for e in self.ENGS:
            kn = self.known[e]
            for (k, v) in evs:
                if k[0] == "E" and k[1] == e:
                    continue
                if kn.get(k, 0) >= v:
                    continue
                kn[k] = v
                self.pending[e].append((k, v))

    def _semname(self, k, v=None):
        if k[0] == "E":
            return "e_%s_%d" % (k[1], (v - 1) // EPOCH)
        return "d_%s_%d" % (k[1], k[2])

    def _semval(self, k, v):
        if k[0] == "E":
            return (v - 1) % EPOCH + 1
        return v

    def run(self):
        nc = self.nc
        names = set()
        for e in self.ENGS:
            for (waits, fn, ev) in self.prog[e]:
                names.add(self._semname(ev[0], ev[1]))
        sems = {}
        for nm in sorted(names):
            sems[nm] = self.stack.enter_context(nc.semaphore(nm))
        self.nsem = len(sems)
        block = self.stack.enter_context(nc.Block())
        engmap = {"pe": block.tensor, "act": block.scalar, "dve": block.vector,
                  "pool": block.gpsimd, "sp": block.sync}
        final_waits = []
        for nm in sorted(names):
            pass

        def make(e):
            prog = self.prog[e]

            def body(eng):
                for (waits, fn, ev) in prog:
                    for (k, v) in waits:
                        eng.wait_ge(sems[self._semname(k, v)], self._semval(k, v))
                    ins = fn(eng)
                    k, v = ev
                    if k[0] == "E":
                        ins.then_inc(sems[self._semname(k, v)], 1)
                    else:
                        ins.then_inc(sems[self._semname(k, v)], 16)
                for (k, v) in self.pending[e]:
                    eng.wait_ge(sems[self._semname(k, v)], self._semval(k, v))
                if e == "sp":
                    for (semkey, gen), cnt in self.dma_cnt.items():
                        eng.wait_ge(sems["d_%s_%d" % (semkey, gen)], cnt)
                    for e2 in self.ENGS:
                        if e2 != "sp" and self.seq[e2] > 0:
                            n = self.seq[e2]
                            eng.wait_ge(sems[self._semname(("E", e2), n)], self._semval(("E", e2), n))
            return body

        for e in self.ENGS:
            if self.prog[e] or e == "sp":
                engmap[e](make(e))
        self.stack.close()


import os
DBG_FLUSH = int(os.environ.get('DBG_FLUSH', '0'))
DBG_NOPIPE = int(os.environ.get('DBG_NOPIPE', '0'))
H = 16
T = 512
W = T + 2 * H
EPS = 1e-6
POOLW = (2, 4, 8, 16)


class Cfg:
    def __init__(self, D=2048, DFF=5632, SC=4096, depth=4):
        self.D = D; self.DFF = DFF; self.SC = SC; self.depth = depth
        self.KD = D // 128; self.KF = DFF // 128
        self.AW = D // 2; self.KA = self.AW // 128
        self.NT = SC // T
        self.XW = SC + 2 * H
        self.N2 = 2 * SC // 128
        self.NE = (depth + 1) // 2; self.NO = depth // 2
        self.CK = 31
        c = 0
        self.v_nmix = []; self.v_nffn = []
        for l in range(depth):
            self.v_nmix.append(c); c += self.KD
            self.v_nffn.append(c); c += self.KD
        self.v_nfin = c; c += self.KD
        self.v_bscale = []
        for i in range(self.NE):
            self.v_bscale.append(c); c += self.KA
        self.v_ccw = []; self.v_ccb = []; self.v_lng = []; self.v_lnb = []
        for i in range(self.NO):
            self.v_ccw.append(c); c += self.CK * self.KA
            self.v_ccb.append(c); c += self.KA
            self.v_lng.append(c); c += self.KA
            self.v_lnb.append(c); c += self.KA
        self.v_fw = []; self.v_fb = []
        for l in range(depth):
            self.v_fw.append(c); c += 3 * self.KF
            self.v_fb.append(c); c += self.KF
        self.v_mask = c; c += 2
        self.NV = c


class Arena:
    def __init__(self, S, nfloats):
        self.t = S.sbuf("arena", [nfloats], F32)
        self.n = nfloats
        self.reset()

    def reset(self):
        self.hw = max(getattr(self, "hw", 0), getattr(self, "o", 0))
        self.o = 0; self.cnt = 0; self.replaying = False; self.log = None

    def mark(self):
        self.log = []; self.replaying = False

    def begin_iter(self, first):
        if first:
            self.log = []; self.replaying = False
        else:
            self.replaying = True; self.ri = 0

    def _replay(self, n, kind):
        b, meta = self.log[self.ri]
        self.ri += 1
        assert meta == (n, kind), (meta, n, kind)
        return b

    def f32(self, n, name="t", parts=128):
        if self.replaying:
            return self._replay(n, "f")
        assert self.o + n <= self.n, ("arena overflow", name, self.o, n, self.n)
        ap = self.t[0:parts, self.o:self.o + n]
        self.o += n
        self.cnt += 1
        b = Buf("%s%d" % (name, self.cnt), ap)
        if self.log is not None:
            self.log.append((b, (n, "f")))
        return b

    def bf(self, n, name="t", parts=128):
        if self.replaying:
            return self._replay(n, "b")
        nf = (n + 1) // 2
        assert self.o + nf <= self.n, ("arena overflow", name, self.o, nf, self.n)
        ap = self.t[0:parts, self.o:self.o + nf].bitcast(BF16)[:, 0:n]
        self.o += nf
        self.cnt += 1
        b = Buf("%s%d" % (name, self.cnt), ap)
        if self.log is not None:
            self.log.append((b, (n, "b")))
        return b


def build(cfg, use_cc=True, debug_out=None):
    nc = bass.Bass("TRN2", target_bir_lowering=False)
    c = cfg
    D, DFF, SC, KD, KF, KA, AW, NT, XW, N2 = c.D, c.DFF, c.SC, c.KD, c.KF, c.KA, c.AW, c.NT, c.XW, c.N2
    NE, NO = c.NE, c.NO

    def din(name, shape, dt=F32):
        return nc.dram_tensor(name, list(shape), dt, kind="ExternalInput").ap()

    x_in = din("x", [SC, D])
    vecs = din("vecs", [128, c.NV])
    ident_in = din("ident", [128, 128])
    ab_w_in = din("ab_w_in", [NE, D, 3 * AW]); ab_w_out = din("ab_w_out", [NE, D, D])
    cd_w_in = din("cd_w_in", [NO, D, 3 * AW]); cd_w_out = din("cd_w_out", [NO, D, D])
    f_w_in = din("f_w_in", [c.depth, D, 2 * DFF]); f_w_out = din("f_w_out", [c.depth, DFF, D])
    a_v_gain = din("a_v_gain", [NE, AW]); a_w_sT = din("a_w_sT", [NE, 4, 128, 128]); a_b_s = din("a_b_s", [NE, 512])
    b_w_g = din("b_w_g", [NE, 4, AW // 4, AW // 4])
    invcnt = din("invcnt", [4, SC])
    Gc = din("Gc", [128, N2 * 2 * 128], BF16)
    F2c = din("F2c", [2 * N2, N2], BF16)
    GDm = AW // 4; NCC = GDm // 128
    FCc = din("FCc", [128, NCC * 2 * GDm], BF16)
    y_out = nc.dram_tensor("y", [SC, D], F32, kind="ExternalOutput").ap()

    def dscr(name, shape, dt=F32):
        return nc.dram_tensor(name, list(shape), dt).ap()

    xbuf = [dscr("xa", [D, XW]), dscr("xb", [D, XW])]
    RCH = min(SC, 1024); NCH = SC // RCH
    xd_loc = dscr("xd_loc", [SC, AW], BF16)
    xd_pair = dscr("xd_pair", [NCH, 2 * RCH, AW], BF16)
    Bs = dscr("Bs", [128, N2, 2, AW], BF16)
    ydT = dscr("ydT", [AW, SC], BF16)
    exin = dscr("exin", [D, 2 * H]); exout = dscr("exout", [2 * D, 2 * H])
    WS = [{"mi": dscr("wmi%d" % q, [3 * KA, 128, KD, 128], BF16), "mo": dscr("wmo%d" % q, [KD, 128, KD, 128], BF16),
           "fi": dscr("wfi%d" % q, [2 * KF, 128, KD, 128], BF16), "fo": dscr("wfo%d" % q, [KD, 128, KF, 128], BF16)} for q in range(2)]
    WC = dict(WS[0])

    S = Sched(nc)
    A = Arena(S, 51500)
    vecs_t = Buf("vecs", S.sbuf("vecs_t", [c.NV], F32))
    ident = Buf("ident", S.sbuf("ident_t", [128], F32))
    ones_b = Buf("ones", S.sbuf("ones_t", [128], BF16))
    identb = Buf("identb", S.sbuf("identb_t", [128], BF16))
    PSUM = S.psum("psum_all", [8 * 512], F32)
    banks = [Buf("bank%d" % i, PSUM[:, i * 512:(i + 1) * 512]) for i in range(8)]
    st = {"sb": 0, "pr": 0}

    def bank():
        b = banks[4 + st["sb"] % 4]; st["sb"] += 1
        return b

    def pair():
        i = (st["pr"] % 2) * 2; st["pr"] += 1
        return (banks[i], banks[i + 1]), PSUM[:, i * 512:(i + 2) * 512]

    def MM(out, lhsT, rhs, start, stop, reads, writes):
        S.op("pe", lambda e: e.matmul(out, lhsT=lhsT, rhs=rhs, start=start, stop=stop), reads, writes)

    def ACTF(out, in_, func, reads, writes, **kw):
        S.op("act", lambda e: e.activation(out=out, in_=in_, func=func, **kw), reads, writes)

    def TT(eng, out, in0, in1, op, reads, writes):
        S.op(eng, lambda e: e.tensor_tensor(out=out, in0=in0, in1=in1, op=op), reads, writes)

    def TS(eng, out, in0, s1, s2, op0, op1, reads, writes):
        if s2 is None:
            S.op(eng, lambda e: e.tensor_scalar(out=out, in0=in0, scalar1=s1, scalar2=None, op0=op0), reads, writes)
        else:
            S.op(eng, lambda e: e.tensor_scalar(out=out, in0=in0, scalar1=s1, scalar2=s2, op0=op0, op1=op1), reads, writes)

    def STT(out, in0, scalar, in1, op0, op1, reads, writes):
        S.op("dve", lambda e: e.scalar_tensor_tensor(out=out, in0=in0, scalar=scalar, in1=in1, op0=op0, op1=op1), reads, writes)

    def CP(eng, out, in_, reads, writes):
        if eng == "act":
            S.op("act", lambda e: e.activation(out=out, in_=in_, func=AF.Copy), reads, writes)
        else:
            S.op(eng, lambda e: e.tensor_copy(out=out, in_=in_), reads, writes)

    def DMA(out, in_, reads, writes, semkey, q="sp"):
        S.dma(q, lambda e: e.dma_start(out=out, in_=in_), reads, writes, semkey)

    def vcol(col, n=1):
        return vecs_t.ap[:, col:col + n]

    rr = {"i": 0}

    def alt(engs=("act", "dve")):
        rr["i"] += 1
        return engs[rr["i"] % len(engs)]

    def phase_end():
        import sys
        pass
        if cv_box and cv_box[0].st is not None:
            cv_box[0]._store()
            cv_box[0].st = None
        S.barrier()
        A.reset()

    cv_box = []
    must_box = [0]

    xtile = [[Buf("x%d_%d" % (b, i)) for i in range(NT)] for b in range(2)]
    xhalo = [[Buf("xh%d_%d" % (b, i)) for i in range(2)] for b in range(2)]
    d_xd_loc = Buf("xd_loc"); d_xd_pair = Buf("xd_pair"); d_Bs = Buf("Bs"); d_ydT = Buf("ydT")
    class DW:
        def __init__(self, nm):
            self.nm = nm; self.b = {}

        def get(self, n, k0):
            if (n, k0) not in self.b:
                self.b[(n, k0)] = Buf("dw_%s_%d_%d" % (self.nm, n, k0))
            return self.b[(n, k0)]

        def chunk(self, n):
            return [b for (nn, k0), b in self.b.items() if nn == n]

    DWS = [{k: DW(k + str(q)) for k in ("mi", "mo", "fi", "fo")} for q in range(2)]
    d_w = dict(DWS[0])

    def set_layer_weights(l):
        WC.update(WS[l % 2]); d_w.update(DWS[l % 2])
    d_ex_in = Buf("exin"); d_ex_out = Buf("exout")

    def xreads(b, i):
        r = [xtile[b][i]]
        r.append(xtile[b][i - 1] if i > 0 else xhalo[b][0])
        r.append(xtile[b][i + 1] if i < NT - 1 else xhalo[b][1])
        return r

    def xview(b):
        return xbuf[b].rearrange("(k p) w -> p k w", p=128)

    DMA(vecs_t.ap[:], vecs[:, :], [], [vecs_t], "c_vecs")
    DMA(ident.ap[:], ident_in[:, :], [], [ident], "c_ident")
    S.op("dve", lambda e: e.memset(ones_b.ap[:], 1.0), [], [ones_b])
    S.op("dve", lambda e: e.tensor_copy(out=identb.ap[:], in_=ident.ap[:]), [ident], [identb])
    zt = A.f32(KD * H, "zt")
    S.op("dve", lambda e: e.memset(zt.ap[:], 0.0), [], [zt])
    for b in range(2):
        for side in range(2):
            c0 = 0 if side == 0 else H + SC
            DMA(xview(b)[:, :, c0:c0 + H], zt.ap.rearrange("p (k h) -> p k h", k=KD), [zt], [xhalo[b][side]], "zt_st")

    def load_xt(b, i, width, name="xt", slot=None, key=None):
        hh = (width - T) // 2
        xt = slot if slot is not None else A.f32(KD * width, name)
        if key is not None:
            name = key
        v = xt.ap.rearrange("p (k w) -> p k w", k=KD)
        c0 = T * i + H - hh
        DMA(v, xview(b)[:, :, c0:c0 + width], xreads(b, i) if hh > 0 else [xtile[b][i]], [xt], "ld_" + name)
        return xt, v

    def norm_bufs(width, out_dt=BF16, name="h"):
        return {"sq": [A.bf(width, "sq") for _ in range(2)], "rs": A.f32(width, "rstd"),
                "hs": [A.bf(width, name) if out_dt == BF16 else A.f32(width, name) for _ in range(KD)]}

    def rmsnorm(xt, xv, width, gcol, out_dt=BF16, name="h", nb=None):
        (b0, b1), pp = pair()
        sq = nb["sq"] if nb else [A.bf(width, "sq") for _ in range(2)]
        scale = float(D) ** -0.5
        for k in range(KD):
            sb = sq[k % 2]
            ACTF(sb.ap[:], xv[:, k, :], AF.Square, [xt], [sb], scale=scale)
            w0 = min(width, 512)
            MM(pp[:, 0:w0], ones_b.ap[:], sb.ap[:, 0:w0], k == 0, k == KD - 1, [ones_b, sb], [b0])
            if width > 512:
                MM(pp[:, 512:width], ones_b.ap[:], sb.ap[:, 512:width], k == 0, k == KD - 1, [ones_b, sb], [b1])
        rs = nb["rs"] if nb else A.f32(width, "rstd")
        ACTF(rs.ap[:], pp[:, 0:width], AF.Sqrt, [b0, b1, eps_t], [rs], bias=eps_t.ap[:, 0:1], scale=1.0)
        S.op("dve", lambda e: e.reciprocal(out=rs.ap[:], in_=rs.ap[:]), [rs], [rs])
        hs = []
        for k in range(KD):
            hb = nb["hs"][k] if nb else (A.bf(width, name) if out_dt == BF16 else A.f32(width, name))
            STT(hb.ap[:], xv[:, k, :], vcol(gcol + k), rs.ap[:], ALU.mult, ALU.mult, [xt, rs, vecs_t], [hb])
            hs.append(hb)
        return hs

    eps_t = Buf("eps", S.sbuf("eps_t", [1], F32))
    S.op("dve", lambda e: e.memset(eps_t.ap[:], EPS), [], [eps_t])

    def load_w(scr, dbuf, n, kk, slot, name):
        DMA(slot.ap.rearrange("p (k c) -> p k c", k=kk), scr[n], dbuf.chunk(n), [slot], "ldw_" + name)
        return slot.ap.rearrange("p (k c) -> p k c", k=kk)

    class WStream:
        def __init__(self, slots, tag, hold=1):
            self.slots = slots; self.tag = tag; self.NS = len(slots); self.hold = hold
            self.seq = []; self.issued = 0; self.views = {}; self.pos = 0

        def plan(self, items):
            self.seq += items

        def next(self):
            j = self.pos; self.pos += 1
            while self.issued < min(len(self.seq), j + self.NS - self.hold + 1):
                q = self.issued
                key, n, kk = self.seq[q]
                slot = self.slots[q % self.NS]
                v = load_w(WC[key], d_w[key], n, kk, slot, "%s%d" % (self.tag, q % self.NS))
                self.views[q] = (v, slot)
                self.issued += 1
            return self.views.pop(j)

    class Converter:
        KH = 4

        def __init__(self):
            self.queue = []
            self.pendq = []
            self.st = None
            self.it = 0
            self.rate = 0.0
            self.acc = 0.0

        def add(self, src, K, N, scr, dbuf):
            kk = K // 128
            sv = src.rearrange("(k p) n -> p k n", p=128)
            for n in range(N // 128):
                for k0 in range(0, kk, self.KH):
                    kh = min(self.KH, kk - k0)
                    self.queue.append((sv, scr, dbuf.get(n, k0), n, k0, kh))

        def attach(self, nsteps=0, eng="pool", frac=1.0, must=0, nst=4):
            self.NST = nst
            self.st = [(A.f32(self.KH * 128, "cvf"), A.bf(self.KH * 128, "cvb")) for _ in range(self.NST)]
            want = max(frac * len(self.queue), min(must, len(self.queue)))
            self.rate = (want / float(nsteps) * 1.1) if nsteps else 0.0
            self.acc = 0.0
            self.eng = eng
            self.pendq = []

        def _store(self, all_=True):
            while self.pendq and (all_ or len(self.pendq) >= self.NST - 1):
                (bt, scr, db, n, k0, kh, q) = self.pendq.pop(0)
                DMA(scr[n, :, k0:k0 + kh, :], bt.ap[:, 0:kh * 128].rearrange("p (k c) -> p k c", k=kh), [bt], [db], "cvs%d" % q, q="act")

        def one(self):
            self._store(all_=False)
            if not self.queue:
                self._store(all_=True)
                return
            (sv, scr, db, n, k0, kh) = self.queue.pop(0)
            q = self.it % self.NST; self.it += 1
            f, bt = self.st[q]
            DMA(f.ap[:, 0:kh * 128].rearrange("p (k c) -> p k c", k=kh), sv[:, k0:k0 + kh, n * 128:(n + 1) * 128], [], [f], "cvl%d" % q, q="act")
            CP(self.eng, bt.ap[:, 0:kh * 128], f.ap[:, 0:kh * 128], [f], [bt])
            self.pendq.append((bt, scr, db, n, k0, kh, q))

        def pump(self):
            self.acc += self.rate
            while self.acc >= 1.0:
                self.acc -= 1.0
                self.one()

        def flush(self, leave=0):
            while len(self.queue) > leave:
                self.one()
            self._store(all_=True)

    cv = Converter()
    cv_box.append(cv)

    def add_mix(l):
        i = l // 2; q = l % 2
        if l % 2 == 0:
            cv.add(ab_w_in[i], D, 3 * AW, WS[q]["mi"], DWS[q]["mi"]); cv.add(ab_w_out[i], D, D, WS[q]["mo"], DWS[q]["mo"])
        else:
            cv.add(cd_w_in[i], D, 3 * AW, WS[q]["mi"], DWS[q]["mi"]); cv.add(cd_w_out[i], D, D, WS[q]["mo"], DWS[q]["mo"])

    def add_ffn(l):
        q = l % 2
        cv.add(f_w_in[l], D, 2 * DFF, WS[q]["fi"], DWS[q]["fi"]); cv.add(f_w_out[l], DFF, D, WS[q]["fo"], DWS[q]["fo"])

    def out_res(xv, m, hh, ps_ap, psb, dstb, i, rd):
        ox = oxs[m % 2]
        TT("dve", ox.ap[:], xv[:, m, hh:hh + T], ps_ap, ALU.add, rd + [psb], [ox])
        DMA(xview(dstb)[:, m, H + T * i:H + T * i + T], ox.ap[:], [ox], [xtile[dstb][i]], "ox%d" % (m % 2))

    xin = [A.f32(4 * D, "xin") for _ in range(1)]
    xo = [A.f32(T, "xo") for _ in range(4)]
    for i in range(NT):
        xi = xin[0]
        xiv = xi.ap.rearrange("p (s d) -> p s d", s=4)
        DMA(xiv, x_in[T * i:T * (i + 1), :].rearrange("(s p) d -> p s d", p=128), [], [xi], "ld_xin")
        for k in range(KD):
            bk = bank()
            for s_ in range(4):
                S.op("pe", (lambda o, i_: lambda e: e.transpose(o, i_, ident.ap[:]))(bk.ap[:, s_ * 128:(s_ + 1) * 128], xiv[:, s_, k * 128:(k + 1) * 128]), [xi, ident], [bk])
            o = xo[k % 4]
            CP(alt(), o.ap[:], bk.ap[:], [bk], [o])
            DMA(xview(0)[:, k, H + T * i:H + T * (i + 1)], o.ap[:], [o], [xtile[0][i]], "st_xo%d" % (k % 4))
    phase_end()
    cur = 0

    def ffn_phase(l, src, dst):
        A.reset()
        acts = [A.bf(T, "act") for _ in range(KF)]
        w1s = WStream([A.bf(KD * 128, "w1") for _ in range(4)], "w1", hold=2)
        NWO = 3
        wos_ = WStream([A.bf(KF * 128, "wo") for _ in range(NWO)], "wo")
        for i in range(NT):
            for cc in range(KF):
                w1s.plan([("fi", cc, KD), ("fi", KF + cc, KD)])
            wos_.plan([("fo", m, KF) for m in range(KD)])
        cvs = [A.f32(T, "cv") for _ in range(2)]
        oxs[:] = [A.f32(T, "ox") for _ in range(2)]
        xts = [A.f32(KD * W, "xt") for _ in range(2)]
        nb = norm_bufs(W)
        cv.attach(NT * (KF + KD), eng="pool", frac=1.0, nst=3)
        fw = c.v_fw[l]; fb = c.v_fb[l]
        xt, xv = load_xt(src, 0, W, slot=xts[0], key="xt0")
        hs = rmsnorm(xt, xv, W, c.v_nffn[l], nb=nb)
        wi = 0
        for i in range(NT):
            nxt = None
            if DBG_NOPIPE and i > 0:
                xt, xv = load_xt(src, i, W, slot=xts[i % 2], key="xt%d" % (i % 2))
                hs = rmsnorm(xt, xv, W, c.v_nffn[l], nb=nb)
            for cc in range(KF):
                wa, wa_slot = w1s.next()
                wg, wg_slot = w1s.next()
                if cc == min(2, KF - 1) and i + 1 < NT and not DBG_NOPIPE:
                    nxt = load_xt(src, i + 1, W, slot=xts[(i + 1) % 2], key="xt%d" % ((i + 1) % 2))
                (b0, b1), pa = pair()
                pg = bank()
                for k in range(KD):
                    MM(pa[:, 0:512], wa[:, k, :], hs[k].ap[:, 0:512], k == 0, k == KD - 1, [wa_slot, hs[k]], [b0])
                    MM(pa[:, 512:W], wa[:, k, :], hs[k].ap[:, 512:W], k == 0, k == KD - 1, [wa_slot, hs[k]], [b1])
                for k in range(KD):
                    MM(pg.ap[:], wg[:, k, :], hs[k].ap[:, H:H + T], k == 0, k == KD - 1, [wg_slot, hs[k]], [pg])
                cvt_ = cvs[cc % 2]; gl = cvt_
                ACTF(cvt_.ap[:], pa[:, H:H + T], AF.Identity, [b0, b1, vecs_t], [cvt_], scale=vcol(fw + 1 * KF + cc), bias=vcol(fb + cc))
                STT(cvt_.ap[:], pa[:, H - 1:H - 1 + T], vcol(fw + 0 * KF + cc), cvt_.ap[:], ALU.mult, ALU.add, [b0, b1, cvt_, vecs_t], [cvt_])
                STT(cvt_.ap[:], pa[:, H + 1:H + 1 + T], vcol(fw + 2 * KF + cc), cvt_.ap[:], ALU.mult, ALU.add, [b0, b1, cvt_, vecs_t], [cvt_])
                ACTF(gl.ap[:], cvt_.ap[:], AF.Gelu_apprx_tanh, [cvt_], [gl])
                TT("dve", acts[cc].ap[:], gl.ap[:], pg.ap[:], ALU.mult, [gl, pg], [acts[cc]])
                cv.pump()
            wo_first = wos_.next()
            xt_cur, xv_cur = xt, xv
            if nxt is not None:
                xt, xv = nxt
                hs = rmsnorm(xt, xv, W, c.v_nffn[l], nb=nb)
            for m in range(KD):
                wo, wo_slot = wo_first if m == 0 else wos_.next()
                po = bank()
                for cc in range(KF):
                    MM(po.ap[:], wo[:, cc, :], acts[cc].ap[:], cc == 0, cc == KF - 1, [wo_slot, acts[cc]], [po])
                out_res(xv_cur, m, H, po.ap[:], po, dst, i, [xt_cur])
                cv.pump()
        phase_end()

    oxs = [None, None]

    def cvt_flush_phase(leave=0):
        if len(cv.queue) <= leave:
            return
        A.reset()
        cv.attach(0)
        cv.flush(leave)
        phase_end()

    def even_phase(l, src, dst):
        A.reset()
        i_ = l // 2
        GD = AW // 4
        NDC = GD // 128
        wsb = A.bf(4 * 128, "wsb")
        vg = A.f32(AW, "vg")
        bsb = A.bf(512, "bsb"); on2 = A.bf(128, "on2")
        wgb = A.bf(4 * NDC * GD, "wgb")
        m0 = A.o
        wsf = A.f32(4 * 128, "wsf")
        DMA(wsf.ap.rearrange("q (h p) -> q h p", h=4), a_w_sT[i_].rearrange("h q p -> q h p"), [], [wsf], "c_wsf")
        CP("dve", wsb.ap[:], wsf.ap[:], [wsf], [wsb])
        DMA(vg.ap[:], a_v_gain[i_, :].partition_broadcast(128), [], [vg], "c_vg")
        bsf = A.f32(512, "bsf"); bs2 = A.f32(512, "bs2")
        S.op("pool", lambda e: e.memset(bsf.ap[:], 0.0), [], [bsf])
        S.op("pool", lambda e: e.memset(bs2.ap[:], 0.0), [], [bs2])
        S.op("pool", lambda e: e.memset(on2.ap[:], 0.0), [], [on2])
        S.op("pool", lambda e: e.memset(on2.ap[0:2, :], 1.0), [], [on2])
        DMA(bsf.ap[0:1, :], a_b_s[i_:i_ + 1, :], [bsf], [bsf], "c_bsf")
        DMA(bsf.ap[1:2, :], a_b_s[i_:i_ + 1, :], [bsf], [bsf], "c_bsf")
        CP("dve", bsb.ap[:], bsf.ap[:], [bsf], [bsb])
        TT("dve", bs2.ap[:], bsf.ap[:], bsb.ap[:], ALU.subtract, [bsf, bsb], [bs2])
        bs3 = A.bf(512, "bs3")
        CP("dve", bs3.ap[:], bs2.ap[:], [bs2], [bs3])
        DMA(bsb.ap[1:2, :], bs3.ap[0:1, :], [bs3, bsb], [bsb], "c_bsb")
        wgf = A.f32(4 * NDC * GD, "wgf")
        DMA(wgf.ap.rearrange("p (g dc e) -> p g dc e", g=4, dc=NDC), b_w_g[i_].rearrange("g (dc p) e -> p g dc e", p=128), [], [wgf], "c_wgf")
        CP("pool", wgb.ap[:], wgf.ap[:], [wgf], [wgb])
        wgv = wgb.ap.rearrange("p (g dc e) -> p g dc e", g=4, dc=NDC)
        S.barrier()
        A.o = m0
        wv = A.bf(KA * KD * 128, "wv")
        wvv = wv.ap.rearrange("p (n k c) -> p n k c", n=KA, k=KD)
        for n in range(KA):
            DMA(wvv[:, n, :, :], WC["mi"][KA + n], d_w["mi"].chunk(KA + n), [wv], "c_wv")
        wst = WStream([A.bf(KD * 128, "ws") for _ in range(4)], "ws")
        for i in range(NT):
            wst.plan([("mi", ch, KD) for ch in range(KA)] + [("mi", 2 * KA + ch, KD) for ch in range(KA)] + [("mo", m, KD) for m in range(KD)])
        oxs[:] = [A.f32(T, "ox") for _ in range(2)]
        ic = [A.f32(T, "ic") for _ in range(2)]
        cv.attach(NT * (2 * KA + KD), eng="act", frac=0.35, must=must_box[0])
        A.mark()
        wi = 0
        for i in range(NT):
            A.begin_iter(i == 0)
            xt, xv = load_xt(src, i, W)
            hs = rmsnorm(xt, xv, W, c.v_nmix[l])
            us = []
            for ch in range(KA):
                wu, sl = wst.next()
                bk = bank()
                for k in range(KD):
                    MM(bk.ap[:], wu[:, k, :], hs[k].ap[:, H:H + T], k == 0, k == KD - 1, [sl, hs[k]], [bk])
                u = A.bf(T, "u")
                ACTF(u.ap[:], bk.ap[:], AF.Gelu_apprx_tanh, [bk], [u])
                us.append(u)
                cv.pump()
            pbs = []
            xbs_ = [A.f32(W, "xb") for _ in range(2)]
            t1s = [A.f32(W, "t1") for _ in range(2)]
            t2s = [A.f32(W, "t2") for _ in range(2)]
            for ch in range(KA):
                wx, sl = wst.next()
                (b0, b1), pp = pair()
                for k in range(KD):
                    MM(pp[:, 0:512], wx[:, k, :], hs[k].ap[:, 0:512], k == 0, k == KD - 1, [sl, hs[k]], [b0])
                    MM(pp[:, 512:W], wx[:, k, :], hs[k].ap[:, 512:W], k == 0, k == KD - 1, [sl, hs[k]], [b1])
                xb_ = xbs_[ch % 2]
                CP("act", xb_.ap[:], pp[:, 0:W], [b0, b1], [xb_])
                g = ch // (KA // 4)
                w_ = POOLW[g]
                if ch % (KA // 4) == 0:
                    DMA(ic[g % 2].ap[:], invcnt[g, T * i:T * (i + 1)].partition_broadcast(128), [], [ic[g % 2]], "ic%d" % (g % 2))
                t1 = t1s[ch % 2]; t2 = t2s[ch % 2]
                TT("pool", t1.ap[:, 0:W - 1], xb_.ap[:, 0:W - 1], xb_.ap[:, 1:W], ALU.add, [xb_], [t1])
                curb = t1; ln = W - 1; step = 2; other = t2
                while step < w_:
                    TT("pool", other.ap[:, 0:ln - step], curb.ap[:, 0:ln - step], curb.ap[:, step:ln], ALU.add, [curb], [other])
                    curb, other = other, curb
                    ln -= step; step *= 2
                st0 = H - w_ // 2
                TT("pool", other.ap[:, 0:T], curb.ap[:, st0:st0 + T], ic[g % 2].ap[:], ALU.mult, [curb, ic[g % 2]], [other])
                pb = A.bf(T, "pb")
                TT("pool", pb.ap[:], other.ap[:, 0:T], xb_.ap[:, H:H + T], ALU.subtract, [other, xb_], [pb])
                pbs.append(pb)
                cv.pump()
            vns = []
            NH = AW // 512
            vgels = [A.f32(AW, "vgel") for _ in range(2)]
            vsqs = [A.bf(AW, "vsq") for _ in range(2)]
            for sb_ in range(4):
                vgel = vgels[sb_ % 2]
                for hf in range(NH):
                    bk = bank()
                    for k in range(KD):
                        MM(bk.ap[:], hs[k].ap[:, H + 128 * sb_:H + 128 * (sb_ + 1)], wvv[:, hf * 4:(hf + 1) * 4, k, :], k == 0, k == KD - 1, [wv, hs[k]], [bk])
                    ACTF(vgel.ap[:, hf * 512:(hf + 1) * 512], bk.ap[:], AF.Gelu_apprx_tanh, [bk], [vgel])
                vsq = vsqs[sb_ % 2]
                ss = A.f32(2, "ss")
                ACTF(vsq.ap[:], vgel.ap[:], AF.Square, [vgel], [vsq], scale=float(AW) ** -0.5)
                S.op("dve", (lambda o, i_: lambda e: e.reduce_sum(out=o, in_=i_, axis=AX.X))(ss.ap[:, 0:1], vsq.ap[:]), [vsq], [ss])
                ACTF(ss.ap[:, 1:2], ss.ap[:, 0:1], AF.Sqrt, [ss, eps_t], [ss], bias=eps_t.ap[:, 0:1], scale=1.0)
                S.op("dve", (lambda o, i_: lambda e: e.reciprocal(out=o, in_=i_))(ss.ap[:, 0:1], ss.ap[:, 1:2]), [ss], [ss])
                vn = A.bf(AW, "vn")
                STT(vn.ap[:], vgel.ap[:], ss.ap[:, 0:1], vg.ap[:], ALU.mult, ALU.mult, [vgel, ss, vg], [vn])
                vns.append(vn)
            ys = []
            for ch in range(KA):
                hd = ch // (KA // 4)
                bk = bank()
                for sb_ in range(4):
                    o = bk.ap[:, 128 * sb_:128 * (sb_ + 1)]
                    MM(o, vns[sb_].ap[:, ch * 128:(ch + 1) * 128], wsb.ap[:, hd * 128:(hd + 1) * 128], True, False, [vns[sb_], wsb], [bk])
                    MM(o, on2.ap[:], bsb.ap[:, hd * 128:(hd + 1) * 128], False, True, [on2, bsb], [bk])
                ya = A.bf(T, "ya")
                TT("dve", ya.ap[:], us[ch].ap[:], bk.ap[:], ALU.mult, [us[ch], bk], [ya])
                ys.append(ya)
            for ech in range(KA):
                g = ech // NDC; eh = ech % NDC
                bk = bank()
                for dc in range(NDC):
                    MM(bk.ap[:], wgv[:, g, dc, eh * 128:(eh + 1) * 128], pbs[g * NDC + dc].ap[:], dc == 0, dc == NDC - 1, [wgb, pbs[g * NDC + dc]], [bk])
                yb = A.bf(T, "yb")
                ACTF(yb.ap[:], bk.ap[:], AF.Copy, [bk, vecs_t], [yb], scale=vcol(c.v_bscale[i_] + ech))
                ys.append(yb)
            for m in range(KD):
                wo, sl = wst.next()
                bk = bank()
                for k in range(KD):
                    MM(bk.ap[:], wo[:, k, :], ys[k].ap[:], k == 0, k == KD - 1, [sl, ys[k]], [bk])
                out_res(xv, m, H, bk.ap[:], bk, dst, i, [xt])
                cv.pump()
        phase_end()

    def odd_phase1(l, src, dst):
        A.reset()
        i_ = l // 2
        ccw = c.v_ccw[i_]
        onesc = A.bf(128, "onesc")
        S.op("pool", lambda e: e.memset(onesc.ap[:], 1.0 / AW), [], [onesc])
        wd = A.bf(KA * KD * 128, "wd")
        wdv = wd.ap.rearrange("p (n k c) -> p n k c", n=KA, k=KD)
        for n in range(KA):
            DMA(wdv[:, n, :, :], WC["mi"][2 * KA + n], d_w["mi"].chunk(2 * KA + n), [wd], "c_wd")
        wst = WStream([A.bf(KD * 128, "ws") for _ in range(5)], "ws", hold=2)
        for i in range(NT):
            seq_ = []
            for ch in range(KA):
                seq_ += [("mi", ch, KD), ("mi", KA + ch, KD)]
            wst.plan(seq_ + [("mo", m, KD) for m in range(KD)])
        oxs[:] = [A.f32(T, "ox") for _ in range(2)]
        cv.attach(NT * (KA + KD) + 16, eng="act", frac=0.3, must=must_box[0])
        A.mark()
        wi = 0
        for i in range(NT):
            A.begin_iter(i == 0)
            xt, xv = load_xt(src, i, W)
            hs = rmsnorm(xt, xv, W, c.v_nmix[l])
            ygs = []
            sgs = [A.f32(W, "sg") for _ in range(2)]
            for ch in range(KA):
                wa, sla = wst.next()
                wg, slg = wst.next()
                (a0, a1), pa = pair()
                (g0, g1), pg = pair()
                for k in range(KD):
                    MM(pa[:, 0:512], wa[:, k, :], hs[k].ap[:, 0:512], k == 0, k == KD - 1, [sla, hs[k]], [a0])
                    MM(pa[:, 512:W], wa[:, k, :], hs[k].ap[:, 512:W], k == 0, k == KD - 1, [sla, hs[k]], [a1])
                for k in range(KD):
                    MM(pg[:, 0:512], wg[:, k, :], hs[k].ap[:, 0:512], k == 0, k == KD - 1, [slg, hs[k]], [g0])
                    MM(pg[:, 512:W], wg[:, k, :], hs[k].ap[:, 512:W], k == 0, k == KD - 1, [slg, hs[k]], [g1])
                sg = sgs[ch % 2]
                ACTF(sg.ap[:], pg[:, 0:W], AF.Sigmoid, [g0, g1], [sg])
                yg = A.bf(W, "yg")
                TT("dve", yg.ap[:], pa[:, 0:W], sg.ap[:], ALU.mult, [a0, a1, sg], [yg])
                ygs.append(yg)
                cv.pump()
            NH = AW // 512
            xdts = [A.bf(AW, "xdt") for _ in range(2)]
            for sb_ in range(4):
                xdt = xdts[sb_ % 2]
                for hf in range(NH):
                    bk = bank()
                    for k in range(KD):
                        MM(bk.ap[:], hs[k].ap[:, H + 128 * sb_:H + 128 * (sb_ + 1)], wdv[:, hf * 4:(hf + 1) * 4, k, :], k == 0, k == KD - 1, [wd, hs[k]], [bk])
                    CP(alt(), xdt.ap[:, hf * 512:(hf + 1) * 512], bk.ap[:], [bk], [xdt])
                r0 = T * i + 128 * sb_
                DMA(xd_loc[r0:r0 + 128, :], xdt.ap[:], [xdt], [d_xd_loc], "st_xdt%d" % (sb_ % 2))
            cvas = [A.f32(T, "cva") for _ in range(KA)]
            dgs = [A.bf(c.CK * 128, "dg") for _ in range(2)]
            for ch in range(KA):
                dg = dgs[ch % 2]
                dgv = dg.ap.rearrange("p (k c) -> p k c", k=c.CK)
                for kt in range(c.CK):
                    if kt % 2 == 0:
                        ACTF(dgv[:, kt, :], identb.ap[:], AF.Copy, [identb, vecs_t], [dg], scale=vcol(ccw + kt * KA + ch))
                    else:
                        TS("dve", dgv[:, kt, :], identb.ap[:], vcol(ccw + kt * KA + ch), None, ALU.mult, None, [identb, vecs_t], [dg])
                bk = bank()
                for kt in range(c.CK):
                    MM(bk.ap[:], dgv[:, kt, :], ygs[ch].ap[:, H - 15 + kt:H - 15 + kt + T], kt == 0, kt == c.CK - 1, [dg, ygs[ch]], [bk])
                ACTF(cvas[ch].ap[:], bk.ap[:], AF.Identity, [bk, vecs_t], [cvas[ch]], bias=vcol(c.v_ccb[i_] + ch), scale=1.0)
            bm = bank(); bq = bank()
            cbs_ = [A.bf(T, "cvb") for _ in range(2)]; sqs_ = [A.bf(T, "csq") for _ in range(2)]
            for ch in range(KA):
                cb_ = cbs_[ch % 2]; sq_ = sqs_[ch % 2]
                CP("pool", cb_.ap[:], cvas[ch].ap[:], [cvas[ch]], [cb_])
                ACTF(sq_.ap[:], cvas[ch].ap[:], AF.Square, [cvas[ch]], [sq_])
                MM(bm.ap[:], onesc.ap[:], cb_.ap[:], ch == 0, ch == KA - 1, [onesc, cb_], [bm])
                MM(bq.ap[:], onesc.ap[:], sq_.ap[:], ch == 0, ch == KA - 1, [onesc, sq_], [bq])
            mean = A.f32(T, "mean"); var = A.f32(T, "var")
            CP("act", mean.ap[:], bm.ap[:], [bm], [mean])
            TT("dve", var.ap[:], mean.ap[:], mean.ap[:], ALU.mult, [mean], [var])
            TT("dve", var.ap[:], bq.ap[:], var.ap[:], ALU.subtract, [bq, var], [var])
            S.op("dve", (lambda o: lambda e: e.tensor_scalar_max(out=o, in0=o, scalar1=0.0))(var.ap[:]), [var], [var])
            ACTF(var.ap[:], var.ap[:], AF.Sqrt, [var, eps_t], [var], bias=eps_t.ap[:, 0:1], scale=1.0)
            S.op("dve", (lambda o: lambda e: e.reciprocal(out=o, in_=o))(var.ap[:]), [var], [var])
            ycs = []
            for ch in range(KA):
                TT("pool", cvas[ch].ap[:], cvas[ch].ap[:], mean.ap[:], ALU.subtract, [cvas[ch], mean], [cvas[ch]])
            for ch in range(KA):
                TT("dve", cvas[ch].ap[:], cvas[ch].ap[:], var.ap[:], ALU.mult, [cvas[ch], var], [cvas[ch]])
            for ch in range(KA):
                yc = A.bf(T, "yc")
                ACTF(yc.ap[:], cvas[ch].ap[:], AF.Silu, [cvas[ch], vecs_t], [yc], scale=vcol(c.v_lng[i_] + ch), bias=vcol(c.v_lnb[i_] + ch))
                ycs.append(yc)
            for m in range(KD):
                wo, sl = wst.next()
                bk = bank()
                for k in range(KA):
                    MM(bk.ap[:], wo[:, k, :], ycs[k].ap[:], k == 0, k == KA - 1, [sl, ycs[k]], [bk])
                out_res(xv, m, H, bk.ap[:], bk, dst, i, [xt])
                cv.pump()
        phase_end()

    def dft_phase():
        A.reset()
        for j in range(NCH):
            S.op("pool", (lambda j: lambda e: e.collective_compute("AllGather", ALU.bypass, replica_groups=[[0, 1], [2, 3], [4, 5], [6, 7]],
                                                                    ins=[xd_loc[j * RCH:(j + 1) * RCH, :].opt()], outs=[xd_pair[j].opt()]))(j), [d_xd_loc], [d_xd_pair])
        cv.attach(N2 // min(8, N2) + 4 * 16, eng="pool", frac=0.2)
        gt = A.bf(N2 * 2 * 128, "gt")
        DMA(gt.ap[:], Gc[:, :], [], [gt], "c_gt")
        gtv = gt.ap.rearrange("p (s r k) -> p s r k", s=N2, r=2)
        f2 = A.bf(N2, "f2", parts=2 * N2)
        DMA(f2.ap[:], F2c[:, :], [], [f2], "c_f2")
        fc = A.bf(NCC * 2 * GDm, "fc")
        DMA(fc.ap[:], FCc[:, :], [], [fc], "c_fc")
        fcv = fc.ap.rearrange("p (cc r e) -> p cc r e", cc=NCC, r=2)
        SG = min(8, N2)
        xs_s = [A.bf(SG * AW, "xs") for _ in range(2)]
        bo_s = [A.bf(SG * 2 * AW, "bo") for _ in range(2)]
        PPC = RCH // N2
        NH = AW // 512
        for sg_ in range(N2 // SG):
            xs = xs_s[sg_ % 2]; bo = bo_s[sg_ % 2]
            xsv = xs.ap.rearrange("p (j c) -> p j c", j=SG)
            bov = bo.ap.rearrange("p (j r c) -> p j r c", j=SG, r=2)
            for rk in range(2):
                for j in range(NCH):
                    p0 = rk * (SC // N2) + j * PPC
                    srcv = xd_pair[j, rk * RCH:(rk + 1) * RCH, :].rearrange("(s1 s2) c -> s1 s2 c", s2=N2)
                    DMA(xsv[p0:p0 + PPC], srcv[:, sg_ * SG:(sg_ + 1) * SG, :], [d_xd_pair], [xs], "ld_xs%d" % (sg_ % 2))
            for j in range(SG):
                s2 = sg_ * SG + j
                for r in range(2):
                    for hf in range(NH):
                        bk = bank()
                        MM(bk.ap[:], gtv[:, s2, r, :], xsv[:, j, hf * 512:(hf + 1) * 512], True, True, [gt, xs], [bk])
                        CP(alt(), bov[:, j, r, hf * 512:(hf + 1) * 512], bk.ap[:], [bk], [bo])
            DMA(Bs[:, sg_ * SG:(sg_ + 1) * SG, :, :], bov, [bo], [d_Bs], "st_bo%d" % (sg_ % 2))
            cv.pump()
        KP = 2 * N2
        NK2 = N2 // 2
        bk_s = [A.bf(8 * GDm, "bkS", parts=KP) for q in range(2)]
        wgt = [A.bf(2 * SC, "wgt") for _ in range(NCC)]
        yd_s = [A.bf(T, "yd") for _ in range(2)]
        for g in range(4):
            for k1g in range(16):
                bks = bk_s[k1g % 2]
                bkv = bks.ap.rearrange("p (k c) -> p k c", k=8)
                DMA(bkv, Bs[k1g * 8:(k1g + 1) * 8, :, :, g * GDm:(g + 1) * GDm].rearrange("k s r c -> (s r) k c"), [d_Bs], [bks], "ld_bk%d" % (k1g % 2))
                for cc in range(NCC):
                    bk = bank()
                    for j in range(8):
                        MM(bk.ap[:, j * N2:(j + 1) * N2], bkv[:, j, cc * 128:(cc + 1) * 128], f2.ap[:], True, True, [bks, f2], [bk])
                    o = wgt[cc].ap.rearrange("p (r k2 k1) -> p r k2 k1", r=2, k2=NK2)[:, :, :, k1g * 8:(k1g + 1) * 8]
                    i_ap = bk.ap[:, 0:8 * N2].rearrange("p (j r k2) -> p r k2 j", j=8, r=2)
                    CP(alt(), o, i_ap, [bk], [wgt[cc]])
                cv.pump()
            for tt in range(NT):
                for eh in range(NCC):
                    bk = bank()
                    n_ = 0
                    for cc in range(NCC):
                        for r in range(2):
                            rhs = wgt[cc].ap.rearrange("p (r t) -> p r t", r=2)[:, r, tt * T:(tt + 1) * T]
                            MM(bk.ap[:], fcv[:, cc, r, eh * 128:(eh + 1) * 128], rhs, n_ == 0, n_ == 2 * NCC - 1, [fc, wgt[cc]], [bk])
                            n_ += 1
                    yd = yd_s[(tt * NCC + eh) % 2]
                    CP(alt(), yd.ap[:], bk.ap[:], [bk], [yd])
                    r0 = (g * NCC + eh) * 128
                    DMA(ydT[r0:r0 + 128, tt * T:(tt + 1) * T], yd.ap[:], [yd], [d_ydT], "st_yd%d" % ((tt * NCC + eh) % 2))
        phase_end()

    def odd_phase2(l, buf):
        A.reset()
        wst = WStream([A.bf(KD * 128, "ws") for _ in range(6)], "ws")
        for i in range(NT):
            wst.plan([("mo", m, KD) for m in range(KD)])
        oxs[:] = [A.f32(T, "ox") for _ in range(2)]
        cv.attach(NT * KD, eng="pool", frac=0.2)
        A.mark()
        wi = 0
        ydv = ydT.rearrange("(k p) t -> p k t", p=128)
        for i in range(NT):
            A.begin_iter(i == 0)
            xt, xv = load_xt(buf, i, T)
            yd = A.bf(KA * T, "ydl")
            ydlv = yd.ap.rearrange("p (k t) -> p k t", k=KA)
            DMA(ydlv, ydv[:, :, T * i:T * (i + 1)], [d_ydT], [yd], "ld_ydl")
            for m in range(KD):
                wo, sl = wst.next()
                bk = bank()
                for k in range(KA):
                    MM(bk.ap[:], wo[:, KA + k, :], ydlv[:, k, :], k == 0, k == KA - 1, [sl, yd], [bk])
                out_res(xv, m, 0, bk.ap[:], bk, buf, i, [xt])
                cv.pump()
        phase_end()

    def halo_exchange(b):
        if not use_cc:
            return
        A.reset()
        xvw = xview(b)
        DMA(exin.rearrange("(k p) w -> p k w", p=128)[:, :, 0:H], xvw[:, :, H:2 * H], [xtile[b][0]], [d_ex_in], "ex_a")
        DMA(exin.rearrange("(k p) w -> p k w", p=128)[:, :, H:2 * H], xvw[:, :, SC:SC + H], [xtile[b][NT - 1]], [d_ex_in], "ex_a")
        S.op("pool", lambda e: e.collective_compute("AllGather", ALU.bypass, replica_groups=[[0, 1], [2, 3], [4, 5], [6, 7]],
                                                     ins=[exin.opt()], outs=[exout.opt()]), [d_ex_in], [d_ex_out])
        hl = A.f32(KD * H, "hl"); hr = A.f32(KD * H, "hr")
        exv = exout.rearrange("(r k p) w -> r p k w", r=2, p=128)
        DMA(hl.ap.rearrange("p (k h) -> p k h", k=KD), exv[0][:, :, H:2 * H], [d_ex_out], [hl], "ex_hl")
        DMA(hr.ap.rearrange("p (k h) -> p k h", k=KD), exv[1][:, :, 0:H], [d_ex_out], [hr], "ex_hr")
        TS("dve", hl.ap[:], hl.ap[:], vcol(c.v_mask + 0), None, ALU.mult, None, [hl, vecs_t], [hl])
        TS("dve", hr.ap[:], hr.ap[:], vcol(c.v_mask + 1), None, ALU.mult, None, [hr, vecs_t], [hr])
        DMA(xvw[:, :, 0:H], hl.ap.rearrange("p (k h) -> p k h", k=KD), [hl], [xhalo[b][0]], "ex_sl")
        DMA(xvw[:, :, H + SC:H + SC + H], hr.ap.rearrange("p (k h) -> p k h", k=KD), [hr], [xhalo[b][1]], "ex_sr")
        phase_end()

    def epilogue(src):
        A.reset()
        yo = [A.f32(D, "yo") for _ in range(2)]
        A.mark()
        for i in range(NT):
            A.begin_iter(i == 0)
            xt, xv = load_xt(src, i, T)
            hs = rmsnorm(xt, xv, T, c.v_nfin, out_dt=F32, name="hf")
            for sb_ in range(4):
                y_ = yo[sb_ % 2]
                for kg in range(KD // 4):
                    bk = bank()
                    for kk in range(4):
                        k = kg * 4 + kk
                        S.op("pe", (lambda o, i_: lambda e: e.transpose(o, i_, ident.ap[:]))(bk.ap[:, kk * 128:(kk + 1) * 128], hs[k].ap[:, sb_ * 128:(sb_ + 1) * 128]), [hs[k], ident], [bk])
                    CP(alt(), y_.ap[:, kg * 512:(kg + 1) * 512], bk.ap[:], [bk], [y_])
                r0 = T * i + 128 * sb_
                DMA(y_out[r0:r0 + 128, :], y_.ap[:], [y_], [], "st_y%d" % (sb_ % 2))
        phase_end()

    halo_exchange(cur)
    add_mix(0)
    cvt_flush_phase()
    add_ffn(0)
    n_ffn0 = len(cv.queue)
    for l in range(c.depth):
        set_layer_weights(l)
        if l + 1 < c.depth:
            add_mix(l + 1); add_ffn(l + 1)
        n_next = len(cv.queue) - (n_ffn0 if l == 0 else 0)
        must_box[0] = n_ffn0 if l == 0 else 0
        if l % 2 == 0:
            even_phase(l, cur, 1 - cur)
            cur = 1 - cur
        else:
            odd_phase1(l, cur, 1 - cur)
            cur = 1 - cur
            dft_phase()
            odd_phase2(l, cur)
        if l == 0:
            cvt_flush_phase(leave=n_next)
        halo_exchange(cur)
        ffn_phase(l, cur, 1 - cur)
        cur = 1 - cur
        cvt_flush_phase()
        if l < c.depth - 1:
            halo_exchange(cur)
    epilogue(cur)
    S.run()
    return nc, S


def _col(v):
    v = np.asarray(v, np.float32)
    return v.reshape(-1, 128).T


def dft_consts(cfg, is_prompt, h):
    N2 = cfg.N2; SC = cfg.SC; NK2 = N2 // 2
    GD = cfg.AW // 4; NCC = GD // 128
    s1 = np.arange(128, dtype=np.float64)[:, None]; k1 = np.arange(128, dtype=np.float64)[None, :]
    G = np.zeros((128, N2, 2, 128), np.float64)
    RPC = SC // N2
    for s2 in range(N2):
        if is_prompt:
            Sq = 2 * SC
            th = 2 * np.pi * (k1 * s1 / 128.0 + k1 * s2 / Sq)
            valid = np.ones_like(th)
        else:
            Sq = SC
            s1p = s1 - RPC * h
            valid = ((s1p >= 0) & (s1p < RPC)).astype(np.float64) * np.ones_like(k1)
            th = 2 * np.pi * (k1 * s1p / RPC + k1 * s2 / SC)
        G[:, s2, 0, :] = np.cos(th) * valid
        G[:, s2, 1, :] = -np.sin(th) * valid
    s2 = np.arange(N2, dtype=np.float64)[:, None]; k2 = np.arange(NK2, dtype=np.float64)[None, :]
    if is_prompt:
        ph = 2 * np.pi * (k2 + NK2 * h) * s2 / N2
    else:
        ph = 2 * np.pi * 2 * k2 * s2 / N2
    Fr = np.cos(ph); Fi = -np.sin(ph)
    F2 = np.zeros((2 * N2, N2), np.float64)
    F2[0::2, 0:NK2] = Fr; F2[1::2, 0:NK2] = -Fi
    F2[0::2, NK2:] = Fi; F2[1::2, NK2:] = Fr
    cc_ = np.arange(GD, dtype=np.float64)[:, None]; cp = np.arange(GD, dtype=np.float64)[None, :]
    ps = 2 * np.pi * cc_ * cp / GD
    nrm = 1.0 / np.sqrt(Sq * GD)
    C = np.cos(ps) * nrm; Sn = np.sin(ps) * nrm
    FC = np.zeros((128, NCC, 2, GD), np.float64)
    for q in range(NCC):
        FC[:, q, 0, :] = C[q * 128:(q + 1) * 128, :]
        FC[:, q, 1, :] = Sn[q * 128:(q + 1) * 128, :]
    bf = ml_dtypes.bfloat16
    return (G.reshape(128, -1).astype(np.float32).astype(bf), F2.astype(np.float32).astype(bf),
            FC.reshape(128, -1).astype(np.float32).astype(bf))


def make_in_maps(cfg, inp, n_prompt_seq=2, n_sample_seq=4):
    c = cfg
    depth = c.depth
    vecs = np.zeros((128, c.NV), np.float32)
    for l in range(depth):
        vecs[:, c.v_nmix[l]:c.v_nmix[l] + c.KD] = _col(inp["norm_mix"][l])
        vecs[:, c.v_nffn[l]:c.v_nffn[l] + c.KD] = _col(inp["norm_ffn"][l])
        fw = np.asarray(inp["f_conv_w"][l], np.float32)
        for k in range(3):
            vecs[:, c.v_fw[l] + k * c.KF:c.v_fw[l] + (k + 1) * c.KF] = _col(fw[k])
        vecs[:, c.v_fb[l]:c.v_fb[l] + c.KF] = _col(inp["f_conv_b"][l])
    vecs[:, c.v_nfin:c.v_nfin + c.KD] = _col(inp["norm_final"])
    for i in range(c.NE):
        vecs[:, c.v_bscale[i]:c.v_bscale[i] + c.KA] = _col(inp["b_scale"][i])
    for i in range(c.NO):
        cw = np.asarray(inp["c_conv_w"][i], np.float32)
        for k in range(c.CK):
            vecs[:, c.v_ccw[i] + k * c.KA:c.v_ccw[i] + (k + 1) * c.KA] = _col(cw[k])
        vecs[:, c.v_ccb[i]:c.v_ccb[i] + c.KA] = _col(inp["c_conv_b"][i])
        vecs[:, c.v_lng[i]:c.v_lng[i] + c.KA] = _col(inp["c_ln_g"][i])
        vecs[:, c.v_lnb[i]:c.v_lnb[i] + c.KA] = _col(inp["c_ln_b"][i])
    shared = {
        "ident": np.eye(128, dtype=np.float32),
        "ab_w_in": np.ascontiguousarray(inp["ab_w_in"], np.float32), "ab_w_out": np.ascontiguousarray(inp["ab_w_out"], np.float32),
        "cd_w_in": np.ascontiguousarray(inp["cd_w_in"], np.float32), "cd_w_out": np.ascontiguousarray(inp["cd_w_out"], np.float32),
        "f_w_in": np.ascontiguousarray(inp["f_w_in"], np.float32), "f_w_out": np.ascontiguousarray(inp["f_w_out"], np.float32),
        "a_v_gain": np.ascontiguousarray(inp["a_v_gain"], np.float32),
        "a_w_sT": np.ascontiguousarray(np.transpose(np.asarray(inp["a_w_s"], np.float32), (0, 1, 3, 2))),
        "a_b_s": np.ascontiguousarray(np.asarray(inp["a_b_s"], np.float32).reshape(c.NE, 512)),
        "b_w_g": np.ascontiguousarray(inp["b_w_g"], np.float32),
    }
    xp = np.asarray(inp["x_prompt"], np.float32); xs = np.asarray(inp["x_sample"], np.float32)
    maps = []
    core = 0
    plan = []
    for b in range(n_prompt_seq):
        for h in range(2):
            plan.append((True, b, h))
    for b in range(n_sample_seq):
        plan.append((False, b, len(plan) % 2))
    for (is_p, b, h) in plan:
        m = dict(shared)
        if is_p:
            m["x"] = np.ascontiguousarray(xp[b, h * c.SC:(h + 1) * c.SC, :]); Sq = 2 * c.SC; off = h * c.SC
        else:
            m["x"] = np.ascontiguousarray(xs[b]); Sq = c.SC; off = 0
        v = vecs.copy()
        v[:, c.v_mask + 0] = 1.0 if (is_p and h == 1) else 0.0
        v[:, c.v_mask + 1] = 1.0 if (is_p and h == 0) else 0.0
        m["vecs"] = v
        t = off + np.arange(c.SC)
        ic = np.zeros((4, c.SC), np.float32)
        for g, w in enumerate(POOLW):
            lo = np.maximum(t - w // 2, 0); hi = np.minimum(t + w // 2, Sq)
            ic[g] = 1.0 / (hi - lo).astype(np.float32)
        m["invcnt"] = ic
        G, F2, FC = dft_consts(c, is_p, h)
        m["Gc"] = G; m["F2c"] = F2; m["FCc"] = FC
        maps.append(m)
    return maps, plan


def assemble(cfg, results, plan, n_prompt_seq=2, n_sample_seq=4):
    c = cfg
    yp = np.zeros((n_prompt_seq, 2 * c.SC, c.D), np.float32)
    ys = np.zeros((n_sample_seq, c.SC, c.D), np.float32)
    for r, (is_p, b, h) in zip(results, plan):
        if is_p:
            yp[b, h * c.SC:(h + 1) * c.SC] = r["y"]
        else:
            ys[b] = r["y"]
    return yp, ys


_CFG = Cfg()
_CACHE = {}


def kernel(**inputs):
    cfg = _CFG
    if "nc" not in _CACHE:
        _CACHE["nc"] = build(cfg)[0]
    nc = _CACHE["nc"]
    maps, plan = make_in_maps(cfg, inputs)
    res = run_bass_kernel_spmd(nc, maps, core_ids=list(range(8)))
    yp, ys = assemble(cfg, res.results, plan)
    return (yp, ys)
```

```python
import contextlib
import numpy as np
import ml_dtypes
import concourse.bass as bass
import concourse.mybir as mybir
from concourse.bass_utils import run_bass_kernel_spmd

F32 = mybir.dt.float32
BF16 = mybir.dt.bfloat16
ALU = mybir.AluOpType
AF = mybir.ActivationFunctionType
AX = mybir.AxisListType

EPOCH = 30000
DMA_SEM_MAX = 30000


class Buf:
    def __init__(self, name, ap=None):
        self.name = name
        self.ap = ap
        self.writer = None
        self.readers = {}


class Sched:
    ENGS = ("pe", "act", "dve", "pool", "sp")

    def __init__(self, nc):
        self.nc = nc
        self.prog = {e: [] for e in self.ENGS}
        self.seq = {e: 0 for e in self.ENGS}
        self.known = {e: {} for e in self.ENGS}
        self.dma_cnt = {}
        self.dma_gen = {}
        self.semnames = set()
        self.stack = contextlib.ExitStack()
        self.nalloc = 0
        self.pending = {e: [] for e in self.ENGS}

    def sbuf(self, name, free, dtype, parts=128):
        t = self.stack.enter_context(self.nc.sbuf_tensor(name, [parts] + list(free), dtype))
        return t

    def psum(self, name, free, dtype=F32):
        t = self.stack.enter_context(self.nc.psum_tensor(name, [128] + list(free), dtype))
        return t

    def _deps(self, reads, writes, skip_dsem=None):
        deps = []
        for b in reads:
            if b.writer is not None:
                deps.append(b.writer)
        for b in writes:
            if b.writer is not None:
                if not (skip_dsem is not None and b.writer[0] == skip_dsem):
                    deps.append(b.writer)
            for k, v in b.readers.items():
                deps.append((k, v))
        return deps

    def _filter(self, e, deps, n, is_dma=False):
        waits = []
        kn = self.known[e]
        for (k, v) in deps:
            if k[0] == "E" and k[1] == e and not is_dma:
                if e in ("pe", "sp"):
                    continue
            if kn.get(k, 0) >= v:
                continue
            kn[k] = v
            waits.append((k, v))
        return waits

    def op(self, e, fn, reads=(), writes=()):
        n = self.seq[e] + 1
        self.seq[e] = n
        deps = self._deps(reads, writes)
        waits = self.pending[e] + self._filter(e, deps, n)
        self.pending[e] = []
        ev = (("E", e), n)
        self.prog[e].append((waits, fn, ev))
        for b in reads:
            if b.readers.get(ev[0], 0) < n:
                b.readers[ev[0]] = n
        for b in writes:
            b.writer = ev
            b.readers = {}
        return ev

    def dma(self, q, fn, reads, writes, semkey):
        gen = self.dma_gen.get(semkey, 0)
        cnt = self.dma_cnt.get((semkey, gen), 0)
        if cnt + 16 > DMA_SEM_MAX:
            gen += 1
            self.dma_gen[semkey] = gen
            cnt = 0
        cnt += 16
        self.dma_cnt[(semkey, gen)] = cnt
        k = ("D", semkey, gen)
        deps = self._deps(reads, writes, skip_dsem=k)
        waits = self.pending[q] + self._filter(q, deps, 0, is_dma=True)
        self.pending[q] = []
        ev = (k, cnt)
        self.prog[q].append((waits, fn, ev))
        for b in reads:
            if b.readers.get(k, 0) < cnt:
                b.readers[k] = cnt
        for b in writes:
            b.writer = ev
            b.readers = {}
        return ev

    def barrier(self):
        evs = []
        for e in self.ENGS:
            if self.seq[e] > 0:
                evs.append((("E", e), self.seq[e]))
        for (semkey, gen), cnt in self.dma_cnt.items():
            evs.append((("D", semkey, gen), cnt))
        for e in self.ENGS:
            kn = self.known[e]
            for (k, v) in evs:
                if k[0] == "E" and k[1] == e:
                    continue
                if kn.get(k, 0) >= v:
                    continue
                kn[k] = v
                self.pending[e].append((k, v))

    def _semname(self, k, v=None):
        if k[0] == "E":
            return "e_%s_%d" % (k[1], (v - 1) // EPOCH)
        return "d_%s_%d" % (k[1], k[2])

    def _semval(self, k, v):
        if k[0] == "E":
            return (v - 1) % EPOCH + 1
        return v

    def run(self):
        nc = self.nc
        names = set()
        for e in self.ENGS:
            for (waits, fn, ev) in self.prog[e]:
                names.add(self._semname(ev[0], ev[1]))
        sems = {}
        for nm in sorted(names):
            sems[nm] = self.stack.enter_context(nc.semaphore(nm))
        self.nsem = len(sems)
        block = self.stack.enter_context(nc.Block())
        engmap = {"pe": block.tensor, "act": block.scalar, "dve": block.vector,
                  "pool": block.gpsimd, "sp": block.sync}
        final_waits = []
        for nm in sorted(names):
            pass

        def make(e):
            prog = self.prog[e]

            def body(eng):
                for (waits, fn, ev) in prog:
                    for (k, v) in waits:
                        eng.wait_ge(sems[self._semname(k, v)], self._semval(k, v))
                    ins = fn(eng)
                    k, v = ev
                    if k[0] == "E":
                        ins.then_inc(sems[self._semname(k, v)], 1)
                    else:
                        ins.then_inc(sems[self._semname(k, v)], 16)
                for (k, v) in self.pending[e]:
                    eng.wait_ge(sems[self._semname(k, v)], self._semval(k, v))
                if e == "sp":
                    for (semkey, gen), cnt in self.dma_cnt.items():
                        eng.wait_ge(sems["d_%s_%d" % (semkey, gen)], cnt)
                    for e2 in self.ENGS:
                        if e2 != "sp" and self.seq[e2] > 0:
                            n = self.seq[e2]
                            eng.wait_ge(sems[self._semname(("E", e2), n)], self._semval(("E", e2), n))
            return body

        for e in self.ENGS:
            if self.prog[e] or e == "sp":
                engmap[e](make(e))
        self.stack.close()


import os
DBG_FLUSH = int(os.environ.get('DBG_FLUSH', '0'))
DBG_NOPIPE = int(os.environ.get('DBG_NOPIPE', '0'))
H = 16
T = 512
W = T + 2 * H
EPS = 1e-6
POOLW = (2, 4, 8, 16)


class Cfg:
    def __init__(self, D=2048, DFF=5632, SC=4096, depth=4):
        self.D = D; self.DFF = DFF; self.SC = SC; self.depth = depth
        self.KD = D // 128; self.KF = DFF // 128
        self.AW = D // 2; self.KA = self.AW // 128
        self.NT = SC // T
        self.XW = SC + 2 * H
        self.N2 = 2 * SC // 128
        self.NE = (depth + 1) // 2; self.NO = depth // 2
        self.CK = 31
        c = 0
        self.v_nmix = []; self.v_nffn = []
        for l in range(depth):
            self.v_nmix.append(c); c += self.KD
            self.v_nffn.append(c); c += self.KD
        self.v_nfin = c; c += self.KD
        self.v_bscale = []
        for i in range(self.NE):
            self.v_bscale.append(c); c += self.KA
        self.v_ccw = []; self.v_ccb = []; self.v_lng = []; self.v_lnb = []
        for i in range(self.NO):
            self.v_ccw.append(c); c += self.CK * self.KA
            self.v_ccb.append(c); c += self.KA
            self.v_lng.append(c); c += self.KA
            self.v_lnb.append(c); c += self.KA
        self.v_fw = []; self.v_fb = []
        for l in range(depth):
            self.v_fw.append(c); c += 3 * self.KF
            self.v_fb.append(c); c += self.KF
        self.v_mask = c; c += 2
        self.NV = c


class Arena:
    def __init__(self, S, nfloats):
        self.t = S.sbuf("arena", [nfloats], F32)
        self.n = nfloats
        self.reset()

    def reset(self):
        self.hw = max(getattr(self, "hw", 0), getattr(self, "o", 0))
        self.o = 0; self.cnt = 0; self.replaying = False; self.log = None

    def mark(self):
        self.log = []; self.replaying = False

    def begin_iter(self, first):
        if first:
            self.log = []; self.replaying = False
        else:
            self.replaying = True; self.ri = 0

    def _replay(self, n, kind):
        b, meta = self.log[self.ri]
        self.ri += 1
        assert meta == (n, kind), (meta, n, kind)
        return b

    def f32(self, n, name="t", parts=128):
        if self.replaying:
            return self._replay(n, "f")
        assert self.o + n <= self.n, ("arena overflow", name, self.o, n, self.n)
        ap = self.t[0:parts, self.o:self.o + n]
        self.o += n
        self.cnt += 1
        b = Buf("%s%d" % (name, self.cnt), ap)
        if self.log is not None:
            self.log.append((b, (n, "f")))
        return b

    def bf(self, n, name="t", parts=128):
        if self.replaying:
            return self._replay(n, "b")
        nf = (n + 1) // 2
        assert self.o + nf <= self.n, ("arena overflow", name, self.o, nf, self.n)
        ap = self.t[0:parts, self.o:self.o + nf].bitcast(BF16)[:, 0:n]
        self.o += nf
        self.cnt += 1
        b = Buf("%s%d" % (name, self.cnt), ap)
        if self.log is not None:
            self.log.append((b, (n, "b")))
        return b


def build(cfg, use_cc=True, debug_out=None):
    nc = bass.Bass("TRN2", target_bir_lowering=False)
    c = cfg
    D, DFF, SC, KD, KF, KA, AW, NT, XW, N2 = c.D, c.DFF, c.SC, c.KD, c.KF, c.KA, c.AW, c.NT, c.XW, c.N2
    NE, NO = c.NE, c.NO

    def din(name, shape, dt=F32):
        return nc.dram_tensor(name, list(shape), dt, kind="ExternalInput").ap()

    x_in = din("x", [SC, D])
    vecs = din("vecs", [128, c.NV])
    ident_in = din("ident", [128, 128])
    ab_w_in = din("ab_w_in", [NE, D, 3 * AW]); ab_w_out = din("ab_w_out", [NE, D, D])
    cd_w_in = din("cd_w_in", [NO, D, 3 * AW]); cd_w_out = din("cd_w_out", [NO, D, D])
    f_w_in = din("f_w_in", [c.depth, D, 2 * DFF]); f_w_out = din("f_w_out", [c.depth, DFF, D])
    a_v_gain = din("a_v_gain", [NE, AW]); a_w_sT = din("a_w_sT", [NE, 4, 128, 128]); a_b_s = din("a_b_s", [NE, 512])
    b_w_g = din("b_w_g", [NE, 4, AW // 4, AW // 4])
    invcnt = din("invcnt", [4, SC])
    Gc = din("Gc", [128, N2 * 2 * 128], BF16)
    F2c = din("F2c", [2 * N2, N2], BF16)
    GDm = AW // 4; NCC = GDm // 128
    FCc = din("FCc", [128, NCC * 2 * GDm], BF16)
    y_out = nc.dram_tensor("y", [SC, D], F32, kind="ExternalOutput").ap()

    def dscr(name, shape, dt=F32):
        return nc.dram_tensor(name, list(shape), dt).ap()

    xbuf = [dscr("xa", [D, XW]), dscr("xb", [D, XW])]
    RCH = min(SC, 1024); NCH = SC // RCH
    xd_loc = dscr("xd_loc", [SC, AW], BF16)
    xd_pair = dscr("xd_pair", [NCH, 2 * RCH, AW], BF16)
    Bs = dscr("Bs", [128, N2, 2, AW], BF16)
    ydT = dscr("ydT", [AW, SC], BF16)
    exin = dscr("exin", [D, 2 * H]); exout = dscr("exout", [2 * D, 2 * H])
    WS = [{"mi": dscr("wmi%d" % q, [3 * KA, 128, KD, 128], BF16), "mo": dscr("wmo%d" % q, [KD, 128, KD, 128], BF16),
           "fi": dscr("wfi%d" % q, [2 * KF, 128, KD, 128], BF16), "fo": dscr("wfo%d" % q, [KD, 128, KF, 128], BF16)} for q in range(2)]
    WC = dict(WS[0])

    S = Sched(nc)
    A = Arena(S, 51500)
    vecs_t = Buf("vecs", S.sbuf("vecs_t", [c.NV], F32))
    ident = Buf("ident", S.sbuf("ident_t", [128], F32))
    ones_b = Buf("ones", S.sbuf("ones_t", [128], BF16))
    identb = Buf("identb", S.sbuf("identb_t", [128], BF16))
    PSUM = S.psum("psum_all", [8 * 512], F32)
    banks = [Buf("bank%d" % i, PSUM[:, i * 512:(i + 1) * 512]) for i in range(8)]
    st = {"sb": 0, "pr": 0}

    def bank():
        b = banks[4 + st["sb"] % 4]; st["sb"] += 1
        return b

    def pair():
        i = (st["pr"] % 2) * 2; st["pr"] += 1
        return (banks[i], banks[i + 1]), PSUM[:, i * 512:(i + 2) * 512]

    def MM(out, lhsT, rhs, start, stop, reads, writes):
        S.op("pe", lambda e: e.matmul(out, lhsT=lhsT, rhs=rhs, start=start, stop=stop), reads, writes)

    def ACTF(out, in_, func, reads, writes, **kw):
        S.op("act", lambda e: e.activation(out=out, in_=in_, func=func, **kw), reads, writes)

    def TT(eng, out, in0, in1, op, reads, writes):
        S.op(eng, lambda e: e.tensor_tensor(out=out, in0=in0, in1=in1, op=op), reads, writes)

    def TS(eng, out, in0, s1, s2, op0, op1, reads, writes):
        if s2 is None:
            S.op(eng, lambda e: e.tensor_scalar(out=out, in0=in0, scalar1=s1, scalar2=None, op0=op0), reads, writes)
        else:
            S.op(eng, lambda e: e.tensor_scalar(out=out, in0=in0, scalar1=s1, scalar2=s2, op0=op0, op1=op1), reads, writes)

    def STT(out, in0, scalar, in1, op0, op1, reads, writes):
        S.op("dve", lambda e: e.scalar_tensor_tensor(out=out, in0=in0, scalar=scalar, in1=in1, op0=op0, op1=op1), reads, writes)

    def CP(eng, out, in_, reads, writes):
        if eng == "act":
            S.op("act", lambda e: e.activation(out=out, in_=in_, func=AF.Copy), reads, writes)
        else:
            S.op(eng, lambda e: e.tensor_copy(out=out, in_=in_), reads, writes)

    def DMA(out, in_, reads, writes, semkey, q="sp"):
        S.dma(q, lambda e: e.dma_start(out=out, in_=in_), reads, writes, semkey)

    def vcol(col, n=1):
        return vecs_t.ap[:, col:col + n]

    rr = {"i": 0}

    def alt(engs=("act", "dve")):
        rr["i"] += 1
        return engs[rr["i"] % len(engs)]

    def phase_end():
        import sys
        pass
        if cv_box and cv_box[0].st is not None:
            cv_box[0]._store()
            cv_box[0].st = None
        S.barrier()
        A.reset()

    cv_box = []
    must_box = [0]

    xtile = [[Buf("x%d_%d" % (b, i)) for i in range(NT)] for b in range(2)]
    xhalo = [[Buf("xh%d_%d" % (b, i)) for i in range(2)] for b in range(2)]
    d_xd_loc = Buf("xd_loc"); d_xd_pair = Buf("xd_pair"); d_Bs = Buf("Bs"); d_ydT = Buf("ydT")
    class DW:
        def __init__(self, nm):
            self.nm = nm; self.b = {}

        def get(self, n, k0):
            if (n, k0) not in self.b:
                self.b[(n, k0)] = Buf("dw_%s_%d_%d" % (self.nm, n, k0))
            return self.b[(n, k0)]

        def chunk(self, n):
            return [b for (nn, k0), b in self.b.items() if nn == n]

    DWS = [{k: DW(k + str(q)) for k in ("mi", "mo", "fi", "fo")} for q in range(2)]
    d_w = dict(DWS[0])

    def set_layer_weights(l):
        WC.update(WS[l % 2]); d_w.update(DWS[l % 2])
    d_ex_in = Buf("exin"); d_ex_out = Buf("exout")

    def xreads(b, i):
        r = [xtile[b][i]]
        r.append(xtile[b][i - 1] if i > 0 else xhalo[b][0])
        r.append(xtile[b][i + 1] if i < NT - 1 else xhalo[b][1])
        return r

    def xview(b):
        return xbuf[b].rearrange("(k p) w -> p k w", p=128)

    DMA(vecs_t.ap[:], vecs[:, :], [], [vecs_t], "c_vecs")
    DMA(ident.ap[:], ident_in[:, :], [], [ident], "c_ident")
    S.op("dve", lambda e: e.memset(ones_b.ap[:], 1.0), [], [ones_b])
    S.op("dve", lambda e: e.tensor_copy(out=identb.ap[:], in_=ident.ap[:]), [ident], [identb])
    zt = A.f32(KD * H, "zt")
    S.op("dve", lambda e: e.memset(zt.ap[:], 0.0), [], [zt])
    for b in range(2):
        for side in range(2):
            c0 = 0 if side == 0 else H + SC
            DMA(xview(b)[:, :, c0:c0 + H], zt.ap.rearrange("p (k h) -> p k h", k=KD), [zt], [xhalo[b][side]], "zt_st")

    def load_xt(b, i, width, name="xt", slot=None, key=None):
        hh = (width - T) // 2
        xt = slot if slot is not None else A.f32(KD * width, name)
        if key is not None:
            name = key
        v = xt.ap.rearrange("p (k w) -> p k w", k=KD)
        c0 = T * i + H - hh
        DMA(v, xview(b)[:, :, c0:c0 + width], xreads(b, i) if hh > 0 else [xtile[b][i]], [xt], "ld_" + name)
        return xt, v

    def norm_bufs(width, out_dt=BF16, name="h"):
        return {"sq": [A.bf(width, "sq") for _ in range(2)], "rs": A.f32(width, "rstd"),
                "hs": [A.bf(width, name) if out_dt == BF16 else A.f32(width, name) for _ in range(KD)]}

    def rmsnorm(xt, xv, width, gcol, out_dt=BF16, name="h", nb=None):
        (b0, b1), pp = pair()
        sq = nb["sq"] if nb else [A.bf(width, "sq") for _ in range(2)]
        scale = float(D) ** -0.5
        for k in range(KD):
            sb = sq[k % 2]
            ACTF(sb.ap[:], xv[:, k, :], AF.Square, [xt], [sb], scale=scale)
            w0 = min(width, 512)
            MM(pp[:, 0:w0], ones_b.ap[:], sb.ap[:, 0:w0], k == 0, k == KD - 1, [ones_b, sb], [b0])
            if width > 512:
                MM(pp[:, 512:width], ones_b.ap[:], sb.ap[:, 512:width], k == 0, k == KD - 1, [ones_b, sb], [b1])
        rs = nb["rs"] if nb else A.f32(width, "rstd")
        ACTF(rs.ap[:], pp[:, 0:width], AF.Sqrt, [b0, b1, eps_t], [rs], bias=eps_t.ap[:, 0:1], scale=1.0)
        S.op("dve", lambda e: e.reciprocal(out=rs.ap[:], in_=rs.ap[:]), [rs], [rs])
        hs = []
        for k in range(KD):
            hb = nb["hs"][k] if nb else (A.bf(width, name) if out_dt == BF16 else A.f32(width, name))
            STT(hb.ap[:], xv[:, k, :], vcol(gcol + k), rs.ap[:], ALU.mult, ALU.mult, [xt, rs, vecs_t], [hb])
            hs.append(hb)
        return hs

    eps_t = Buf("eps", S.sbuf("eps_t", [1], F32))
    S.op("dve", lambda e: e.memset(eps_t.ap[:], EPS), [], [eps_t])

    def load_w(scr, dbuf, n, kk, slot, name):
        DMA(slot.ap.rearrange("p (k c) -> p k c", k=kk), scr[n], dbuf.chunk(n), [slot], "ldw_" + name)
        return slot.ap.rearrange("p (k c) -> p k c", k=kk)

    class WStream:
        def __init__(self, slots, tag, hold=1):
            self.slots = slots; self.tag = tag; self.NS = len(slots); self.hold = hold
            self.seq = []; self.issued = 0; self.views = {}; self.pos = 0

        def plan(self, items):
            self.seq += items

        def next(self):
            j = self.pos; self.pos += 1
            while self.issued < min(len(self.seq), j + self.NS - self.hold + 1):
                q = self.issued
                key, n, kk = self.seq[q]
                slot = self.slots[q % self.NS]
                v = load_w(WC[key], d_w[key], n, kk, slot, "%s%d" % (self.tag, q % self.NS))
                self.views[q] = (v, slot)
                self.issued += 1
            return self.views.pop(j)

    class Converter:
        KH = 4

        def __init__(self):
            self.queue = []
            self.pendq = []
            self.st = None
            self.it = 0
            self.rate = 0.0
            self.acc = 0.0

        def add(self, src, K, N, scr, dbuf):
            kk = K // 128
            sv = src.rearrange("(k p) n -> p k n", p=128)
            for n in range(N // 128):
                for k0 in range(0, kk, self.KH):
                    kh = min(self.KH, kk - k0)
                    self.queue.append((sv, scr, dbuf.get(n, k0), n, k0, kh))

        def attach(self, nsteps=0, eng="pool", frac=1.0, must=0, nst=4):
            self.NST = nst
            self.st = [(A.f32(self.KH * 128, "cvf"), A.bf(self.KH * 128, "cvb")) for _ in range(self.NST)]
            want = max(frac * len(self.queue), min(must, len(self.queue)))
            self.rate = (want / float(nsteps) * 1.1) if nsteps else 0.0
            self.acc = 0.0
            self.eng = eng
            self.pendq = []

        def _store(self, all_=True):
            while self.pendq and (all_ or len(self.pendq) >= self.NST - 1):
                (bt, scr, db, n, k0, kh, q) = self.pendq.pop(0)
                DMA(scr[n, :, k0:k0 + kh, :], bt.ap[:, 0:kh * 128].rearrange("p (k c) -> p k c", k=kh), [bt], [db], "cvs%d" % q, q="act")

        def one(self):
            self._store(all_=False)
            if not self.queue:
                self._store(all_=True)
                return
            (sv, scr, db, n, k0, kh) = self.queue.pop(0)
            q = self.it % self.NST; self.it += 1
            f, bt = self.st[q]
            DMA(f.ap[:, 0:kh * 128].rearrange("p (k c) -> p k c", k=kh), sv[:, k0:k0 + kh, n * 128:(n + 1) * 128], [], [f], "cvl%d" % q, q="act")
            CP(self.eng, bt.ap[:, 0:kh * 128], f.ap[:, 0:kh * 128], [f], [bt])
            self.pendq.append((bt, scr, db, n, k0, kh, q))

        def pump(self):
            self.acc += self.rate
            while self.acc >= 1.0:
                self.acc -= 1.0
                self.one()

        def flush(self, leave=0):
            while len(self.queue) > leave:
                self.one()
            self._store(all_=True)

    cv = Converter()
    cv_box.append(cv)

    def add_mix(l):
        i = l // 2; q = l % 2
        if l % 2 == 0:
            cv.add(ab_w_in[i], D, 3 * AW, WS[q]["mi"], DWS[q]["mi"]); cv.add(ab_w_out[i], D, D, WS[q]["mo"], DWS[q]["mo"])
        else:
            cv.add(cd_w_in[i], D, 3 * AW, WS[q]["mi"], DWS[q]["mi"]); cv.add(cd_w_out[i], D, D, WS[q]["mo"], DWS[q]["mo"])

    def add_ffn(l):
        q = l % 2
        cv.add(f_w_in[l], D, 2 * DFF, WS[q]["fi"], DWS[q]["fi"]); cv.add(f_w_out[l], DFF, D, WS[q]["fo"], DWS[q]["fo"])

    def out_res(xv, m, hh, ps_ap, psb, dstb, i, rd):
        ox = oxs[m % 2]
        TT("dve", ox.ap[:], xv[:, m, hh:hh + T], ps_ap, ALU.add, rd + [psb], [ox])
        DMA(xview(dstb)[:, m, H + T * i:H + T * i + T], ox.ap[:], [ox], [xtile[dstb][i]], "ox%d" % (m % 2))

    xin = [A.f32(4 * D, "xin") for _ in range(1)]
    xo = [A.f32(T, "xo") for _ in range(4)]
    for i in range(NT):
        xi = xin[0]
        xiv = xi.ap.rearrange("p (s d) -> p s d", s=4)
        DMA(xiv, x_in[T * i:T * (i + 1), :].rearrange("(s p) d -> p s d", p=128), [], [xi], "ld_xin")
        for k in range(KD):
            bk = bank()
            for s_ in range(4):
                S.op("pe", (lambda o, i_: lambda e: e.transpose(o, i_, ident.ap[:]))(bk.ap[:, s_ * 128:(s_ + 1) * 128], xiv[:, s_, k * 128:(k + 1) * 128]), [xi, ident], [bk])
            o = xo[k % 4]
            CP(alt(), o.ap[:], bk.ap[:], [bk], [o])
            DMA(xview(0)[:, k, H + T * i:H + T * (i + 1)], o.ap[:], [o], [xtile[0][i]], "st_xo%d" % (k % 4))
    phase_end()
    cur = 0

    def ffn_phase(l, src, dst):
        A.reset()
        acts = [A.bf(T, "act") for _ in range(KF)]
        w1s = WStream([A.bf(KD * 128, "w1") for _ in range(4)], "w1", hold=2)
        NWO = 3
        wos_ = WStream([A.bf(KF * 128, "wo") for _ in range(NWO)], "wo")
        for i in range(NT):
            for cc in range(KF):
                w1s.plan([("fi", cc, KD), ("fi", KF + cc, KD)])
            wos_.plan([("fo", m, KF) for m in range(KD)])
        cvs = [A.f32(T, "cv") for _ in range(2)]
        oxs[:] = [A.f32(T, "ox") for _ in range(2)]
        xts = [A.f32(KD * W, "xt") for _ in range(2)]
        nb = norm_bufs(W)
        cv.attach(NT * (KF + KD), eng="pool", frac=1.0, nst=3)
        fw = c.v_fw[l]; fb = c.v_fb[l]
        xt, xv = load_xt(src, 0, W, slot=xts[0], key="xt0")
        hs = rmsnorm(xt, xv, W, c.v_nffn[l], nb=nb)
        wi = 0
        for i in range(NT):
            nxt = None
            if DBG_NOPIPE and i > 0:
                xt, xv = load_xt(src, i, W, slot=xts[i % 2], key="xt%d" % (i % 2))
                hs = rmsnorm(xt, xv, W, c.v_nffn[l], nb=nb)
            for cc in range(KF):
                wa, wa_slot = w1s.next()
                wg, wg_slot = w1s.next()
                if cc == min(2, KF - 1) and i + 1 < NT and not DBG_NOPIPE:
                    nxt = load_xt(src, i + 1, W, slot=xts[(i + 1) % 2], key="xt%d" % ((i + 1) % 2))
                (b0, b1), pa = pair()
                pg = bank()
                for k in range(KD):
                    MM(pa[:, 0:512], wa[:, k, :], hs[k].ap[:, 0:512], k == 0, k == KD - 1, [wa_slot, hs[k]], [b0])
                    MM(pa[:, 512:W], wa[:, k, :], hs[k].ap[:, 512:W], k == 0, k == KD - 1, [wa_slot, hs[k]], [b1])
                for k in range(KD):
                    MM(pg.ap[:], wg[:, k, :], hs[k].ap[:, H:H + T], k == 0, k == KD - 1, [wg_slot, hs[k]], [pg])
                cvt_ = cvs[cc % 2]; gl = cvt_
                ACTF(cvt_.ap[:], pa[:, H:H + T], AF.Identity, [b0, b1, vecs_t], [cvt_], scale=vcol(fw + 1 * KF + cc), bias=vcol(fb + cc))
                STT(cvt_.ap[:], pa[:, H - 1:H - 1 + T], vcol(fw + 0 * KF + cc), cvt_.ap[:], ALU.mult, ALU.add, [b0, b1, cvt_, vecs_t], [cvt_])
                STT(cvt_.ap[:], pa[:, H + 1:H + 1 + T], vcol(fw + 2 * KF + cc), cvt_.ap[:], ALU.mult, ALU.add, [b0, b1, cvt_, vecs_t], [cvt_])
                ACTF(gl.ap[:], cvt_.ap[:], AF.Gelu_apprx_tanh, [cvt_], [gl])
                TT("dve", acts[cc].ap[:], gl.ap[:], pg.ap[:], ALU.mult, [gl, pg], [acts[cc]])
                cv.pump()
            wo_first = wos_.next()
            xt_cur, xv_cur = xt, xv
            if nxt is not None:
                xt, xv = nxt
                hs = rmsnorm(xt, xv, W, c.v_nffn[l], nb=nb)
            for m in range(KD):
                wo, wo_slot = wo_first if m == 0 else wos_.next()
                po = bank()
                for cc in range(KF):
                    MM(po.ap[:], wo[:, cc, :], acts[cc].ap[:], cc == 0, cc == KF - 1, [wo_slot, acts[cc]], [po])
                out_res(xv_cur, m, H, po.ap[:], po, dst, i, [xt_cur])
                cv.pump()
        phase_end()

    oxs = [None, None]

    def cvt_flush_phase(leave=0):
        if len(cv.queue) <= leave:
            return
        A.reset()
        cv.attach(0)
        cv.flush(leave)
        phase_end()

    def even_phase(l, src, dst):
        A.reset()
        i_ = l // 2
        GD = AW // 4
        NDC = GD // 128
        wsb = A.bf(4 * 128, "wsb")
        vg = A.f32(AW, "vg")
        bsb = A.bf(512, "bsb"); on2 = A.bf(128, "on2")
        wgb = A.bf(4 * NDC * GD, "wgb")
        m0 = A.o
        wsf = A.f32(4 * 128, "wsf")
        DMA(wsf.ap.rearrange("q (h p) -> q h p", h=4), a_w_sT[i_].rearrange("h q p -> q h p"), [], [wsf], "c_wsf")
        CP("dve", wsb.ap[:], wsf.ap[:], [wsf], [wsb])
        DMA(vg.ap[:], a_v_gain[i_, :].partition_broadcast(128), [], [vg], "c_vg")
        bsf = A.f32(512, "bsf"); bs2 = A.f32(512, "bs2")
        S.op("pool", lambda e: e.memset(bsf.ap[:], 0.0), [], [bsf])
        S.op("pool", lambda e: e.memset(bs2.ap[:], 0.0), [], [bs2])
        S.op("pool", lambda e: e.memset(on2.ap[:], 0.0), [], [on2])
        S.op("pool", lambda e: e.memset(on2.ap[0:2, :], 1.0), [], [on2])
        DMA(bsf.ap[0:1, :], a_b_s[i_:i_ + 1, :], [bsf], [bsf], "c_bsf")
        DMA(bsf.ap[1:2, :], a_b_s[i_:i_ + 1, :], [bsf], [bsf], "c_bsf")
        CP("dve", bsb.ap[:], bsf.ap[:], [bsf], [bsb])
        TT("dve", bs2.ap[:], bsf.ap[:], bsb.ap[:], ALU.subtract, [bsf, bsb], [bs2])
        bs3 = A.bf(512, "bs3")
        CP("dve", bs3.ap[:], bs2.ap[:], [bs2], [bs3])
        DMA(bsb.ap[1:2, :], bs3.ap[0:1, :], [bs3, bsb], [bsb], "c_bsb")
        wgf = A.f32(4 * NDC * GD, "wgf")
        DMA(wgf.ap.rearrange("p (g dc e) -> p g dc e", g=4, dc=NDC), b_w_g[i_].rearrange("g (dc p) e -> p g dc e", p=128), [], [wgf], "c_wgf")
        CP("pool", wgb.ap[:], wgf.ap[:], [wgf], [wgb])
        wgv = wgb.ap.rearrange("p (g dc e) -> p g dc e", g=4, dc=NDC)
        S.barrier()
        A.o = m0
        wv = A.bf(KA * KD * 128, "wv")
        wvv = wv.ap.rearrange("p (n k c) -> p n k c", n=KA, k=KD)
        for n in range(KA):
            DMA(wvv[:, n, :, :], WC["mi"][KA + n], d_w["mi"].chunk(KA + n), [wv], "c_wv")
        wst = WStream([A.bf(KD * 128, "ws") for _ in range(4)], "ws")
        for i in range(NT):
            wst.plan([("mi", ch, KD) for ch in range(KA)] + [("mi", 2 * KA + ch, KD) for ch in range(KA)] + [("mo", m, KD) for m in range(KD)])
        oxs[:] = [A.f32(T, "ox") for _ in range(2)]
        ic = [A.f32(T, "ic") for _ in range(2)]
        cv.attach(NT * (2 * KA + KD), eng="act", frac=0.35, must=must_box[0])
        A.mark()
        wi = 0
        for i in range(NT):
            A.begin_iter(i == 0)
            xt, xv = load_xt(src, i, W)
            hs = rmsnorm(xt, xv, W, c.v_nmix[l])
            us = []
            for ch in range(KA):
                wu, sl = wst.next()
                bk = bank()
                for k in range(KD):
                    MM(bk.ap[:], wu[:, k, :], hs[k].ap[:, H:H + T], k == 0, k == KD - 1, [sl, hs[k]], [bk])
                u = A.bf(T, "u")
                ACTF(u.ap[:], bk.ap[:], AF.Gelu_apprx_tanh, [bk], [u])
                us.append(u)
                cv.pump()
            pbs = []
            xbs_ = [A.f32(W, "xb") for _ in range(2)]
            t1s = [A.f32(W, "t1") for _ in range(2)]
            t2s = [A.f32(W, "t2") for _ in range(2)]
            for ch in range(KA):
                wx, sl = wst.next()
                (b0, b1), pp = pair()
                for k in range(KD):
                    MM(pp[:, 0:512], wx[:, k, :], hs[k].ap[:, 0:512], k == 0, k == KD - 1, [sl, hs[k]], [b0])
                    MM(pp[:, 512:W], wx[:, k, :], hs[k].ap[:, 512:W], k == 0, k == KD - 1, [sl, hs[k]], [b1])
                xb_ = xbs_[ch % 2]
                CP("act", xb_.ap[:], pp[:, 0:W], [b0, b1], [xb_])
                g = ch // (KA // 4)
                w_ = POOLW[g]
                if ch % (KA // 4) == 0:
                    DMA(ic[g % 2].ap[:], invcnt[g, T * i:T * (i + 1)].partition_broadcast(128), [], [ic[g % 2]], "ic%d" % (g % 2))
                t1 = t1s[ch % 2]; t2 = t2s[ch % 2]
                TT("pool", t1.ap[:, 0:W - 1], xb_.ap[:, 0:W - 1], xb_.ap[:, 1:W], ALU.add, [xb_], [t1])
                curb = t1; ln = W - 1; step = 2; other = t2
                while step < w_:
                    TT("pool", other.ap[:, 0:ln - step], curb.ap[:, 0:ln - step], curb.ap[:, step:ln], ALU.add, [curb], [other])
                    curb, other = other, curb
                    ln -= step; step *= 2
                st0 = H - w_ // 2
                TT("pool", other.ap[:, 0:T], curb.ap[:, st0:st0 + T], ic[g % 2].ap[:], ALU.mult, [curb, ic[g % 2]], [other])
                pb = A.bf(T, "pb")
                TT("pool", pb.ap[:], other.ap[:, 0:T], xb_.ap[:, H:H + T], ALU.subtract, [other, xb_], [pb])
                pbs.append(pb)
                cv.pump()
            vns = []
            NH = AW // 512
            vgels = [A.f32(AW, "vgel") for _ in range(2)]
            vsqs = [A.bf(AW, "vsq") for _ in range(2)]
            for sb_ in range(4):
                vgel = vgels[sb_ % 2]
                for hf in range(NH):
                    bk = bank()
                    for k in range(KD):
                        MM(bk.ap[:], hs[k].ap[:, H + 128 * sb_:H + 128 * (sb_ + 1)], wvv[:, hf * 4:(hf + 1) * 4, k, :], k == 0, k == KD - 1, [wv, hs[k]], [bk])
                    ACTF(vgel.ap[:, hf * 512:(hf + 1) * 512], bk.ap[:], AF.Gelu_apprx_tanh, [bk], [vgel])
                vsq = vsqs[sb_ % 2]
                ss = A.f32(2, "ss")
                ACTF(vsq.ap[:], vgel.ap[:], AF.Square, [vgel], [vsq], scale=float(AW) ** -0.5)
                S.op("dve", (lambda o, i_: lambda e: e.reduce_sum(out=o, in_=i_, axis=AX.X))(ss.ap[:, 0:1], vsq.ap[:]), [vsq], [ss])
                ACTF(ss.ap[:, 1:2], ss.ap[:, 0:1], AF.Sqrt, [ss, eps_t], [ss], bias=eps_t.ap[:, 0:1], scale=1.0)
                S.op("dve", (lambda o, i_: lambda e: e.reciprocal(out=o, in_=i_))(ss.ap[:, 0:1], ss.ap[:, 1:2]), [ss], [ss])
                vn = A.bf(AW, "vn")
                STT(vn.ap[:], vgel.ap[:], ss.ap[:, 0:1], vg.ap[:], ALU.mult, ALU.mult, [vgel, ss, vg], [vn])
                vns.append(vn)
            ys = []
            for ch in range(KA):
                hd = ch // (KA // 4)
                bk = bank()
                for sb_ in range(4):
                    o = bk.ap[:, 128 * sb_:128 * (sb_ + 1)]
                    MM(o, vns[sb_].ap[:, ch * 128:(ch + 1) * 128], wsb.ap[:, hd * 128:(hd + 1) * 128], True, False, [vns[sb_], wsb], [bk])
                    MM(o, on2.ap[:], bsb.ap[:, hd * 128:(hd + 1) * 128], False, True, [on2, bsb], [bk])
                ya = A.bf(T, "ya")
                TT("dve", ya.ap[:], us[ch].ap[:], bk.ap[:], ALU.mult, [us[ch], bk], [ya])
                ys.append(ya)
            for ech in range(KA):
                g = ech // NDC; eh = ech % NDC
                bk = bank()
                for dc in range(NDC):
                    MM(bk.ap[:], wgv[:, g, dc, eh * 128:(eh + 1) * 128], pbs[g * NDC + dc].ap[:], dc == 0, dc == NDC - 1, [wgb, pbs[g * NDC + dc]], [bk])
                yb = A.bf(T, "yb")
                ACTF(yb.ap[:], bk.ap[:], AF.Copy, [bk, vecs_t], [yb], scale=vcol(c.v_bscale[i_] + ech))
                ys.append(yb)
            for m in range(KD):
                wo, sl = wst.next()
                bk = bank()
                for k in range(KD):
                    MM(bk.ap[:], wo[:, k, :], ys[k].ap[:], k == 0, k == KD - 1, [sl, ys[k]], [bk])
                out_res(xv, m, H, bk.ap[:], bk, dst, i, [xt])
                cv.pump()
        phase_end()

    def odd_phase1(l, src, dst):
        A.reset()
        i_ = l // 2
        ccw = c.v_ccw[i_]
        onesc = A.bf(128, "onesc")
        S.op("pool", lambda e: e.memset(onesc.ap[:], 1.0 / AW), [], [onesc])
        wd = A.bf(KA * KD * 128, "wd")
        wdv = wd.ap.rearrange("p (n k c) -> p n k c", n=KA, k=KD)
        for n in range(KA):
            DMA(wdv[:, n, :, :], WC["mi"][2 * KA + n], d_w["mi"].chunk(2 * KA + n), [wd], "c_wd")
        wst = WStream([A.bf(KD * 128, "ws") for _ in range(5)], "ws", hold=2)
        seqA = []
        for ch in range(KA):
            seqA += [("mi", ch, KD), ("mi", KA + ch, KD)]
        seqO = [("mo", m, KD) for m in range(KD)]
        HA = KA // 2
        wst.plan(seqA)
        for i in range(NT):
            if i + 1 < NT:
                wst.plan(seqA[:2 * HA] + seqA[2 * HA:])
            wst.plan(seqO)
        oxs[:] = [A.f32(T, "ox") for _ in range(2)]
        xrs = [A.f32(T, "xr") for _ in range(2)]
        cv.attach(NT * (KA + KD) + 16, eng="act", frac=0.3, must=must_box[0])
        xt = A.f32(KD * W, "xt")
        nb = norm_bufs(W)
        ygs = [A.bf(W, "yg") for _ in range(KA)]
        sgs = [A.f32(W, "sg") for _ in range(2)]
        xdts = [A.bf(AW, "xdt") for _ in range(2)]
        cvas = [A.f32(T, "cva") for _ in range(KA)]
        dgs = [A.bf(c.CK * 128, "dg") for _ in range(2)]
        cbs_ = [A.bf(T, "cvb") for _ in range(2)]; sqs_ = [A.bf(T, "csq") for _ in range(2)]
        mean = A.f32(T, "mean"); var = A.f32(T, "var"); msq = A.f32(T, "msq")
        ycs = [A.bf(T, "yc") for _ in range(KA)]
        NH = AW // 512

        def stage_A(hs, chs):
            for ch in chs:
                wa, sla = wst.next()
                wg, slg = wst.next()
                (a0, a1), pa = pair()
                (g0, g1), pg = pair()
                for k in range(KD):
                    MM(pa[:, 0:512], wa[:, k, :], hs[k].ap[:, 0:512], k == 0, k == KD - 1, [sla, hs[k]], [a0])
                    MM(pa[:, 512:W], wa[:, k, :], hs[k].ap[:, 512:W], k == 0, k == KD - 1, [sla, hs[k]], [a1])
                for k in range(KD):
                    MM(pg[:, 0:512], wg[:, k, :], hs[k].ap[:, 0:512], k == 0, k == KD - 1, [slg, hs[k]], [g0])
                    MM(pg[:, 512:W], wg[:, k, :], hs[k].ap[:, 512:W], k == 0, k == KD - 1, [slg, hs[k]], [g1])
                sg = sgs[ch % 2]
                ACTF(sg.ap[:], pg[:, 0:W], AF.Sigmoid, [g0, g1], [sg])
                TT("dve", ygs[ch].ap[:], pa[:, 0:W], sg.ap[:], ALU.mult, [a0, a1, sg], [ygs[ch]])
                cv.pump()

        _, xv = load_xt(src, 0, W, slot=xt, key="xt")
        hs = rmsnorm(xt, xv, W, c.v_nmix[l], nb=nb)
        if NT > 1:
            load_xt(src, 1, W, slot=xt, key="xt")
        stage_A(hs, range(KA))
        for i in range(NT):
            for sb_ in range(4):
                xdt = xdts[sb_ % 2]
                for hf in range(NH):
                    bk = bank()
                    for k in range(KD):
                        MM(bk.ap[:], hs[k].ap[:, H + 128 * sb_:H + 128 * (sb_ + 1)], wdv[:, hf * 4:(hf + 1) * 4, k, :], k == 0, k == KD - 1, [wd, hs[k]], [bk])
                    CP(alt(), xdt.ap[:, hf * 512:(hf + 1) * 512], bk.ap[:], [bk], [xdt])
                r0 = T * i + 128 * sb_
                DMA(xd_loc[r0:r0 + 128, :], xdt.ap[:], [xdt], [d_xd_loc], "st_xdt%d" % (sb_ % 2))
            bm = banks[0]; bq = banks[1]
            for ch in range(KA):
                dg = dgs[ch % 2]
                dgv = dg.ap.rearrange("p (k c) -> p k c", k=c.CK)
                for kt in range(c.CK):
                    if kt % 2 == 0:
                        ACTF(dgv[:, kt, :], identb.ap[:], AF.Copy, [identb, vecs_t], [dg], scale=vcol(ccw + kt * KA + ch))
                    else:
                        TS("dve", dgv[:, kt, :], identb.ap[:], vcol(ccw + kt * KA + ch), None, ALU.mult, None, [identb, vecs_t], [dg])
                bk = bank()
                for kt in range(c.CK):
                    MM(bk.ap[:], dgv[:, kt, :], ygs[ch].ap[:, H - 15 + kt:H - 15 + kt + T], kt == 0, kt == c.CK - 1, [dg, ygs[ch]], [bk])
                cb_ = cbs_[ch % 2]; sq_ = sqs_[ch % 2]
                ACTF(cvas[ch].ap[:], bk.ap[:], AF.Identity, [bk, vecs_t], [cvas[ch]], bias=vcol(c.v_ccb[i_] + ch), scale=1.0)
                CP("dve", cb_.ap[:], cvas[ch].ap[:], [cvas[ch]], [cb_])
                ACTF(sq_.ap[:], cvas[ch].ap[:], AF.Square, [cvas[ch]], [sq_])
                MM(bm.ap[:], onesc.ap[:], cb_.ap[:], ch == 0, ch == KA - 1, [onesc, cb_], [bm])
                MM(bq.ap[:], onesc.ap[:], sq_.ap[:], ch == 0, ch == KA - 1, [onesc, sq_], [bq])
            CP("act", mean.ap[:], bm.ap[:], [bm], [mean])
            CP("dve", var.ap[:], bq.ap[:], [bq], [var])
            if i + 1 < NT:
                hs = rmsnorm(xt, xv, W, c.v_nmix[l], nb=nb)
                if i + 2 < NT:
                    load_xt(src, i + 2, W, slot=xt, key="xt")
                stage_A(hs, range(0, HA))
            TT("dve", msq.ap[:], mean.ap[:], mean.ap[:], ALU.mult, [mean], [msq])
            TT("dve", var.ap[:], var.ap[:], msq.ap[:], ALU.subtract, [var, msq], [var])
            S.op("dve", (lambda o: lambda e: e.tensor_scalar_max(out=o, in0=o, scalar1=0.0))(var.ap[:]), [var], [var])
            ACTF(var.ap[:], var.ap[:], AF.Sqrt, [var, eps_t], [var], bias=eps_t.ap[:, 0:1], scale=1.0)
            S.op("dve", (lambda o: lambda e: e.reciprocal(out=o, in_=o))(var.ap[:]), [var], [var])
            for ch in range(KA):
                TT("dve", cvas[ch].ap[:], cvas[ch].ap[:], mean.ap[:], ALU.subtract, [cvas[ch], mean], [cvas[ch]])
            for ch in range(KA):
                TT("dve", cvas[ch].ap[:], cvas[ch].ap[:], var.ap[:], ALU.mult, [cvas[ch], var], [cvas[ch]])
            for ch in range(KA):
                ACTF(ycs[ch].ap[:], cvas[ch].ap[:], AF.Silu, [cvas[ch], vecs_t], [ycs[ch]], scale=vcol(c.v_lng[i_] + ch), bias=vcol(c.v_lnb[i_] + ch))
            if i + 1 < NT:
                stage_A(hs, range(HA, KA))
            for m in range(KD):
                wo, sl = wst.next()
                bk = bank()
                for k in range(KA):
                    MM(bk.ap[:], wo[:, k, :], ycs[k].ap[:], k == 0, k == KA - 1, [sl, ycs[k]], [bk])
                xr = xrs[m % 2]; ox = oxs[m % 2]
                DMA(xr.ap[:], xview(src)[:, m, H + T * i:H + T * i + T], [xtile[src][i]], [xr], "xr%d" % (m % 2))
                TT("dve", ox.ap[:], xr.ap[:], bk.ap[:], ALU.add, [xr, bk], [ox])
                DMA(xview(dst)[:, m, H + T * i:H + T * i + T], ox.ap[:], [ox], [xtile[dst][i]], "ox%d" % (m % 2))
                cv.pump()
        phase_end()

    def dft_phase():
        A.reset()
        for j in range(NCH):
            S.op("pool", (lambda j: lambda e: e.collective_compute("AllGather", ALU.bypass, replica_groups=[[0, 1], [2, 3], [4, 5], [6, 7]],
                                                                    ins=[xd_loc[j * RCH:(j + 1) * RCH, :].opt()], outs=[xd_pair[j].opt()]))(j), [d_xd_loc], [d_xd_pair])
        cv.attach(N2 // min(8, N2) + 4 * 16, eng="pool", frac=0.2)
        gt = A.bf(N2 * 2 * 128, "gt")
        DMA(gt.ap[:], Gc[:, :], [], [gt], "c_gt")
        gtv = gt.ap.rearrange("p (s r k) -> p s r k", s=N2, r=2)
        f2 = A.bf(N2, "f2", parts=2 * N2)
        DMA(f2.ap[:], F2c[:, :], [], [f2], "c_f2")
        fc = A.bf(NCC * 2 * GDm, "fc")
        DMA(fc.ap[:], FCc[:, :], [], [fc], "c_fc")
        fcv = fc.ap.rearrange("p (cc r e) -> p cc r e", cc=NCC, r=2)
        SG = min(8, N2)
        xs_s = [A.bf(SG * AW, "xs") for _ in range(2)]
        bo_s = [A.bf(SG * 2 * AW, "bo") for _ in range(2)]
        PPC = RCH // N2
        NH = AW // 512
        for sg_ in range(N2 // SG):
            xs = xs_s[sg_ % 2]; bo = bo_s[sg_ % 2]
            xsv = xs.ap.rearrange("p (j c) -> p j c", j=SG)
            bov = bo.ap.rearrange("p (j r c) -> p j r c", j=SG, r=2)
            for rk in range(2):
                for j in range(NCH):
                    p0 = rk * (SC // N2) + j * PPC
                    srcv = xd_pair[j, rk * RCH:(rk + 1) * RCH, :].rearrange("(s1 s2) c -> s1 s2 c", s2=N2)
                    DMA(xsv[p0:p0 + PPC], srcv[:, sg_ * SG:(sg_ + 1) * SG, :], [d_xd_pair], [xs], "ld_xs%d" % (sg_ % 2))
            for j in range(SG):
                s2 = sg_ * SG + j
                for r in range(2):
                    for hf in range(NH):
                        bk = bank()
                        MM(bk.ap[:], gtv[:, s2, r, :], xsv[:, j, hf * 512:(hf + 1) * 512], True, True, [gt, xs], [bk])
                        CP(alt(), bov[:, j, r, hf * 512:(hf + 1) * 512], bk.ap[:], [bk], [bo])
            DMA(Bs[:, sg_ * SG:(sg_ + 1) * SG, :, :], bov, [bo], [d_Bs], "st_bo%d" % (sg_ % 2))
            cv.pump()
        KP = 2 * N2
        NK2 = N2 // 2
        bk_s = [A.bf(8 * GDm, "bkS", parts=KP) for q in range(2)]
        wgt = [A.bf(2 * SC, "wgt") for _ in range(NCC)]
        yd_s = [A.bf(T, "yd") for _ in range(2)]
        for g in range(4):
            for k1g in range(16):
                bks = bk_s[k1g % 2]
                bkv = bks.ap.rearrange("p (k c) -> p k c", k=8)
                DMA(bkv, Bs[k1g * 8:(k1g + 1) * 8, :, :, g * GDm:(g + 1) * GDm].rearrange("k s r c -> (s r) k c"), [d_Bs], [bks], "ld_bk%d" % (k1g % 2))
                for cc in range(NCC):
                    bk = bank()
                    for j in range(8):
                        MM(bk.ap[:, j * N2:(j + 1) * N2], bkv[:, j, cc * 128:(cc + 1) * 128], f2.ap[:], True, True, [bks, f2], [bk])
                    o = wgt[cc].ap.rearrange("p (r k2 k1) -> p r k2 k1", r=2, k2=NK2)[:, :, :, k1g * 8:(k1g + 1) * 8]
                    i_ap = bk.ap[:, 0:8 * N2].rearrange("p (j r k2) -> p r k2 j", j=8, r=2)
                    CP(alt(), o, i_ap, [bk], [wgt[cc]])
                cv.pump()
            for tt in range(NT):
                for eh in range(NCC):
                    bk = bank()
                    n_ = 0
                    for cc in range(NCC):
                        for r in range(2):
                            rhs = wgt[cc].ap.rearrange("p (r t) -> p r t", r=2)[:, r, tt * T:(tt + 1) * T]
                            MM(bk.ap[:], fcv[:, cc, r, eh * 128:(eh + 1) * 128], rhs, n_ == 0, n_ == 2 * NCC - 1, [fc, wgt[cc]], [bk])
                            n_ += 1
                    yd = yd_s[(tt * NCC + eh) % 2]
                    CP(alt(), yd.ap[:], bk.ap[:], [bk], [yd])
                    r0 = (g * NCC + eh) * 128
                    DMA(ydT[r0:r0 + 128, tt * T:(tt + 1) * T], yd.ap[:], [yd], [d_ydT], "st_yd%d" % ((tt * NCC + eh) % 2))
        phase_end()

    def odd_phase2(l, buf):
        A.reset()
        wst = WStream([A.bf(KD * 128, "ws") for _ in range(6)], "ws")
        for i in range(NT):
            wst.plan([("mo", m, KD) for m in range(KD)])
        oxs[:] = [A.f32(T, "ox") for _ in range(2)]
        cv.attach(NT * KD, eng="pool", frac=0.2)
        A.mark()
        wi = 0
        ydv = ydT.rearrange("(k p) t -> p k t", p=128)
        for i in range(NT):
            A.begin_iter(i == 0)
            xt, xv = load_xt(buf, i, T)
            yd = A.bf(KA * T, "ydl")
            ydlv = yd.ap.rearrange("p (k t) -> p k t", k=KA)
            DMA(ydlv, ydv[:, :, T * i:T * (i + 1)], [d_ydT], [yd], "ld_ydl")
            for m in range(KD):
                wo, sl = wst.next()
                bk = bank()
                for k in range(KA):
                    MM(bk.ap[:], wo[:, KA + k, :], ydlv[:, k, :], k == 0, k == KA - 1, [sl, yd], [bk])
                out_res(xv, m, 0, bk.ap[:], bk, buf, i, [xt])
                cv.pump()
        phase_end()

    def halo_exchange(b):
        if not use_cc:
            return
        A.reset()
        xvw = xview(b)
        DMA(exin.rearrange("(k p) w -> p k w", p=128)[:, :, 0:H], xvw[:, :, H:2 * H], [xtile[b][0]], [d_ex_in], "ex_a")
        DMA(exin.rearrange("(k p) w -> p k w", p=128)[:, :, H:2 * H], xvw[:, :, SC:SC + H], [xtile[b][NT - 1]], [d_ex_in], "ex_a")
        S.op("pool", lambda e: e.collective_compute("AllGather", ALU.bypass, replica_groups=[[0, 1], [2, 3], [4, 5], [6, 7]],
                                                     ins=[exin.opt()], outs=[exout.opt()]), [d_ex_in], [d_ex_out])
        hl = A.f32(KD * H, "hl"); hr = A.f32(KD * H, "hr")
        exv = exout.rearrange("(r k p) w -> r p k w", r=2, p=128)
        DMA(hl.ap.rearrange("p (k h) -> p k h", k=KD), exv[0][:, :, H:2 * H], [d_ex_out], [hl], "ex_hl")
        DMA(hr.ap.rearrange("p (k h) -> p k h", k=KD), exv[1][:, :, 0:H], [d_ex_out], [hr], "ex_hr")
        TS("dve", hl.ap[:], hl.ap[:], vcol(c.v_mask + 0), None, ALU.mult, None, [hl, vecs_t], [hl])
        TS("dve", hr.ap[:], hr.ap[:], vcol(c.v_mask + 1), None, ALU.mult, None, [hr, vecs_t], [hr])
        DMA(xvw[:, :, 0:H], hl.ap.rearrange("p (k h) -> p k h", k=KD), [hl], [xhalo[b][0]], "ex_sl")
        DMA(xvw[:, :, H + SC:H + SC + H], hr.ap.rearrange("p (k h) -> p k h", k=KD), [hr], [xhalo[b][1]], "ex_sr")
        phase_end()

    def epilogue(src):
        A.reset()
        yo = [A.f32(D, "yo") for _ in range(2)]
        A.mark()
        for i in range(NT):
            A.begin_iter(i == 0)
            xt, xv = load_xt(src, i, T)
            hs = rmsnorm(xt, xv, T, c.v_nfin, out_dt=F32, name="hf")
            for sb_ in range(4):
                y_ = yo[sb_ % 2]
                for kg in range(KD // 4):
                    bk = bank()
                    for kk in range(4):
                        k = kg * 4 + kk
                        S.op("pe", (lambda o, i_: lambda e: e.transpose(o, i_, ident.ap[:]))(bk.ap[:, kk * 128:(kk + 1) * 128], hs[k].ap[:, sb_ * 128:(sb_ + 1) * 128]), [hs[k], ident], [bk])
                    CP(alt(), y_.ap[:, kg * 512:(kg + 1) * 512], bk.ap[:], [bk], [y_])
                r0 = T * i + 128 * sb_
                DMA(y_out[r0:r0 + 128, :], y_.ap[:], [y_], [], "st_y%d" % (sb_ % 2))
        phase_end()

    halo_exchange(cur)
    add_mix(0)
    cvt_flush_phase()
    add_ffn(0)
    n_ffn0 = len(cv.queue)
    for l in range(c.depth):
        set_layer_weights(l)
        if l + 1 < c.depth:
            add_mix(l + 1); add_ffn(l + 1)
        n_next = len(cv.queue) - (n_ffn0 if l == 0 else 0)
        must_box[0] = n_ffn0 if l == 0 else 0
        if l % 2 == 0:
            even_phase(l, cur, 1 - cur)
            cur = 1 - cur
        else:
            odd_phase1(l, cur, 1 - cur)
            cur = 1 - cur
            dft_phase()
            odd_phase2(l, cur)
        if l == 0:
            cvt_flush_phase(leave=n_next)
        halo_exchange(cur)
        ffn_phase(l, cur, 1 - cur)
        cur = 1 - cur
        cvt_flush_phase()
        if l < c.depth - 1:
            halo_exchange(cur)
    epilogue(cur)
    S.run()
    return nc, S


def _col(v):
    v = np.asarray(v, np.float32)
    return v.reshape(-1, 128).T


def dft_consts(cfg, is_prompt, h):
    N2 = cfg.N2; SC = cfg.SC; NK2 = N2 // 2
    GD = cfg.AW // 4; NCC = GD // 128
    s1 = np.arange(128, dtype=np.float64)[:, None]; k1 = np.arange(128, dtype=np.float64)[None, :]
    G = np.zeros((128, N2, 2, 128), np.float64)
    RPC = SC // N2
    for s2 in range(N2):
        if is_prompt:
            Sq = 2 * SC
            th = 2 * np.pi * (k1 * s1 / 128.0 + k1 * s2 / Sq)
            valid = np.ones_like(th)
        else:
            Sq = SC
            s1p = s1 - RPC * h
            valid = ((s1p >= 0) & (s1p < RPC)).astype(np.float64) * np.ones_like(k1)
            th = 2 * np.pi * (k1 * s1p / RPC + k1 * s2 / SC)
        G[:, s2, 0, :] = np.cos(th) * valid
        G[:, s2, 1, :] = -np.sin(th) * valid
    s2 = np.arange(N2, dtype=np.float64)[:, None]; k2 = np.arange(NK2, dtype=np.float64)[None, :]
    if is_prompt:
        ph = 2 * np.pi * (k2 + NK2 * h) * s2 / N2
    else:
        ph = 2 * np.pi * 2 * k2 * s2 / N2
    Fr = np.cos(ph); Fi = -np.sin(ph)
    F2 = np.zeros((2 * N2, N2), np.float64)
    F2[0::2, 0:NK2] = Fr; F2[1::2, 0:NK2] = -Fi
    F2[0::2, NK2:] = Fi; F2[1::2, NK2:] = Fr
    cc_ = np.arange(GD, dtype=np.float64)[:, None]; cp = np.arange(GD, dtype=np.float64)[None, :]
    ps = 2 * np.pi * cc_ * cp / GD
    nrm = 1.0 / np.sqrt(Sq * GD)
    C = np.cos(ps) * nrm; Sn = np.sin(ps) * nrm
    FC = np.zeros((128, NCC, 2, GD), np.float64)
    for q in range(NCC):
        FC[:, q, 0, :] = C[q * 128:(q + 1) * 128, :]
        FC[:, q, 1, :] = Sn[q * 128:(q + 1) * 128, :]
    bf = ml_dtypes.bfloat16
    return (G.reshape(128, -1).astype(np.float32).astype(bf), F2.astype(np.float32).astype(bf),
            FC.reshape(128, -1).astype(np.float32).astype(bf))


def make_in_maps(cfg, inp, n_prompt_seq=2, n_sample_seq=4):
    c = cfg
    depth = c.depth
    vecs = np.zeros((128, c.NV), np.float32)
    for l in range(depth):
        vecs[:, c.v_nmix[l]:c.v_nmix[l] + c.KD] = _col(inp["norm_mix"][l])
        vecs[:, c.v_nffn[l]:c.v_nffn[l] + c.KD] = _col(inp["norm_ffn"][l])
        fw = np.asarray(inp["f_conv_w"][l], np.float32)
        for k in range(3):
            vecs[:, c.v_fw[l] + k * c.KF:c.v_fw[l] + (k + 1) * c.KF] = _col(fw[k])
        vecs[:, c.v_fb[l]:c.v_fb[l] + c.KF] = _col(inp["f_conv_b"][l])
    vecs[:, c.v_nfin:c.v_nfin + c.KD] = _col(inp["norm_final"])
    for i in range(c.NE):
        vecs[:, c.v_bscale[i]:c.v_bscale[i] + c.KA] = _col(inp["b_scale"][i])
    for i in range(c.NO):
        cw = np.asarray(inp["c_conv_w"][i], np.float32)
        for k in range(c.CK):
            vecs[:, c.v_ccw[i] + k * c.KA:c.v_ccw[i] + (k + 1) * c.KA] = _col(cw[k])
        vecs[:, c.v_ccb[i]:c.v_ccb[i] + c.KA] = _col(inp["c_conv_b"][i])
        vecs[:, c.v_lng[i]:c.v_lng[i] + c.KA] = _col(inp["c_ln_g"][i])
        vecs[:, c.v_lnb[i]:c.v_lnb[i] + c.KA] = _col(inp["c_ln_b"][i])
    shared = {
        "ident": np.eye(128, dtype=np.float32),
        "ab_w_in": np.ascontiguousarray(inp["ab_w_in"], np.float32), "ab_w_out": np.ascontiguousarray(inp["ab_w_out"], np.float32),
        "cd_w_in": np.ascontiguousarray(inp["cd_w_in"], np.float32), "cd_w_out": np.ascontiguousarray(inp["cd_w_out"], np.float32),
        "f_w_in": np.ascontiguousarray(inp["f_w_in"], np.float32), "f_w_out": np.ascontiguousarray(inp["f_w_out"], np.float32),
        "a_v_gain": np.ascontiguousarray(inp["a_v_gain"], np.float32),
        "a_w_sT": np.ascontiguousarray(np.transpose(np.asarray(inp["a_w_s"], np.float32), (0, 1, 3, 2))),
        "a_b_s": np.ascontiguousarray(np.asarray(inp["a_b_s"], np.float32).reshape(c.NE, 512)),
        "b_w_g": np.ascontiguousarray(inp["b_w_g"], np.float32),
    }
    xp = np.asarray(inp["x_prompt"], np.float32); xs = np.asarray(inp["x_sample"], np.float32)
    maps = []
    core = 0
    plan = []
    for b in range(n_prompt_seq):
        for h in range(2):
            plan.append((True, b, h))
    for b in range(n_sample_seq):
        plan.append((False, b, len(plan) % 2))
    for (is_p, b, h) in plan:
        m = dict(shared)
        if is_p:
            m["x"] = np.ascontiguousarray(xp[b, h * c.SC:(h + 1) * c.SC, :]); Sq = 2 * c.SC; off = h * c.SC
        else:
            m["x"] = np.ascontiguousarray(xs[b]); Sq = c.SC; off = 0
        v = vecs.copy()
        v[:, c.v_mask + 0] = 1.0 if (is_p and h == 1) else 0.0
        v[:, c.v_mask + 1] = 1.0 if (is_p and h == 0) else 0.0
        m["vecs"] = v
        t = off + np.arange(c.SC)
        ic = np.zeros((4, c.SC), np.float32)
        for g, w in enumerate(POOLW):
            lo = np.maximum(t - w // 2, 0); hi = np.minimum(t + w // 2, Sq)
            ic[g] = 1.0 / (hi - lo).astype(np.float32)
        m["invcnt"] = ic
        G, F2, FC = dft_consts(c, is_p, h)
        m["Gc"] = G; m["F2c"] = F2; m["FCc"] = FC
        maps.append(m)
    return maps, plan


def assemble(cfg, results, plan, n_prompt_seq=2, n_sample_seq=4):
    c = cfg
    yp = np.zeros((n_prompt_seq, 2 * c.SC, c.D), np.float32)
    ys = np.zeros((n_sample_seq, c.SC, c.D), np.float32)
    for r, (is_p, b, h) in zip(results, plan):
        if is_p:
            yp[b, h * c.SC:(h + 1) * c.SC] = r["y"]
        else:
            ys[b] = r["y"]
    return yp, ys


_CFG = Cfg()
_CACHE = {}


def kernel(**inputs):
    cfg = _CFG
    if "nc" not in _CACHE:
        _CACHE["nc"] = build(cfg)[0]
    nc = _CACHE["nc"]
    maps, plan = make_in_maps(cfg, inputs)
    res = run_bass_kernel_spmd(nc, maps, core_ids=list(range(8)))
    yp, ys = assemble(cfg, res.results, plan)
    return (yp, ys)
```

```python
import contextlib
import numpy as np
import ml_dtypes
import concourse.bass as bass
import concourse.mybir as mybir
from concourse.bass_utils import run_bass_kernel_spmd

F32 = mybir.dt.float32
BF16 = mybir.dt.bfloat16
ALU = mybir.AluOpType
AF = mybir.ActivationFunctionType
AX = mybir.AxisListType

EPOCH = 30000
DMA_SEM_MAX = 30000


class Buf:
    def __init__(self, name, ap=None):
        self.name = name
        self.ap = ap
        self.writer = None
        self.readers = {}


class Sched:
    ENGS = ("pe", "act", "dve", "pool", "sp")

    def __init__(self, nc):
        self.nc = nc
        self.prog = {e: [] for e in self.ENGS}
        self.seq = {e: 0 for e in self.ENGS}
        self.known = {e: {} for e in self.ENGS}
        self.dma_cnt = {}
        self.dma_gen = {}
        self.semnames = set()
        self.stack = contextlib.ExitStack()
        self.nalloc = 0
        self.pending = {e: [] for e in self.ENGS}

    def sbuf(self, name, free, dtype, parts=128):
        t = self.stack.enter_context(self.nc.sbuf_tensor(name, [parts] + list(free), dtype))
        return t

    def psum(self, name, free, dtype=F32):
        t = self.stack.enter_context(self.nc.psum_tensor(name, [128] + list(free), dtype))
        return t

    def _deps(self, reads, writes, skip_dsem=None):
        deps = []
        for b in reads:
            if b.writer is not None:
                deps.append(b.writer)
        for b in writes:
            if b.writer is not None:
                if not (skip_dsem is not None and b.writer[0] == skip_dsem):
                    deps.append(b.writer)
            for k, v in b.readers.items():
                deps.append((k, v))
        return deps

    def _filter(self, e, deps, n, is_dma=False):
        waits = []
        kn = self.known[e]
        for (k, v) in deps:
            if k[0] == "E" and k[1] == e and not is_dma:
                if e in ("pe", "sp"):
                    continue
            if kn.get(k, 0) >= v:
                continue
            kn[k] = v
            waits.append((k, v))
        return waits

    def op(self, e, fn, reads=(), writes=()):
        n = self.seq[e] + 1
        self.seq[e] = n
        deps = self._deps(reads, writes)
        waits = self.pending[e] + self._filter(e, deps, n)
        self.pending[e] = []
        ev = (("E", e), n)
        self.prog[e].append((waits, fn, ev))
        for b in reads:
            if b.readers.get(ev[0], 0) < n:
                b.readers[ev[0]] = n
        for b in writes:
            b.writer = ev
            b.readers = {}
        return ev

    def dma(self, q, fn, reads, writes, semkey):
        gen = self.dma_gen.get(semkey, 0)
        cnt = self.dma_cnt.get((semkey, gen), 0)
        if cnt + 16 > DMA_SEM_MAX:
            gen += 1
            self.dma_gen[semkey] = gen
            cnt = 0
        cnt += 16
        self.dma_cnt[(semkey, gen)] = cnt
        k = ("D", semkey, gen)
        deps = self._deps(reads, writes, skip_dsem=k)
        waits = self.pending[q] + self._filter(q, deps, 0, is_dma=True)
        self.pending[q] = []
        ev = (k, cnt)
        self.prog[q].append((waits, fn, ev))
        for b in reads:
            if b.readers.get(k, 0) < cnt:
                b.readers[k] = cnt
        for b in writes:
            b.writer = ev
            b.readers = {}
        return ev

    def barrier(self):
        evs = []
        for e in self.ENGS:
            if self.seq[e] > 0:
                evs.append((("E", e), self.seq[e]))
        for (semkey, gen), cnt in self.dma_cnt.items():
            evs.append((("D", semkey, gen), cnt))
        for e in self.ENGS:
            kn = self.known[e]
            for (k, v) in evs:
                if k[0] == "E" and k[1] == e:
                    continue
                if kn.get(k, 0) >= v:
                    continue
                kn[k] = v
                self.pending[e].append((k, v))

    def _semname(self, k, v=None):
        if k[0] == "E":
            return "e_%s_%d" % (k[1], (v - 1) // EPOCH)
        return "d_%s_%d" % (k[1], k[2])

    def _semval(self, k, v):
        if k[0] == "E":
            return (v - 1) % EPOCH + 1
        return v

    def run(self):
        nc = self.nc
        names = set()
        for e in self.ENGS:
            for (waits, fn, ev) in self.prog[e]:
                names.add(self._semname(ev[0], ev[1]))
        sems = {}
        for nm in sorted(names):
            sems[nm] = self.stack.enter_context(nc.semaphore(nm))
        self.nsem = len(sems)
        block = self.stack.enter_context(nc.Block())
        engmap = {"pe": block.tensor, "act": block.scalar, "dve": block.vector,
                  "pool": block.gpsimd, "sp": block.sync}
        final_waits = []
        for nm in sorted(names):
            pass

        def make(e):
            prog = self.prog[e]

            def body(eng):
                for (waits, fn, ev) in prog:
                    for (k, v) in waits:
                        eng.wait_ge(sems[self._semname(k, v)], self._semval(k, v))
                    ins = fn(eng)
                    k, v = ev
                    if k[0] == "E":
                        ins.then_inc(sems[self._semname(k, v)], 1)
                    else:
                        ins.then_inc(sems[self._semname(k, v)], 16)
                for (k, v) in self.pending[e]:
                    eng.wait_ge(sems[self._semname(k, v)], self._semval(k, v))
                if e == "sp":
                    for (semkey, gen), cnt in self.dma_cnt.items():
                        eng.wait_ge(sems["d_%s_%d" % (semkey, gen)], cnt)
                    for e2 in self.ENGS:
                        if e2 != "sp" and self.seq[e2] > 0:
                            n = self.seq[e2]
                            eng.wait_ge(sems[self._semname(("E", e2), n)], self._semval(("E", e2), n))
            return body

        for e in self.ENGS:
            if self.prog[e] or e == "sp":
                engmap[e](make(e))
        self.stack.close()


import os
DBG_FLUSH = int(os.environ.get('DBG_FLUSH', '0'))
DBG_NOPIPE = int(os.environ.get('DBG_NOPIPE', '0'))
H = 16
T = 512
W = T + 2 * H
EPS = 1e-6
POOLW = (2, 4, 8, 16)


class Cfg:
    def __init__(self, D=2048, DFF=5632, SC=4096, depth=4):
        self.D = D; self.DFF = DFF; self.SC = SC; self.depth = depth
        self.KD = D // 128; self.KF = DFF // 128
        self.AW = D // 2; self.KA = self.AW // 128
        self.NT = SC // T
        self.XW = SC + 2 * H
        self.N2 = 2 * SC // 128
        self.NE = (depth + 1) // 2; self.NO = depth // 2
        self.CK = 31
        c = 0
        self.v_nmix = []; self.v_nffn = []
        for l in range(depth):
            self.v_nmix.append(c); c += self.KD
            self.v_nffn.append(c); c += self.KD
        self.v_nfin = c; c += self.KD
        self.v_bscale = []
        for i in range(self.NE):
            self.v_bscale.append(c); c += self.KA
        self.v_ccw = []; self.v_ccb = []; self.v_lng = []; self.v_lnb = []
        for i in range(self.NO):
            self.v_ccw.append(c); c += self.CK * self.KA
            self.v_ccb.append(c); c += self.KA
            self.v_lng.append(c); c += self.KA
            self.v_lnb.append(c); c += self.KA
        self.v_fw = []; self.v_fb = []
        for l in range(depth):
            self.v_fw.append(c); c += 3 * self.KF
            self.v_fb.append(c); c += self.KF
        self.v_mask = c; c += 2
        self.NV = c


class Arena:
    def __init__(self, S, nfloats):
        self.t = S.sbuf("arena", [nfloats], F32)
        self.n = nfloats
        self.reset()

    def reset(self):
        self.hw = max(getattr(self, "hw", 0), getattr(self, "o", 0))
        self.o = 0; self.cnt = 0; self.replaying = False; self.log = None

    def mark(self):
        self.log = []; self.replaying = False

    def begin_iter(self, first):
        if first:
            self.log = []; self.replaying = False
        else:
            self.replaying = True; self.ri = 0

    def _replay(self, n, kind):
        b, meta = self.log[self.ri]
        self.ri += 1
        assert meta == (n, kind), (meta, n, kind)
        return b

    def f32(self, n, name="t", parts=128):
        if self.replaying:
            return self._replay(n, "f")
        assert self.o + n <= self.n, ("arena overflow", name, self.o, n, self.n)
        ap = self.t[0:parts, self.o:self.o + n]
        self.o += n
        self.cnt += 1
        b = Buf("%s%d" % (name, self.cnt), ap)
        if self.log is not None:
            self.log.append((b, (n, "f")))
        return b

    def bf(self, n, name="t", parts=128):
        if self.replaying:
            return self._replay(n, "b")
        nf = (n + 1) // 2
        assert self.o + nf <= self.n, ("arena overflow", name, self.o, nf, self.n)
        ap = self.t[0:parts, self.o:self.o + nf].bitcast(BF16)[:, 0:n]
        self.o += nf
        self.cnt += 1
        b = Buf("%s%d" % (name, self.cnt), ap)
        if self.log is not None:
            self.log.append((b, (n, "b")))
        return b


def build(cfg, use_cc=True, debug_out=None):
    nc = bass.Bass("TRN2", target_bir_lowering=False)
    c = cfg
    D, DFF, SC, KD, KF, KA, AW, NT, XW, N2 = c.D, c.DFF, c.SC, c.KD, c.KF, c.KA, c.AW, c.NT, c.XW, c.N2
    NE, NO = c.NE, c.NO

    def din(name, shape, dt=F32):
        return nc.dram_tensor(name, list(shape), dt, kind="ExternalInput").ap()

    x_in = din("x", [SC, D])
    vecs = din("vecs", [128, c.NV])
    ident_in = din("ident", [128, 128])
    ab_w_in = din("ab_w_in", [NE, D, 3 * AW]); ab_w_out = din("ab_w_out", [NE, D, D])
    cd_w_in = din("cd_w_in", [NO, D, 3 * AW]); cd_w_out = din("cd_w_out", [NO, D, D])
    f_w_in = din("f_w_in", [c.depth, D, 2 * DFF]); f_w_out = din("f_w_out", [c.depth, DFF, D])
    a_v_gain = din("a_v_gain", [NE, AW]); a_w_sT = din("a_w_sT", [NE, 4, 128, 128]); a_b_s = din("a_b_s", [NE, 512])
    b_w_g = din("b_w_g", [NE, 4, AW // 4, AW // 4])
    invcnt = din("invcnt", [4, SC])
    Gc = din("Gc", [128, N2 * 2 * 128], BF16)
    F2c = din("F2c", [2 * N2, N2], BF16)
    GDm = AW // 4; NCC = GDm // 128
    FCc = din("FCc", [128, NCC * 2 * GDm], BF16)
    y_out = nc.dram_tensor("y", [SC, D], F32, kind="ExternalOutput").ap()

    def dscr(name, shape, dt=F32):
        return nc.dram_tensor(name, list(shape), dt).ap()

    xbuf = [dscr("xa", [D, XW]), dscr("xb", [D, XW])]
    RCH = min(SC, 1024); NCH = SC // RCH
    xd_loc = dscr("xd_loc", [SC, AW], BF16)
    xd_pair = dscr("xd_pair", [NCH, 2 * RCH, AW], BF16)
    Bs = dscr("Bs", [128, N2, 2, AW], BF16)
    ydT = dscr("ydT", [AW, SC], BF16)
    exin = dscr("exin", [D, 2 * H]); exout = dscr("exout", [2 * D, 2 * H])
    WS = [{"mi": dscr("wmi%d" % q, [3 * KA, 128, KD, 128], BF16), "mo": dscr("wmo%d" % q, [KD, 128, KD, 128], BF16),
           "fi": dscr("wfi%d" % q, [2 * KF, 128, KD, 128], BF16), "fo": dscr("wfo%d" % q, [KD, 128, KF, 128], BF16)} for q in range(2)]
    WC = dict(WS[0])

    S = Sched(nc)
    A = Arena(S, 51500)
    vecs_t = Buf("vecs", S.sbuf("vecs_t", [c.NV], F32))
    ident = Buf("ident", S.sbuf("ident_t", [128], F32))
    ones_b = Buf("ones", S.sbuf("ones_t", [128], BF16))
    identb = Buf("identb", S.sbuf("identb_t", [128], BF16))
    PSUM = S.psum("psum_all", [8 * 512], F32)
    banks = [Buf("bank%d" % i, PSUM[:, i * 512:(i + 1) * 512]) for i in range(8)]
    st = {"sb": 0, "pr": 0}

    def bank():
        b = banks[4 + st["sb"] % 4]; st["sb"] += 1
        return b

    def pair():
        i = (st["pr"] % 2) * 2; st["pr"] += 1
        return (banks[i], banks[i + 1]), PSUM[:, i * 512:(i + 2) * 512]

    def MM(out, lhsT, rhs, start, stop, reads, writes):
        S.op("pe", lambda e: e.matmul(out, lhsT=lhsT, rhs=rhs, start=start, stop=stop), reads, writes)

    def ACTF(out, in_, func, reads, writes, **kw):
        S.op("act", lambda e: e.activation(out=out, in_=in_, func=func, **kw), reads, writes)

    def TT(eng, out, in0, in1, op, reads, writes):
        S.op(eng, lambda e: e.tensor_tensor(out=out, in0=in0, in1=in1, op=op), reads, writes)

    def TS(eng, out, in0, s1, s2, op0, op1, reads, writes):
        if s2 is None:
            S.op(eng, lambda e: e.tensor_scalar(out=out, in0=in0, scalar1=s1, scalar2=None, op0=op0), reads, writes)
        else:
            S.op(eng, lambda e: e.tensor_scalar(out=out, in0=in0, scalar1=s1, scalar2=s2, op0=op0, op1=op1), reads, writes)

    def STT(out, in0, scalar, in1, op0, op1, reads, writes):
        S.op("dve", lambda e: e.scalar_tensor_tensor(out=out, in0=in0, scalar=scalar, in1=in1, op0=op0, op1=op1), reads, writes)

    def CP(eng, out, in_, reads, writes):
        if eng == "act":
            S.op("act", lambda e: e.activation(out=out, in_=in_, func=AF.Copy), reads, writes)
        else:
            S.op(eng, lambda e: e.tensor_copy(out=out, in_=in_), reads, writes)

    def DMA(out, in_, reads, writes, semkey, q="sp"):
        S.dma(q, lambda e: e.dma_start(out=out, in_=in_), reads, writes, semkey)

    def vcol(col, n=1):
        return vecs_t.ap[:, col:col + n]

    rr = {"i": 0}

    def alt(engs=("act", "dve")):
        rr["i"] += 1
        return engs[rr["i"] % len(engs)]

    def phase_end():
        import sys
        pass
        if cv_box and cv_box[0].st is not None:
            cv_box[0]._store()
            cv_box[0].st = None
        S.barrier()
        A.reset()

    cv_box = []
    must_box = [0]

    xtile = [[Buf("x%d_%d" % (b, i)) for i in range(NT)] for b in range(2)]
    xhalo = [[Buf("xh%d_%d" % (b, i)) for i in range(2)] for b in range(2)]
    d_xd_loc = Buf("xd_loc"); d_xd_pair = Buf("xd_pair"); d_Bs = Buf("Bs"); d_ydT = Buf("ydT")
    class DW:
        def __init__(self, nm):
            self.nm = nm; self.b = {}

        def get(self, n, k0):
            if (n, k0) not in self.b:
                self.b[(n, k0)] = Buf("dw_%s_%d_%d" % (self.nm, n, k0))
            return self.b[(n, k0)]

        def chunk(self, n):
            return [b for (nn, k0), b in self.b.items() if nn == n]

    DWS = [{k: DW(k + str(q)) for k in ("mi", "mo", "fi", "fo")} for q in range(2)]
    d_w = dict(DWS[0])

    def set_layer_weights(l):
        WC.update(WS[l % 2]); d_w.update(DWS[l % 2])
    d_ex_in = Buf("exin"); d_ex_out = Buf("exout")

    def xreads(b, i):
        r = [xtile[b][i]]
        r.append(xtile[b][i - 1] if i > 0 else xhalo[b][0])
        r.append(xtile[b][i + 1] if i < NT - 1 else xhalo[b][1])
        return r

    def xview(b):
        return xbuf[b].rearrange("(k p) w -> p k w", p=128)

    DMA(vecs_t.ap[:], vecs[:, :], [], [vecs_t], "c_vecs")
    DMA(ident.ap[:], ident_in[:, :], [], [ident], "c_ident")
    S.op("dve", lambda e: e.memset(ones_b.ap[:], 1.0), [], [ones_b])
    S.op("dve", lambda e: e.tensor_copy(out=identb.ap[:], in_=ident.ap[:]), [ident], [identb])
    zt = A.f32(KD * H, "zt")
    S.op("dve", lambda e: e.memset(zt.ap[:], 0.0), [], [zt])
    for b in range(2):
        for side in range(2):
            c0 = 0 if side == 0 else H + SC
            DMA(xview(b)[:, :, c0:c0 + H], zt.ap.rearrange("p (k h) -> p k h", k=KD), [zt], [xhalo[b][side]], "zt_st")

    def load_xt(b, i, width, name="xt", slot=None, key=None):
        hh = (width - T) // 2
        xt = slot if slot is not None else A.f32(KD * width, name)
        if key is not None:
            name = key
        v = xt.ap.rearrange("p (k w) -> p k w", k=KD)
        c0 = T * i + H - hh
        DMA(v, xview(b)[:, :, c0:c0 + width], xreads(b, i) if hh > 0 else [xtile[b][i]], [xt], "ld_" + name)
        return xt, v

    def norm_bufs(width, out_dt=BF16, name="h"):
        return {"sq": [A.bf(width, "sq") for _ in range(2)], "rs": A.f32(width, "rstd"),
                "hs": [A.bf(width, name) if out_dt == BF16 else A.f32(width, name) for _ in range(KD)]}

    def rmsnorm(xt, xv, width, gcol, out_dt=BF16, name="h", nb=None):
        (b0, b1), pp = pair()
        sq = nb["sq"] if nb else [A.bf(width, "sq") for _ in range(2)]
        scale = float(D) ** -0.5
        for k in range(KD):
            sb = sq[k % 2]
            ACTF(sb.ap[:], xv[:, k, :], AF.Square, [xt], [sb], scale=scale)
            w0 = min(width, 512)
            MM(pp[:, 0:w0], ones_b.ap[:], sb.ap[:, 0:w0], k == 0, k == KD - 1, [ones_b, sb], [b0])
            if width > 512:
                MM(pp[:, 512:width], ones_b.ap[:], sb.ap[:, 512:width], k == 0, k == KD - 1, [ones_b, sb], [b1])
        rs = nb["rs"] if nb else A.f32(width, "rstd")
        ACTF(rs.ap[:], pp[:, 0:width], AF.Sqrt, [b0, b1, eps_t], [rs], bias=eps_t.ap[:, 0:1], scale=1.0)
        S.op("dve", lambda e: e.reciprocal(out=rs.ap[:], in_=rs.ap[:]), [rs], [rs])
        hs = []
        for k in range(KD):
            hb = nb["hs"][k] if nb else (A.bf(width, name) if out_dt == BF16 else A.f32(width, name))
            STT(hb.ap[:], xv[:, k, :], vcol(gcol + k), rs.ap[:], ALU.mult, ALU.mult, [xt, rs, vecs_t], [hb])
            hs.append(hb)
        return hs

    eps_t = Buf("eps", S.sbuf("eps_t", [1], F32))
    S.op("dve", lambda e: e.memset(eps_t.ap[:], EPS), [], [eps_t])

    def load_w(scr, dbuf, n, kk, slot, name):
        DMA(slot.ap.rearrange("p (k c) -> p k c", k=kk), scr[n], dbuf.chunk(n), [slot], "ldw_" + name)
        return slot.ap.rearrange("p (k c) -> p k c", k=kk)

    class WStream:
        def __init__(self, slots, tag, hold=1):
            self.slots = slots; self.tag = tag; self.NS = len(slots); self.hold = hold
            self.seq = []; self.issued = 0; self.views = {}; self.pos = 0

        def plan(self, items):
            self.seq += items

        def next(self):
            j = self.pos; self.pos += 1
            while self.issued < min(len(self.seq), j + self.NS - self.hold + 1):
                q = self.issued
                key, n, kk = self.seq[q]
                slot = self.slots[q % self.NS]
                v = load_w(WC[key], d_w[key], n, kk, slot, "%s%d" % (self.tag, q % self.NS))
                self.views[q] = (v, slot)
                self.issued += 1
            return self.views.pop(j)

    class Converter:
        KH = 4

        def __init__(self):
            self.queue = []
            self.pendq = []
            self.done = 0
            self.st = None
            self.it = 0
            self.rate = 0.0
            self.acc = 0.0

        def add(self, src, K, N, scr, dbuf):
            kk = K // 128
            sv = src.rearrange("(k p) n -> p k n", p=128)
            for n in range(N // 128):
                for k0 in range(0, kk, self.KH):
                    kh = min(self.KH, kk - k0)
                    self.queue.append((sv, scr, dbuf.get(n, k0), n, k0, kh))

        def attach(self, nsteps=0, eng="pool", frac=1.0, must=0, nst=4):
            self.NST = nst
            self.st = [(A.f32(self.KH * 128, "cvf"), A.bf(self.KH * 128, "cvb")) for _ in range(self.NST)]
            want = max(frac * len(self.queue), min(must, len(self.queue)))
            self.rate = (want / float(nsteps) * 1.1) if nsteps else 0.0
            self.acc = 0.0
            self.eng = eng
            self.pendq = []

        def _store(self, all_=True):
            while self.pendq and (all_ or len(self.pendq) >= self.NST - 1):
                (bt, scr, db, n, k0, kh, q) = self.pendq.pop(0)
                DMA(scr[n, :, k0:k0 + kh, :], bt.ap[:, 0:kh * 128].rearrange("p (k c) -> p k c", k=kh), [bt], [db], "cvs%d" % q, q="act")

        def one(self):
            self._store(all_=False)
            if not self.queue:
                self._store(all_=True)
                return
            (sv, scr, db, n, k0, kh) = self.queue.pop(0)
            self.done += 1
            q = self.it % self.NST; self.it += 1
            f, bt = self.st[q]
            DMA(f.ap[:, 0:kh * 128].rearrange("p (k c) -> p k c", k=kh), sv[:, k0:k0 + kh, n * 128:(n + 1) * 128], [], [f], "cvl%d" % q, q="act")
            CP(self.eng, bt.ap[:, 0:kh * 128], f.ap[:, 0:kh * 128], [f], [bt])
            self.pendq.append((bt, scr, db, n, k0, kh, q))

        def pump(self):
            self.acc += self.rate
            while self.acc >= 1.0:
                self.acc -= 1.0
                self.one()

        def flush(self, leave=0):
            while len(self.queue) > leave:
                self.one()
            self._store(all_=True)

    cv = Converter()
    cv_box.append(cv)

    def add_mix(l):
        i = l // 2; q = l % 2
        if l % 2 == 0:
            cv.add(ab_w_in[i], D, 3 * AW, WS[q]["mi"], DWS[q]["mi"]); cv.add(ab_w_out[i], D, D, WS[q]["mo"], DWS[q]["mo"])
        else:
            cv.add(cd_w_in[i], D, 3 * AW, WS[q]["mi"], DWS[q]["mi"]); cv.add(cd_w_out[i], D, D, WS[q]["mo"], DWS[q]["mo"])

    def add_ffn(l):
        q = l % 2
        cv.add(f_w_in[l], D, 2 * DFF, WS[q]["fi"], DWS[q]["fi"]); cv.add(f_w_out[l], DFF, D, WS[q]["fo"], DWS[q]["fo"])

    def out_res(xv, m, hh, ps_ap, psb, dstb, i, rd):
        ox = oxs[m % 2]
        TT("dve", ox.ap[:], xv[:, m, hh:hh + T], ps_ap, ALU.add, rd + [psb], [ox])
        DMA(xview(dstb)[:, m, H + T * i:H + T * i + T], ox.ap[:], [ox], [xtile[dstb][i]], "ox%d" % (m % 2))

    xin = [A.f32(4 * D, "xin") for _ in range(1)]
    xo = [A.f32(T, "xo") for _ in range(4)]
    add_mix(0)
    n_mix0 = len(cv.queue)
    add_ffn(0)
    cv.attach(NT * KD, eng="pool", frac=0.0, must=n_mix0 + (len(cv.queue) - n_mix0) // 4)
    for i in range(NT):
        xi = xin[0]
        xiv = xi.ap.rearrange("p (s d) -> p s d", s=4)
        DMA(xiv, x_in[T * i:T * (i + 1), :].rearrange("(s p) d -> p s d", p=128), [], [xi], "ld_xin")
        for k in range(KD):
            bk = bank()
            for s_ in range(4):
                S.op("pe", (lambda o, i_: lambda e: e.transpose(o, i_, ident.ap[:]))(bk.ap[:, s_ * 128:(s_ + 1) * 128], xiv[:, s_, k * 128:(k + 1) * 128]), [xi, ident], [bk])
            o = xo[k % 4]
            CP(alt(), o.ap[:], bk.ap[:], [bk], [o])
            DMA(xview(0)[:, k, H + T * i:H + T * (i + 1)], o.ap[:], [o], [xtile[0][i]], "st_xo%d" % (k % 4))
            cv.pump()
    phase_end()
    cur = 0

    def ffn_phase(l, src, dst):
        A.reset()
        acts = [A.bf(T, "act") for _ in range(KF)]
        w1s = WStream([A.bf(KD * 128, "w1") for _ in range(4)], "w1", hold=2)
        NWO = 3
        wos_ = WStream([A.bf(KF * 128, "wo") for _ in range(NWO)], "wo")
        for i in range(NT):
            for cc in range(KF):
                w1s.plan([("fi", cc, KD), ("fi", KF + cc, KD)])
            wos_.plan([("fo", m, KF) for m in range(KD)])
        cvs = [A.f32(T, "cv") for _ in range(2)]
        oxs[:] = [A.f32(T, "ox") for _ in range(2)]
        xts = [A.f32(KD * W, "xt") for _ in range(2)]
        nb = norm_bufs(W)
        cv.attach(NT * (KF + KD), eng="pool", frac=1.0, nst=3)
        fw = c.v_fw[l]; fb = c.v_fb[l]
        xt, xv = load_xt(src, 0, W, slot=xts[0], key="xt0")
        hs = rmsnorm(xt, xv, W, c.v_nffn[l], nb=nb)
        wi = 0
        for i in range(NT):
            nxt = None
            if DBG_NOPIPE and i > 0:
                xt, xv = load_xt(src, i, W, slot=xts[i % 2], key="xt%d" % (i % 2))
                hs = rmsnorm(xt, xv, W, c.v_nffn[l], nb=nb)
            for cc in range(KF):
                wa, wa_slot = w1s.next()
                wg, wg_slot = w1s.next()
                if cc == min(2, KF - 1) and i + 1 < NT and not DBG_NOPIPE:
                    nxt = load_xt(src, i + 1, W, slot=xts[(i + 1) % 2], key="xt%d" % ((i + 1) % 2))
                (b0, b1), pa = pair()
                pg = bank()
                for k in range(KD):
                    MM(pa[:, 0:512], wa[:, k, :], hs[k].ap[:, 0:512], k == 0, k == KD - 1, [wa_slot, hs[k]], [b0])
                    MM(pa[:, 512:W], wa[:, k, :], hs[k].ap[:, 512:W], k == 0, k == KD - 1, [wa_slot, hs[k]], [b1])
                for k in range(KD):
                    MM(pg.ap[:], wg[:, k, :], hs[k].ap[:, H:H + T], k == 0, k == KD - 1, [wg_slot, hs[k]], [pg])
                cvt_ = cvs[cc % 2]; gl = cvt_
                ACTF(cvt_.ap[:], pa[:, H:H + T], AF.Identity, [b0, b1, vecs_t], [cvt_], scale=vcol(fw + 1 * KF + cc), bias=vcol(fb + cc))
                STT(cvt_.ap[:], pa[:, H - 1:H - 1 + T], vcol(fw + 0 * KF + cc), cvt_.ap[:], ALU.mult, ALU.add, [b0, b1, cvt_, vecs_t], [cvt_])
                STT(cvt_.ap[:], pa[:, H + 1:H + 1 + T], vcol(fw + 2 * KF + cc), cvt_.ap[:], ALU.mult, ALU.add, [b0, b1, cvt_, vecs_t], [cvt_])
                ACTF(gl.ap[:], cvt_.ap[:], AF.Gelu_apprx_tanh, [cvt_], [gl])
                TT("dve", acts[cc].ap[:], gl.ap[:], pg.ap[:], ALU.mult, [gl, pg], [acts[cc]])
                cv.pump()
            wo_first = wos_.next()
            xt_cur, xv_cur = xt, xv
            if nxt is not None:
                xt, xv = nxt
                hs = rmsnorm(xt, xv, W, c.v_nffn[l], nb=nb)
            for m in range(KD):
                wo, wo_slot = wo_first if m == 0 else wos_.next()
                po = bank()
                for cc in range(KF):
                    MM(po.ap[:], wo[:, cc, :], acts[cc].ap[:], cc == 0, cc == KF - 1, [wo_slot, acts[cc]], [po])
                out_res(xv_cur, m, H, po.ap[:], po, dst, i, [xt_cur])
                cv.pump()
        phase_end()

    oxs = [None, None]

    def cvt_flush_phase(leave=0):
        if len(cv.queue) <= leave:
            return
        A.reset()
        cv.attach(0)
        cv.flush(leave)
        phase_end()

    def even_phase(l, src, dst):
        A.reset()
        i_ = l // 2
        GD = AW // 4
        NDC = GD // 128
        wsb = A.bf(4 * 128, "wsb")
        vg = A.f32(AW, "vg")
        bsb = A.bf(512, "bsb"); on2 = A.bf(128, "on2")
        wgb = A.bf(4 * NDC * GD, "wgb")
        m0 = A.o
        wsf = A.f32(4 * 128, "wsf")
        DMA(wsf.ap.rearrange("q (h p) -> q h p", h=4), a_w_sT[i_].rearrange("h q p -> q h p"), [], [wsf], "c_wsf")
        CP("dve", wsb.ap[:], wsf.ap[:], [wsf], [wsb])
        DMA(vg.ap[:], a_v_gain[i_, :].partition_broadcast(128), [], [vg], "c_vg")
        bsf = A.f32(512, "bsf"); bs2 = A.f32(512, "bs2")
        S.op("pool", lambda e: e.memset(bsf.ap[:], 0.0), [], [bsf])
        S.op("pool", lambda e: e.memset(bs2.ap[:], 0.0), [], [bs2])
        S.op("pool", lambda e: e.memset(on2.ap[:], 0.0), [], [on2])
        S.op("pool", lambda e: e.memset(on2.ap[0:2, :], 1.0), [], [on2])
        DMA(bsf.ap[0:1, :], a_b_s[i_:i_ + 1, :], [bsf], [bsf], "c_bsf")
        DMA(bsf.ap[1:2, :], a_b_s[i_:i_ + 1, :], [bsf], [bsf], "c_bsf")
        CP("dve", bsb.ap[:], bsf.ap[:], [bsf], [bsb])
        TT("dve", bs2.ap[:], bsf.ap[:], bsb.ap[:], ALU.subtract, [bsf, bsb], [bs2])
        bs3 = A.bf(512, "bs3")
        CP("dve", bs3.ap[:], bs2.ap[:], [bs2], [bs3])
        DMA(bsb.ap[1:2, :], bs3.ap[0:1, :], [bs3, bsb], [bsb], "c_bsb")
        wgf = A.f32(4 * NDC * GD, "wgf")
        DMA(wgf.ap.rearrange("p (g dc e) -> p g dc e", g=4, dc=NDC), b_w_g[i_].rearrange("g (dc p) e -> p g dc e", p=128), [], [wgf], "c_wgf")
        CP("pool", wgb.ap[:], wgf.ap[:], [wgf], [wgb])
        wgv = wgb.ap.rearrange("p (g dc e) -> p g dc e", g=4, dc=NDC)
        S.barrier()
        A.o = m0
        wv = A.bf(KA * KD * 128, "wv")
        wvv = wv.ap.rearrange("p (n k c) -> p n k c", n=KA, k=KD)
        for n in range(KA):
            DMA(wvv[:, n, :, :], WC["mi"][KA + n], d_w["mi"].chunk(KA + n), [wv], "c_wv")
        wst = WStream([A.bf(KD * 128, "ws") for _ in range(4)], "ws")
        for i in range(NT):
            wst.plan([("mi", ch, KD) for ch in range(KA)] + [("mi", 2 * KA + ch, KD) for ch in range(KA)] + [("mo", m, KD) for m in range(KD)])
        oxs[:] = [A.f32(T, "ox") for _ in range(2)]
        ic = [A.f32(T, "ic") for _ in range(2)]
        cv.attach(NT * (2 * KA + KD), eng="act", frac=0.35, must=must_box[0])
        A.mark()
        wi = 0
        for i in range(NT):
            A.begin_iter(i == 0)
            xt, xv = load_xt(src, i, W)
            hs = rmsnorm(xt, xv, W, c.v_nmix[l])
            us = []
            for ch in range(KA):
                wu, sl = wst.next()
                bk = bank()
                for k in range(KD):
                    MM(bk.ap[:], wu[:, k, :], hs[k].ap[:, H:H + T], k == 0, k == KD - 1, [sl, hs[k]], [bk])
                u = A.bf(T, "u")
                ACTF(u.ap[:], bk.ap[:], AF.Gelu_apprx_tanh, [bk], [u])
                us.append(u)
                cv.pump()
            pbs = []
            xbs_ = [A.f32(W, "xb") for _ in range(2)]
            t1s = [A.f32(W, "t1") for _ in range(2)]
            t2s = [A.f32(W, "t2") for _ in range(2)]
            for ch in range(KA):
                wx, sl = wst.next()
                (b0, b1), pp = pair()
                for k in range(KD):
                    MM(pp[:, 0:512], wx[:, k, :], hs[k].ap[:, 0:512], k == 0, k == KD - 1, [sl, hs[k]], [b0])
                    MM(pp[:, 512:W], wx[:, k, :], hs[k].ap[:, 512:W], k == 0, k == KD - 1, [sl, hs[k]], [b1])
                xb_ = xbs_[ch % 2]
                CP("act", xb_.ap[:], pp[:, 0:W], [b0, b1], [xb_])
                g = ch // (KA // 4)
                w_ = POOLW[g]
                if ch % (KA // 4) == 0:
                    DMA(ic[g % 2].ap[:], invcnt[g, T * i:T * (i + 1)].partition_broadcast(128), [], [ic[g % 2]], "ic%d" % (g % 2))
                t1 = t1s[ch % 2]; t2 = t2s[ch % 2]
                TT("pool", t1.ap[:, 0:W - 1], xb_.ap[:, 0:W - 1], xb_.ap[:, 1:W], ALU.add, [xb_], [t1])
                curb = t1; ln = W - 1; step = 2; other = t2
                while step < w_:
                    TT("pool", other.ap[:, 0:ln - step], curb.ap[:, 0:ln - step], curb.ap[:, step:ln], ALU.add, [curb], [other])
                    curb, other = other, curb
                    ln -= step; step *= 2
                st0 = H - w_ // 2
                TT("pool", other.ap[:, 0:T], curb.ap[:, st0:st0 + T], ic[g % 2].ap[:], ALU.mult, [curb, ic[g % 2]], [other])
                pb = A.bf(T, "pb")
                TT("pool", pb.ap[:], other.ap[:, 0:T], xb_.ap[:, H:H + T], ALU.subtract, [other, xb_], [pb])
                pbs.append(pb)
                cv.pump()
            vns = []
            NH = AW // 512
            vgels = [A.f32(AW, "vgel") for _ in range(2)]
            vsqs = [A.bf(AW, "vsq") for _ in range(2)]
            for sb_ in range(4):
                vgel = vgels[sb_ % 2]
                for hf in range(NH):
                    bk = bank()
                    for k in range(KD):
                        MM(bk.ap[:], hs[k].ap[:, H + 128 * sb_:H + 128 * (sb_ + 1)], wvv[:, hf * 4:(hf + 1) * 4, k, :], k == 0, k == KD - 1, [wv, hs[k]], [bk])
                    ACTF(vgel.ap[:, hf * 512:(hf + 1) * 512], bk.ap[:], AF.Gelu_apprx_tanh, [bk], [vgel])
                vsq = vsqs[sb_ % 2]
                ss = A.f32(2, "ss")
                ACTF(vsq.ap[:], vgel.ap[:], AF.Square, [vgel], [vsq], scale=float(AW) ** -0.5)
                S.op("dve", (lambda o, i_: lambda e: e.reduce_sum(out=o, in_=i_, axis=AX.X))(ss.ap[:, 0:1], vsq.ap[:]), [vsq], [ss])
                ACTF(ss.ap[:, 1:2], ss.ap[:, 0:1], AF.Sqrt, [ss, eps_t], [ss], bias=eps_t.ap[:, 0:1], scale=1.0)
                S.op("dve", (lambda o, i_: lambda e: e.reciprocal(out=o, in_=i_))(ss.ap[:, 0:1], ss.ap[:, 1:2]), [ss], [ss])
                vn = A.bf(AW, "vn")
                STT(vn.ap[:], vgel.ap[:], ss.ap[:, 0:1], vg.ap[:], ALU.mult, ALU.mult, [vgel, ss, vg], [vn])
                vns.append(vn)
            ys = []
            for ch in range(KA):
                hd = ch // (KA // 4)
                bk = bank()
                for sb_ in range(4):
                    o = bk.ap[:, 128 * sb_:128 * (sb_ + 1)]
                    MM(o, vns[sb_].ap[:, ch * 128:(ch + 1) * 128], wsb.ap[:, hd * 128:(hd + 1) * 128], True, False, [vns[sb_], wsb], [bk])
                    MM(o, on2.ap[:], bsb.ap[:, hd * 128:(hd + 1) * 128], False, True, [on2, bsb], [bk])
                ya = A.bf(T, "ya")
                TT("dve", ya.ap[:], us[ch].ap[:], bk.ap[:], ALU.mult, [us[ch], bk], [ya])
                ys.append(ya)
            for ech in range(KA):
                g = ech // NDC; eh = ech % NDC
                bk = bank()
                for dc in range(NDC):
                    MM(bk.ap[:], wgv[:, g, dc, eh * 128:(eh + 1) * 128], pbs[g * NDC + dc].ap[:], dc == 0, dc == NDC - 1, [wgb, pbs[g * NDC + dc]], [bk])
                yb = A.bf(T, "yb")
                ACTF(yb.ap[:], bk.ap[:], AF.Copy, [bk, vecs_t], [yb], scale=vcol(c.v_bscale[i_] + ech))
                ys.append(yb)
            for m in range(KD):
                wo, sl = wst.next()
                bk = bank()
                for k in range(KD):
                    MM(bk.ap[:], wo[:, k, :], ys[k].ap[:], k == 0, k == KD - 1, [sl, ys[k]], [bk])
                out_res(xv, m, H, bk.ap[:], bk, dst, i, [xt])
                cv.pump()
        phase_end()

    def odd_phase1(l, src, dst):
        A.reset()
        i_ = l // 2
        ccw = c.v_ccw[i_]
        onesc = A.bf(128, "onesc")
        S.op("pool", lambda e: e.memset(onesc.ap[:], 1.0 / AW), [], [onesc])
        wd = A.bf(KA * KD * 128, "wd")
        wdv = wd.ap.rearrange("p (n k c) -> p n k c", n=KA, k=KD)
        for n in range(KA):
            DMA(wdv[:, n, :, :], WC["mi"][2 * KA + n], d_w["mi"].chunk(2 * KA + n), [wd], "c_wd")
        wst = WStream([A.bf(KD * 128, "ws") for _ in range(5)], "ws", hold=2)
        seqA = []
        for ch in range(KA):
            seqA += [("mi", ch, KD), ("mi", KA + ch, KD)]
        seqO = [("mo", m, KD) for m in range(KD)]
        HA = KA // 2
        wst.plan(seqA)
        for i in range(NT):
            if i + 1 < NT:
                wst.plan(seqA[:2 * HA] + seqA[2 * HA:])
            wst.plan(seqO)
        oxs[:] = [A.f32(T, "ox") for _ in range(2)]
        xrs = [A.f32(T, "xr") for _ in range(2)]
        cv.attach(NT * (KA + KD) + 16, eng="act", frac=0.3, must=must_box[0])
        xt = A.f32(KD * W, "xt")
        nb = norm_bufs(W)
        ygs = [A.bf(W, "yg") for _ in range(KA)]
        sgs = [A.f32(W, "sg") for _ in range(2)]
        xdts = [A.bf(AW, "xdt") for _ in range(2)]
        cvas = [A.f32(T, "cva") for _ in range(KA)]
        dgs = [A.bf(c.CK * 128, "dg") for _ in range(2)]
        cbs_ = [A.bf(T, "cvb") for _ in range(2)]; sqs_ = [A.bf(T, "csq") for _ in range(2)]
        mean = A.f32(T, "mean"); var = A.f32(T, "var"); msq = A.f32(T, "msq")
        ycs = [A.bf(T, "yc") for _ in range(KA)]
        NH = AW // 512

        def stage_A(hs, chs):
            for ch in chs:
                wa, sla = wst.next()
                wg, slg = wst.next()
                (a0, a1), pa = pair()
                (g0, g1), pg = pair()
                for k in range(KD):
                    MM(pa[:, 0:512], wa[:, k, :], hs[k].ap[:, 0:512], k == 0, k == KD - 1, [sla, hs[k]], [a0])
                    MM(pa[:, 512:W], wa[:, k, :], hs[k].ap[:, 512:W], k == 0, k == KD - 1, [sla, hs[k]], [a1])
                for k in range(KD):
                    MM(pg[:, 0:512], wg[:, k, :], hs[k].ap[:, 0:512], k == 0, k == KD - 1, [slg, hs[k]], [g0])
                    MM(pg[:, 512:W], wg[:, k, :], hs[k].ap[:, 512:W], k == 0, k == KD - 1, [slg, hs[k]], [g1])
                sg = sgs[ch % 2]
                ACTF(sg.ap[:], pg[:, 0:W], AF.Sigmoid, [g0, g1], [sg])
                TT("dve", ygs[ch].ap[:], pa[:, 0:W], sg.ap[:], ALU.mult, [a0, a1, sg], [ygs[ch]])
                cv.pump()

        _, xv = load_xt(src, 0, W, slot=xt, key="xt")
        hs = rmsnorm(xt, xv, W, c.v_nmix[l], nb=nb)
        if NT > 1:
            load_xt(src, 1, W, slot=xt, key="xt")
        stage_A(hs, range(KA))
        for i in range(NT):
            for sb_ in range(4):
                xdt = xdts[sb_ % 2]
                for hf in range(NH):
                    bk = bank()
                    for k in range(KD):
                        MM(bk.ap[:], hs[k].ap[:, H + 128 * sb_:H + 128 * (sb_ + 1)], wdv[:, hf * 4:(hf + 1) * 4, k, :], k == 0, k == KD - 1, [wd, hs[k]], [bk])
                    CP(alt(), xdt.ap[:, hf * 512:(hf + 1) * 512], bk.ap[:], [bk], [xdt])
                r0 = T * i + 128 * sb_
                DMA(xd_loc[r0:r0 + 128, :], xdt.ap[:], [xdt], [d_xd_loc], "st_xdt%d" % (sb_ % 2))
            bm = banks[0]; bq = banks[1]
            for ch in range(KA):
                dg = dgs[ch % 2]
                dgv = dg.ap.rearrange("p (k c) -> p k c", k=c.CK)
                wtap = vecs_t.ap[:, ccw + ch:ccw + ch + c.CK * KA:KA]
                in0_ = identb.ap[:, :].unsqueeze(1).to_broadcast([128, c.CK, 128])
                in1_ = wtap.unsqueeze(2).to_broadcast([128, c.CK, 128])
                TT("dve" if ch % 2 == 0 else "pool", dgv, in0_, in1_, ALU.mult, [identb, vecs_t], [dg])
                bk = bank()
                for kt in range(c.CK):
                    MM(bk.ap[:], dgv[:, kt, :], ygs[ch].ap[:, H - 15 + kt:H - 15 + kt + T], kt == 0, kt == c.CK - 1, [dg, ygs[ch]], [bk])
                cb_ = cbs_[ch % 2]; sq_ = sqs_[ch % 2]
                ACTF(cvas[ch].ap[:], bk.ap[:], AF.Identity, [bk, vecs_t], [cvas[ch]], bias=vcol(c.v_ccb[i_] + ch), scale=1.0)
                CP("dve", cb_.ap[:], cvas[ch].ap[:], [cvas[ch]], [cb_])
                ACTF(sq_.ap[:], cvas[ch].ap[:], AF.Square, [cvas[ch]], [sq_])
                MM(bm.ap[:], onesc.ap[:], cb_.ap[:], ch == 0, ch == KA - 1, [onesc, cb_], [bm])
                MM(bq.ap[:], onesc.ap[:], sq_.ap[:], ch == 0, ch == KA - 1, [onesc, sq_], [bq])
            CP("act", mean.ap[:], bm.ap[:], [bm], [mean])
            CP("dve", var.ap[:], bq.ap[:], [bq], [var])
            if i + 1 < NT:
                hs = rmsnorm(xt, xv, W, c.v_nmix[l], nb=nb)
                if i + 2 < NT:
                    load_xt(src, i + 2, W, slot=xt, key="xt")
                stage_A(hs, range(0, HA))
            TT("dve", msq.ap[:], mean.ap[:], mean.ap[:], ALU.mult, [mean], [msq])
            TT("dve", var.ap[:], var.ap[:], msq.ap[:], ALU.subtract, [var, msq], [var])
            S.op("dve", (lambda o: lambda e: e.tensor_scalar_max(out=o, in0=o, scalar1=0.0))(var.ap[:]), [var], [var])
            ACTF(var.ap[:], var.ap[:], AF.Sqrt, [var, eps_t], [var], bias=eps_t.ap[:, 0:1], scale=1.0)
            S.op("dve", (lambda o: lambda e: e.reciprocal(out=o, in_=o))(var.ap[:]), [var], [var])
            for ch in range(KA):
                TT("dve", cvas[ch].ap[:], cvas[ch].ap[:], mean.ap[:], ALU.subtract, [cvas[ch], mean], [cvas[ch]])
            for ch in range(KA):
                TT("dve", cvas[ch].ap[:], cvas[ch].ap[:], var.ap[:], ALU.mult, [cvas[ch], var], [cvas[ch]])
            for ch in range(KA):
                ACTF(ycs[ch].ap[:], cvas[ch].ap[:], AF.Silu, [cvas[ch], vecs_t], [ycs[ch]], scale=vcol(c.v_lng[i_] + ch), bias=vcol(c.v_lnb[i_] + ch))
            if i + 1 < NT:
                stage_A(hs, range(HA, KA))
            for m in range(KD):
                wo, sl = wst.next()
                bk = bank()
                for k in range(KA):
                    MM(bk.ap[:], wo[:, k, :], ycs[k].ap[:], k == 0, k == KA - 1, [sl, ycs[k]], [bk])
                xr = xrs[m % 2]; ox = oxs[m % 2]
                DMA(xr.ap[:], xview(src)[:, m, H + T * i:H + T * i + T], [xtile[src][i]], [xr], "xr%d" % (m % 2))
                TT("dve", ox.ap[:], xr.ap[:], bk.ap[:], ALU.add, [xr, bk], [ox])
                DMA(xview(dst)[:, m, H + T * i:H + T * i + T], ox.ap[:], [ox], [xtile[dst][i]], "ox%d" % (m % 2))
                cv.pump()
        phase_end()

    def dft_phase():
        A.reset()
        for j in range(NCH):
            S.op("pool", (lambda j: lambda e: e.collective_compute("AllGather", ALU.bypass, replica_groups=[[0, 1], [2, 3], [4, 5], [6, 7]],
                                                                    ins=[xd_loc[j * RCH:(j + 1) * RCH, :].opt()], outs=[xd_pair[j].opt()]))(j), [d_xd_loc], [d_xd_pair])
        cv.attach(N2 // min(8, N2) + 4 * 16, eng="pool", frac=0.2)
        gt = A.bf(N2 * 2 * 128, "gt")
        DMA(gt.ap[:], Gc[:, :], [], [gt], "c_gt")
        gtv = gt.ap.rearrange("p (s r k) -> p s r k", s=N2, r=2)
        f2 = A.bf(N2, "f2", parts=2 * N2)
        DMA(f2.ap[:], F2c[:, :], [], [f2], "c_f2")
        fc = A.bf(NCC * 2 * GDm, "fc")
        DMA(fc.ap[:], FCc[:, :], [], [fc], "c_fc")
        fcv = fc.ap.rearrange("p (cc r e) -> p cc r e", cc=NCC, r=2)
        SG = min(8, N2)
        xs_s = [A.bf(SG * AW, "xs") for _ in range(2)]
        bo_s = [A.bf(SG * 2 * AW, "bo") for _ in range(2)]
        PPC = RCH // N2
        NH = AW // 512
        for sg_ in range(N2 // SG):
            xs = xs_s[sg_ % 2]; bo = bo_s[sg_ % 2]
            xsv = xs.ap.rearrange("p (j c) -> p j c", j=SG)
            bov = bo.ap.rearrange("p (j r c) -> p j r c", j=SG, r=2)
            for rk in range(2):
                for j in range(NCH):
                    p0 = rk * (SC // N2) + j * PPC
                    srcv = xd_pair[j, rk * RCH:(rk + 1) * RCH, :].rearrange("(s1 s2) c -> s1 s2 c", s2=N2)
                    DMA(xsv[p0:p0 + PPC], srcv[:, sg_ * SG:(sg_ + 1) * SG, :], [d_xd_pair], [xs], "ld_xs%d" % (sg_ % 2))
            for j in range(SG):
                s2 = sg_ * SG + j
                for r in range(2):
                    for hf in range(NH):
                        bk = bank()
                        MM(bk.ap[:], gtv[:, s2, r, :], xsv[:, j, hf * 512:(hf + 1) * 512], True, True, [gt, xs], [bk])
                        CP(alt(), bov[:, j, r, hf * 512:(hf + 1) * 512], bk.ap[:], [bk], [bo])
            DMA(Bs[:, sg_ * SG:(sg_ + 1) * SG, :, :], bov, [bo], [d_Bs], "st_bo%d" % (sg_ % 2))
            cv.pump()
        KP = 2 * N2
        NK2 = N2 // 2
        bk_s = [A.bf(8 * GDm, "bkS", parts=KP) for q in range(2)]
        wgt = [A.bf(2 * SC, "wgt") for _ in range(NCC)]
        yd_s = [A.bf(T, "yd") for _ in range(2)]
        for g in range(4):
            for k1g in range(16):
                bks = bk_s[k1g % 2]
                bkv = bks.ap.rearrange("p (k c) -> p k c", k=8)
                DMA(bkv, Bs[k1g * 8:(k1g + 1) * 8, :, :, g * GDm:(g + 1) * GDm].rearrange("k s r c -> (s r) k c"), [d_Bs], [bks], "ld_bk%d" % (k1g % 2))
                for cc in range(NCC):
                    bk = bank()
                    for j in range(8):
                        MM(bk.ap[:, j * N2:(j + 1) * N2], bkv[:, j, cc * 128:(cc + 1) * 128], f2.ap[:], True, True, [bks, f2], [bk])
                    o = wgt[cc].ap.rearrange("p (r k2 k1) -> p r k2 k1", r=2, k2=NK2)[:, :, :, k1g * 8:(k1g + 1) * 8]
                    i_ap = bk.ap[:, 0:8 * N2].rearrange("p (j r k2) -> p r k2 j", j=8, r=2)
                    CP(alt(), o, i_ap, [bk], [wgt[cc]])
                cv.pump()
            for tt in range(NT):
                for eh in range(NCC):
                    bk = bank()
                    n_ = 0
                    for cc in range(NCC):
                        for r in range(2):
                            rhs = wgt[cc].ap.rearrange("p (r t) -> p r t", r=2)[:, r, tt * T:(tt + 1) * T]
                            MM(bk.ap[:], fcv[:, cc, r, eh * 128:(eh + 1) * 128], rhs, n_ == 0, n_ == 2 * NCC - 1, [fc, wgt[cc]], [bk])
                            n_ += 1
                    yd = yd_s[(tt * NCC + eh) % 2]
                    CP(alt(), yd.ap[:], bk.ap[:], [bk], [yd])
                    r0 = (g * NCC + eh) * 128
                    DMA(ydT[r0:r0 + 128, tt * T:(tt + 1) * T], yd.ap[:], [yd], [d_ydT], "st_yd%d" % ((tt * NCC + eh) % 2))
        phase_end()

    def odd_phase2(l, buf):
        A.reset()
        wst = WStream([A.bf(KD * 128, "ws") for _ in range(6)], "ws")
        for i in range(NT):
            wst.plan([("mo", m, KD) for m in range(KD)])
        oxs[:] = [A.f32(T, "ox") for _ in range(2)]
        cv.attach(NT * KD, eng="pool", frac=0.2)
        A.mark()
        wi = 0
        ydv = ydT.rearrange("(k p) t -> p k t", p=128)
        for i in range(NT):
            A.begin_iter(i == 0)
            xt, xv = load_xt(buf, i, T)
            yd = A.bf(KA * T, "ydl")
            ydlv = yd.ap.rearrange("p (k t) -> p k t", k=KA)
            DMA(ydlv, ydv[:, :, T * i:T * (i + 1)], [d_ydT], [yd], "ld_ydl")
            for m in range(KD):
                wo, sl = wst.next()
                bk = bank()
                for k in range(KA):
                    MM(bk.ap[:], wo[:, KA + k, :], ydlv[:, k, :], k == 0, k == KA - 1, [sl, yd], [bk])
                out_res(xv, m, 0, bk.ap[:], bk, buf, i, [xt])
                cv.pump()
        phase_end()

    def halo_exchange(b):
        if not use_cc:
            return
        A.reset()
        xvw = xview(b)
        DMA(exin.rearrange("(k p) w -> p k w", p=128)[:, :, 0:H], xvw[:, :, H:2 * H], [xtile[b][0]], [d_ex_in], "ex_a")
        DMA(exin.rearrange("(k p) w -> p k w", p=128)[:, :, H:2 * H], xvw[:, :, SC:SC + H], [xtile[b][NT - 1]], [d_ex_in], "ex_a")
        S.op("pool", lambda e: e.collective_compute("AllGather", ALU.bypass, replica_groups=[[0, 1], [2, 3], [4, 5], [6, 7]],
                                                     ins=[exin.opt()], outs=[exout.opt()]), [d_ex_in], [d_ex_out])
        hl = A.f32(KD * H, "hl"); hr = A.f32(KD * H, "hr")
        exv = exout.rearrange("(r k p) w -> r p k w", r=2, p=128)
        DMA(hl.ap.rearrange("p (k h) -> p k h", k=KD), exv[0][:, :, H:2 * H], [d_ex_out], [hl], "ex_hl")
        DMA(hr.ap.rearrange("p (k h) -> p k h", k=KD), exv[1][:, :, 0:H], [d_ex_out], [hr], "ex_hr")
        TS("dve", hl.ap[:], hl.ap[:], vcol(c.v_mask + 0), None, ALU.mult, None, [hl, vecs_t], [hl])
        TS("dve", hr.ap[:], hr.ap[:], vcol(c.v_mask + 1), None, ALU.mult, None, [hr, vecs_t], [hr])
        DMA(xvw[:, :, 0:H], hl.ap.rearrange("p (k h) -> p k h", k=KD), [hl], [xhalo[b][0]], "ex_sl")
        DMA(xvw[:, :, H + SC:H + SC + H], hr.ap.rearrange("p (k h) -> p k h", k=KD), [hr], [xhalo[b][1]], "ex_sr")
        phase_end()

    def epilogue(src):
        A.reset()
        yo = [A.f32(D, "yo") for _ in range(2)]
        A.mark()
        for i in range(NT):
            A.begin_iter(i == 0)
            xt, xv = load_xt(src, i, T)
            hs = rmsnorm(xt, xv, T, c.v_nfin, out_dt=F32, name="hf")
            for sb_ in range(4):
                y_ = yo[sb_ % 2]
                for kg in range(KD // 4):
                    bk = bank()
                    for kk in range(4):
                        k = kg * 4 + kk
                        S.op("pe", (lambda o, i_: lambda e: e.transpose(o, i_, ident.ap[:]))(bk.ap[:, kk * 128:(kk + 1) * 128], hs[k].ap[:, sb_ * 128:(sb_ + 1) * 128]), [hs[k], ident], [bk])
                    CP(alt(), y_.ap[:, kg * 512:(kg + 1) * 512], bk.ap[:], [bk], [y_])
                r0 = T * i + 128 * sb_
                DMA(y_out[r0:r0 + 128, :], y_.ap[:], [y_], [], "st_y%d" % (sb_ % 2))
        phase_end()

    halo_exchange(cur)
    cvt_flush_phase(leave=len(cv.queue) - max(0, n_mix0 - cv.done))
    n_ffn0 = len(cv.queue)
    for l in range(c.depth):
        set_layer_weights(l)
        if l + 1 < c.depth:
            add_mix(l + 1); add_ffn(l + 1)
        n_next = len(cv.queue) - (n_ffn0 if l == 0 else 0)
        must_box[0] = n_ffn0 if l == 0 else 0
        if l % 2 == 0:
            even_phase(l, cur, 1 - cur)
            cur = 1 - cur
        else:
            odd_phase1(l, cur, 1 - cur)
            cur = 1 - cur
            dft_phase()
            odd_phase2(l, cur)
        if l == 0:
            cvt_flush_phase(leave=n_next)
        halo_exchange(cur)
        ffn_phase(l, cur, 1 - cur)
        cur = 1 - cur
        cvt_flush_phase()
        if l < c.depth - 1:
            halo_exchange(cur)
    epilogue(cur)
    S.run()
    return nc, S


def _col(v):
    v = np.asarray(v, np.float32)
    return v.reshape(-1, 128).T


def dft_consts(cfg, is_prompt, h):
    N2 = cfg.N2; SC = cfg.SC; NK2 = N2 // 2
    GD = cfg.AW // 4; NCC = GD // 128
    s1 = np.arange(128, dtype=np.float64)[:, None]; k1 = np.arange(128, dtype=np.float64)[None, :]
    G = np.zeros((128, N2, 2, 128), np.float64)
    RPC = SC // N2
    for s2 in range(N2):
        if is_prompt:
            Sq = 2 * SC
            th = 2 * np.pi * (k1 * s1 / 128.0 + k1 * s2 / Sq)
            valid = np.ones_like(th)
        else:
            Sq = SC
            s1p = s1 - RPC * h
            valid = ((s1p >= 0) & (s1p < RPC)).astype(np.float64) * np.ones_like(k1)
            th = 2 * np.pi * (k1 * s1p / RPC + k1 * s2 / SC)
        G[:, s2, 0, :] = np.cos(th) * valid
        G[:, s2, 1, :] = -np.sin(th) * valid
    s2 = np.arange(N2, dtype=np.float64)[:, None]; k2 = np.arange(NK2, dtype=np.float64)[None, :]
    if is_prompt:
        ph = 2 * np.pi * (k2 + NK2 * h) * s2 / N2
    else:
        ph = 2 * np.pi * 2 * k2 * s2 / N2
    Fr = np.cos(ph); Fi = -np.sin(ph)
    F2 = np.zeros((2 * N2, N2), np.float64)
    F2[0::2, 0:NK2] = Fr; F2[1::2, 0:NK2] = -Fi
    F2[0::2, NK2:] = Fi; F2[1::2, NK2:] = Fr
    cc_ = np.arange(GD, dtype=np.float64)[:, None]; cp = np.arange(GD, dtype=np.float64)[None, :]
    ps = 2 * np.pi * cc_ * cp / GD
    nrm = 1.0 / np.sqrt(Sq * GD)
    C = np.cos(ps) * nrm; Sn = np.sin(ps) * nrm
    FC = np.zeros((128, NCC, 2, GD), np.float64)
    for q in range(NCC):
        FC[:, q, 0, :] = C[q * 128:(q + 1) * 128, :]
        FC[:, q, 1, :] = Sn[q * 128:(q + 1) * 128, :]
    bf = ml_dtypes.bfloat16
    return (G.reshape(128, -1).astype(np.float32).astype(bf), F2.astype(np.float32).astype(bf),
            FC.reshape(128, -1).astype(np.float32).astype(bf))


def make_in_maps(cfg, inp, n_prompt_seq=2, n_sample_seq=4):
    c = cfg
    depth = c.depth
    vecs = np.zeros((128, c.NV), np.float32)
    for l in range(depth):
        vecs[:, c.v_nmix[l]:c.v_nmix[l] + c.KD] = _col(inp["norm_mix"][l])
        vecs[:, c.v_nffn[l]:c.v_nffn[l] + c.KD] = _col(inp["norm_ffn"][l])
        fw = np.asarray(inp["f_conv_w"][l], np.float32)
        for k in range(3):
            vecs[:, c.v_fw[l] + k * c.KF:c.v_fw[l] + (k + 1) * c.KF] = _col(fw[k])
        vecs[:, c.v_fb[l]:c.v_fb[l] + c.KF] = _col(inp["f_conv_b"][l])
    vecs[:, c.v_nfin:c.v_nfin + c.KD] = _col(inp["norm_final"])
    for i in range(c.NE):
        vecs[:, c.v_bscale[i]:c.v_bscale[i] + c.KA] = _col(inp["b_scale"][i])
    for i in range(c.NO):
        cw = np.asarray(inp["c_conv_w"][i], np.float32)
        for k in range(c.CK):
            vecs[:, c.v_ccw[i] + k * c.KA:c.v_ccw[i] + (k + 1) * c.KA] = _col(cw[k])
        vecs[:, c.v_ccb[i]:c.v_ccb[i] + c.KA] = _col(inp["c_conv_b"][i])
        vecs[:, c.v_lng[i]:c.v_lng[i] + c.KA] = _col(inp["c_ln_g"][i])
        vecs[:, c.v_lnb[i]:c.v_lnb[i] + c.KA] = _col(inp["c_ln_b"][i])
    shared = {
        "ident": np.eye(128, dtype=np.float32),
        "ab_w_in": np.ascontiguousarray(inp["ab_w_in"], np.float32), "ab_w_out": np.ascontiguousarray(inp["ab_w_out"], np.float32),
        "cd_w_in": np.ascontiguousarray(inp["cd_w_in"], np.float32), "cd_w_out": np.ascontiguousarray(inp["cd_w_out"], np.float32),
        "f_w_in": np.ascontiguousarray(inp["f_w_in"], np.float32), "f_w_out": np.ascontiguousarray(inp["f_w_out"], np.float32),
        "a_v_gain": np.ascontiguousarray(inp["a_v_gain"], np.float32),
        "a_w_sT": np.ascontiguousarray(np.transpose(np.asarray(inp["a_w_s"], np.float32), (0, 1, 3, 2))),
        "a_b_s": np.ascontiguousarray(np.asarray(inp["a_b_s"], np.float32).reshape(c.NE, 512)),
        "b_w_g": np.ascontiguousarray(inp["b_w_g"], np.float32),
    }
    xp = np.asarray(inp["x_prompt"], np.float32); xs = np.asarray(inp["x_sample"], np.float32)
    maps = []
    core = 0
    plan = []
    for b in range(n_prompt_seq):
        for h in range(2):
            plan.append((True, b, h))
    for b in range(n_sample_seq):
        plan.append((False, b, len(plan) % 2))
    for (is_p, b, h) in plan:
        m = dict(shared)
        if is_p:
            m["x"] = np.ascontiguousarray(xp[b, h * c.SC:(h + 1) * c.SC, :]); Sq = 2 * c.SC; off = h * c.SC
        else:
            m["x"] = np.ascontiguousarray(xs[b]); Sq = c.SC; off = 0
        v = vecs.copy()
        v[:, c.v_mask + 0] = 1.0 if (is_p and h == 1) else 0.0
        v[:, c.v_mask + 1] = 1.0 if (is_p and h == 0) else 0.0
        m["vecs"] = v
        t = off + np.arange(c.SC)
        ic = np.zeros((4, c.SC), np.float32)
        for g, w in enumerate(POOLW):
            lo = np.maximum(t - w // 2, 0); hi = np.minimum(t + w // 2, Sq)
            ic[g] = 1.0 / (hi - lo).astype(np.float32)
        m["invcnt"] = ic
        G, F2, FC = dft_consts(c, is_p, h)
        m["Gc"] = G; m["F2c"] = F2; m["FCc"] = FC
        maps.append(m)
    return maps, plan


def assemble(cfg, results, plan, n_prompt_seq=2, n_sample_seq=4):
    c = cfg
    yp = np.zeros((n_prompt_seq, 2 * c.SC, c.D), np.float32)
    ys = np.zeros((n_sample_seq, c.SC, c.D), np.float32)
    for r, (is_p, b, h) in zip(results, plan):
        if is_p:
            yp[b, h * c.SC:(h + 1) * c.SC] = r["y"]
        else:
            ys[b] = r["y"]
    return yp, ys


_CFG = Cfg()
_CACHE = {}


def kernel(**inputs):
    cfg = _CFG
    if "nc" not in _CACHE:
        _CACHE["nc"] = build(cfg)[0]
    nc = _CACHE["nc"]
    maps, plan = make_in_maps(cfg, inputs)
    res = run_bass_kernel_spmd(nc, maps, core_ids=list(range(8)))
    yp, ys = assemble(cfg, res.results, plan)
    return (yp, ys)
```

```python
import contextlib
import numpy as np
import ml_dtypes
import concourse.bass as bass
import concourse.mybir as mybir
from concourse.bass_utils import run_bass_kernel_spmd

F32 = mybir.dt.float32
BF16 = mybir.dt.bfloat16
ALU = mybir.AluOpType
AF = mybir.ActivationFunctionType
AX = mybir.AxisListType

EPOCH = 30000
DMA_SEM_MAX = 30000


class Buf:
    def __init__(self, name, ap=None):
        self.name = name
        self.ap = ap
        self.writer = None
        self.readers = {}


class Sched:
    ENGS = ("pe", "act", "dve", "pool", "sp")

    def __init__(self, nc):
        self.nc = nc
        self.prog = {e: [] for e in self.ENGS}
        self.seq = {e: 0 for e in self.ENGS}
        self.known = {e: {} for e in self.ENGS}
        self.dma_cnt = {}
        self.dma_gen = {}
        self.semnames = set()
        self.stack = contextlib.ExitStack()
        self.nalloc = 0
        self.pending = {e: [] for e in self.ENGS}

    def sbuf(self, name, free, dtype, parts=128):
        t = self.stack.enter_context(self.nc.sbuf_tensor(name, [parts] + list(free), dtype))
        return t

    def psum(self, name, free, dtype=F32):
        t = self.stack.enter_context(self.nc.psum_tensor(name, [128] + list(free), dtype))
        return t

    def _deps(self, reads, writes, skip_dsem=None):
        deps = []
        for b in reads:
            if b.writer is not None:
                deps.append(b.writer)
        for b in writes:
            if b.writer is not None:
                if not (skip_dsem is not None and b.writer[0] == skip_dsem):
                    deps.append(b.writer)
            for k, v in b.readers.items():
                deps.append((k, v))
        return deps

    def _filter(self, e, deps, n, is_dma=False):
        waits = []
        kn = self.known[e]
        for (k, v) in deps:
            if k[0] == "E" and k[1] == e and not is_dma:
                if e in ("pe", "sp"):
                    continue
            if kn.get(k, 0) >= v:
                continue
            kn[k] = v
            waits.append((k, v))
        return waits

    def op(self, e, fn, reads=(), writes=()):
        n = self.seq[e] + 1
        self.seq[e] = n
        deps = self._deps(reads, writes)
        waits = self.pending[e] + self._filter(e, deps, n)
        self.pending[e] = []
        ev = (("E", e), n)
        self.prog[e].append((waits, fn, ev))
        for b in reads:
            if b.readers.get(ev[0], 0) < n:
                b.readers[ev[0]] = n
        for b in writes:
            b.writer = ev
            b.readers = {}
        return ev

    def dma(self, q, fn, reads, writes, semkey):
        gen = self.dma_gen.get(semkey, 0)
        cnt = self.dma_cnt.get((semkey, gen), 0)
        if cnt + 16 > DMA_SEM_MAX:
            gen += 1
            self.dma_gen[semkey] = gen
            cnt = 0
        cnt += 16
        self.dma_cnt[(semkey, gen)] = cnt
        k = ("D", semkey, gen)
        deps = self._deps(reads, writes, skip_dsem=k)
        waits = self.pending[q] + self._filter(q, deps, 0, is_dma=True)
        self.pending[q] = []
        ev = (k, cnt)
        self.prog[q].append((waits, fn, ev))
        for b in reads:
            if b.readers.get(k, 0) < cnt:
                b.readers[k] = cnt
        for b in writes:
            b.writer = ev
            b.readers = {}
        return ev

    def barrier(self):
        evs = []
        for e in self.ENGS:
            if self.seq[e] > 0:
                evs.append((("E", e), self.seq[e]))
        for (semkey, gen), cnt in self.dma_cnt.items():
            evs.append((("D", semkey, gen), cnt))
        for e in self.ENGS:
            kn = self.known[e]
            for (k, v) in evs:
                if k[0] == "E" and k[1] == e:
                    continue
                if kn.get(k, 0) >= v:
                    continue
                kn[k] = v
                self.pending[e].append((k, v))

    def _semname(self, k, v=None):
        if k[0] == "E":
            return "e_%s_%d" % (k[1], (v - 1) // EPOCH)
        return "d_%s_%d" % (k[1], k[2])

    def _semval(self, k, v):
        if k[0] == "E":
            return (v - 1) % EPOCH + 1
        return v

    def run(self):
        nc = self.nc
        names = set()
        for e in self.ENGS:
            for (waits, fn, ev) in self.prog[e]:
                names.add(self._semname(ev[0], ev[1]))
        sems = {}
        for nm in sorted(names):
            sems[nm] = self.stack.enter_context(nc.semaphore(nm))
        self.nsem = len(sems)
        block = self.stack.enter_context(nc.Block())
        engmap = {"pe": block.tensor, "act": block.scalar, "dve": block.vector,
                  "pool": block.gpsimd, "sp": block.sync}
        final_waits = []
        for nm in sorted(names):
            pass

        def make(e):
            prog = self.prog[e]

            def body(eng):
                for (waits, fn, ev) in prog:
                    for (k, v) in waits:
                        eng.wait_ge(sems[self._semname(k, v)], self._semval(k, v))
                    ins = fn(eng)
                    k, v = ev
                    if k[0] == "E":
                        ins.then_inc(sems[self._semname(k, v)], 1)
                    else:
                        ins.then_inc(sems[self._semname(k, v)], 16)
                for (k, v) in self.pending[e]:
                    eng.wait_ge(sems[self._semname(k, v)], self._semval(k, v))
                if e == "sp":
                    for (semkey, gen), cnt in self.dma_cnt.items():
                        eng.wait_ge(sems["d_%s_%d" % (semkey, gen)], cnt)
                    for e2 in self.ENGS:
                        if e2 != "sp" and self.seq[e2] > 0:
                            n = self.seq[e2]
                            eng.wait_ge(sems[self._semname(("E", e2), n)], self._semval(("E", e2), n))
            return body

        for e in self.ENGS:
            if self.prog[e] or e == "sp":
                engmap[e](make(e))
        self.stack.close()


import os
DBG_FLUSH = int(os.environ.get('DBG_FLUSH', '0'))
DBG_NOPIPE = int(os.environ.get('DBG_NOPIPE', '0'))
H = 16
T = 512
W = T + 2 * H
EPS = 1e-6
POOLW = (2, 4, 8, 16)


class Cfg:
    def __init__(self, D=2048, DFF=5632, SC=4096, depth=4):
        self.D = D; self.DFF = DFF; self.SC = SC; self.depth = depth
        self.KD = D // 128; self.KF = DFF // 128
        self.AW = D // 2; self.KA = self.AW // 128
        self.NT = SC // T
        self.XW = SC + 2 * H
        self.N2 = 2 * SC // 128
        self.NE = (depth + 1) // 2; self.NO = depth // 2
        self.CK = 31
        c = 0
        self.v_nmix = []; self.v_nffn = []
        for l in range(depth):
            self.v_nmix.append(c); c += self.KD
            self.v_nffn.append(c); c += self.KD
        self.v_nfin = c; c += self.KD
        self.v_bscale = []
        for i in range(self.NE):
            self.v_bscale.append(c); c += self.KA
        self.v_ccw = []; self.v_ccb = []; self.v_lng = []; self.v_lnb = []
        for i in range(self.NO):
            self.v_ccw.append(c); c += self.CK * self.KA
            self.v_ccb.append(c); c += self.KA
            self.v_lng.append(c); c += self.KA
            self.v_lnb.append(c); c += self.KA
        self.v_fw = []; self.v_fb = []
        for l in range(depth):
            self.v_fw.append(c); c += 3 * self.KF
            self.v_fb.append(c); c += self.KF
        self.v_mask = c; c += 2
        self.NV = c


class Arena:
    def __init__(self, S, nfloats):
        self.t = S.sbuf("arena", [nfloats], F32)
        self.n = nfloats
        self.reset()

    def reset(self):
        self.hw = max(getattr(self, "hw", 0), getattr(self, "o", 0))
        self.o = 0; self.cnt = 0; self.replaying = False; self.log = None

    def mark(self):
        self.log = []; self.replaying = False

    def begin_iter(self, first):
        if first:
            self.log = []; self.replaying = False
        else:
            self.replaying = True; self.ri = 0

    def _replay(self, n, kind):
        b, meta = self.log[self.ri]
        self.ri += 1
        assert meta == (n, kind), (meta, n, kind)
        return b

    def f32(self, n, name="t", parts=128):
        if self.replaying:
            return self._replay(n, "f")
        assert self.o + n <= self.n, ("arena overflow", name, self.o, n, self.n)
        ap = self.t[0:parts, self.o:self.o + n]
        self.o += n
        self.cnt += 1
        b = Buf("%s%d" % (name, self.cnt), ap)
        if self.log is not None:
            self.log.append((b, (n, "f")))
        return b

    def bf(self, n, name="t", parts=128):
        if self.replaying:
            return self._replay(n, "b")
        nf = (n + 1) // 2
        assert self.o + nf <= self.n, ("arena overflow", name, self.o, nf, self.n)
        ap = self.t[0:parts, self.o:self.o + nf].bitcast(BF16)[:, 0:n]
        self.o += nf
        self.cnt += 1
        b = Buf("%s%d" % (name, self.cnt), ap)
        if self.log is not None:
            self.log.append((b, (n, "b")))
        return b


def build(cfg, use_cc=True, debug_out=None):
    nc = bass.Bass("TRN2", target_bir_lowering=False)
    c = cfg
    D, DFF, SC, KD, KF, KA, AW, NT, XW, N2 = c.D, c.DFF, c.SC, c.KD, c.KF, c.KA, c.AW, c.NT, c.XW, c.N2
    NE, NO = c.NE, c.NO

    def din(name, shape, dt=F32):
        return nc.dram_tensor(name, list(shape), dt, kind="ExternalInput").ap()

    x_in = din("x", [SC, D])
    vecs = din("vecs", [128, c.NV])
    ident_in = din("ident", [128, 128])
    ab_w_in = din("ab_w_in", [NE, D, 3 * AW]); ab_w_out = din("ab_w_out", [NE, D, D])
    cd_w_in = din("cd_w_in", [NO, D, 3 * AW]); cd_w_out = din("cd_w_out", [NO, D, D])
    f_w_in = din("f_w_in", [c.depth, D, 2 * DFF]); f_w_out = din("f_w_out", [c.depth, DFF, D])
    a_v_gain = din("a_v_gain", [NE, AW]); a_w_sT = din("a_w_sT", [NE, 4, 128, 128]); a_b_s = din("a_b_s", [NE, 512])
    b_w_g = din("b_w_g", [NE, 4, AW // 4, AW // 4])
    invcnt = din("invcnt", [4, SC])
    Gc = din("Gc", [128, N2 * 2 * 128], BF16)
    F2c = din("F2c", [2 * N2, N2], BF16)
    GDm = AW // 4; NCC = GDm // 128
    FCc = din("FCc", [128, NCC * 2 * GDm], BF16)
    y_out = nc.dram_tensor("y", [SC, D], F32, kind="ExternalOutput").ap()

    def dscr(name, shape, dt=F32):
        return nc.dram_tensor(name, list(shape), dt).ap()

    xbuf = [dscr("xa", [D, XW]), dscr("xb", [D, XW])]
    RCH = min(SC, 1024); NCH = SC // RCH
    xd_loc = dscr("xd_loc", [SC, AW], BF16)
    xd_pair = dscr("xd_pair", [NCH, 2 * RCH, AW], BF16)
    Bs = dscr("Bs", [128, N2, 2, AW], BF16)
    ydT = dscr("ydT", [AW, SC], BF16)
    exin = dscr("exin", [D, 2 * H]); exout = dscr("exout", [2 * D, 2 * H])
    WS = [{"mi": dscr("wmi%d" % q, [3 * KA, 128, KD, 128], BF16), "mo": dscr("wmo%d" % q, [KD, 128, KD, 128], BF16),
           "fi": dscr("wfi%d" % q, [2 * KF, 128, KD, 128], BF16), "fo": dscr("wfo%d" % q, [KD, 128, KF, 128], BF16)} for q in range(2)]
    WC = dict(WS[0])

    S = Sched(nc)
    A = Arena(S, 51500)
    vecs_t = Buf("vecs", S.sbuf("vecs_t", [c.NV], F32))
    ident = Buf("ident", S.sbuf("ident_t", [128], F32))
    ones_b = Buf("ones", S.sbuf("ones_t", [128], BF16))
    identb = Buf("identb", S.sbuf("identb_t", [128], BF16))
    PSUM = S.psum("psum_all", [8 * 512], F32)
    banks = [Buf("bank%d" % i, PSUM[:, i * 512:(i + 1) * 512]) for i in range(8)]
    st = {"sb": 0, "pr": 0}

    def bank():
        b = banks[4 + st["sb"] % 4]; st["sb"] += 1
        return b

    def pair():
        i = (st["pr"] % 2) * 2; st["pr"] += 1
        return (banks[i], banks[i + 1]), PSUM[:, i * 512:(i + 2) * 512]

    def MM(out, lhsT, rhs, start, stop, reads, writes):
        S.op("pe", lambda e: e.matmul(out, lhsT=lhsT, rhs=rhs, start=start, stop=stop), reads, writes)

    def ACTF(out, in_, func, reads, writes, **kw):
        S.op("act", lambda e: e.activation(out=out, in_=in_, func=func, **kw), reads, writes)

    def TT(eng, out, in0, in1, op, reads, writes):
        S.op(eng, lambda e: e.tensor_tensor(out=out, in0=in0, in1=in1, op=op), reads, writes)

    def TS(eng, out, in0, s1, s2, op0, op1, reads, writes):
        if s2 is None:
            S.op(eng, lambda e: e.tensor_scalar(out=out, in0=in0, scalar1=s1, scalar2=None, op0=op0), reads, writes)
        else:
            S.op(eng, lambda e: e.tensor_scalar(out=out, in0=in0, scalar1=s1, scalar2=s2, op0=op0, op1=op1), reads, writes)

    def STT(out, in0, scalar, in1, op0, op1, reads, writes):
        S.op("dve", lambda e: e.scalar_tensor_tensor(out=out, in0=in0, scalar=scalar, in1=in1, op0=op0, op1=op1), reads, writes)

    def CP(eng, out, in_, reads, writes):
        if eng == "act":
            S.op("act", lambda e: e.activation(out=out, in_=in_, func=AF.Copy), reads, writes)
        else:
            S.op(eng, lambda e: e.tensor_copy(out=out, in_=in_), reads, writes)

    def DMA(out, in_, reads, writes, semkey, q="sp"):
        S.dma(q, lambda e: e.dma_start(out=out, in_=in_), reads, writes, semkey)

    def vcol(col, n=1):
        return vecs_t.ap[:, col:col + n]

    rr = {"i": 0}

    def alt(engs=("act", "dve")):
        rr["i"] += 1
        return engs[rr["i"] % len(engs)]

    def phase_end():
        import sys
        pass
        if cv_box and cv_box[0].st is not None:
            cv_box[0]._store()
            cv_box[0].st = None
        S.barrier()
        A.reset()

    cv_box = []
    must_box = [0]

    xtile = [[Buf("x%d_%d" % (b, i)) for i in range(NT)] for b in range(2)]
    xhalo = [[Buf("xh%d_%d" % (b, i)) for i in range(2)] for b in range(2)]
    d_xd_loc = Buf("xd_loc"); d_xd_pair = Buf("xd_pair"); d_Bs = Buf("Bs"); d_ydT = Buf("ydT")
    class DW:
        def __init__(self, nm):
            self.nm = nm; self.b = {}

        def get(self, n, k0):
            if (n, k0) not in self.b:
                self.b[(n, k0)] = Buf("dw_%s_%d_%d" % (self.nm, n, k0))
            return self.b[(n, k0)]

        def chunk(self, n):
            return [b for (nn, k0), b in self.b.items() if nn == n]

    DWS = [{k: DW(k + str(q)) for k in ("mi", "mo", "fi", "fo")} for q in range(2)]
    d_w = dict(DWS[0])

    def set_layer_weights(l):
        WC.update(WS[l % 2]); d_w.update(DWS[l % 2])
    d_ex_in = Buf("exin"); d_ex_out = Buf("exout")

    def xreads(b, i):
        r = [xtile[b][i]]
        r.append(xtile[b][i - 1] if i > 0 else xhalo[b][0])
        r.append(xtile[b][i + 1] if i < NT - 1 else xhalo[b][1])
        return r

    def xview(b):
        return xbuf[b].rearrange("(k p) w -> p k w", p=128)

    DMA(vecs_t.ap[:], vecs[:, :], [], [vecs_t], "c_vecs")
    DMA(ident.ap[:], ident_in[:, :], [], [ident], "c_ident")
    S.op("dve", lambda e: e.memset(ones_b.ap[:], 1.0), [], [ones_b])
    S.op("dve", lambda e: e.tensor_copy(out=identb.ap[:], in_=ident.ap[:]), [ident], [identb])
    zt = A.f32(KD * H, "zt")
    S.op("dve", lambda e: e.memset(zt.ap[:], 0.0), [], [zt])
    for b in range(2):
        for side in range(2):
            c0 = 0 if side == 0 else H + SC
            DMA(xview(b)[:, :, c0:c0 + H], zt.ap.rearrange("p (k h) -> p k h", k=KD), [zt], [xhalo[b][side]], "zt_st")

    def load_xt(b, i, width, name="xt", slot=None, key=None):
        hh = (width - T) // 2
        xt = slot if slot is not None else A.f32(KD * width, name)
        if key is not None:
            name = key
        v = xt.ap.rearrange("p (k w) -> p k w", k=KD)
        c0 = T * i + H - hh
        DMA(v, xview(b)[:, :, c0:c0 + width], xreads(b, i) if hh > 0 else [xtile[b][i]], [xt], "ld_" + name)
        return xt, v

    def norm_bufs(width, out_dt=BF16, name="h"):
        return {"sq": [A.bf(width, "sq") for _ in range(2)], "rs": A.f32(width, "rstd"),
                "hs": [A.bf(width, name) if out_dt == BF16 else A.f32(width, name) for _ in range(KD)]}

    def rmsnorm(xt, xv, width, gcol, out_dt=BF16, name="h", nb=None):
        (b0, b1), pp = pair()
        sq = nb["sq"] if nb else [A.bf(width, "sq") for _ in range(2)]
        scale = float(D) ** -0.5
        for k in range(KD):
            sb = sq[k % 2]
            ACTF(sb.ap[:], xv[:, k, :], AF.Square, [xt], [sb], scale=scale)
            w0 = min(width, 512)
            MM(pp[:, 0:w0], ones_b.ap[:], sb.ap[:, 0:w0], k == 0, k == KD - 1, [ones_b, sb], [b0])
            if width > 512:
                MM(pp[:, 512:width], ones_b.ap[:], sb.ap[:, 512:width], k == 0, k == KD - 1, [ones_b, sb], [b1])
        rs = nb["rs"] if nb else A.f32(width, "rstd")
        ACTF(rs.ap[:], pp[:, 0:width], AF.Sqrt, [b0, b1, eps_t], [rs], bias=eps_t.ap[:, 0:1], scale=1.0)
        S.op("dve", lambda e: e.reciprocal(out=rs.ap[:], in_=rs.ap[:]), [rs], [rs])
        hs = []
        for k in range(KD):
            hb = nb["hs"][k] if nb else (A.bf(width, name) if out_dt == BF16 else A.f32(width, name))
            STT(hb.ap[:], xv[:, k, :], vcol(gcol + k), rs.ap[:], ALU.mult, ALU.mult, [xt, rs, vecs_t], [hb])
            hs.append(hb)
        return hs

    eps_t = Buf("eps", S.sbuf("eps_t", [1], F32))
    S.op("dve", lambda e: e.memset(eps_t.ap[:], EPS), [], [eps_t])

    def load_w(scr, dbuf, n, kk, slot, name, k0=None, k1=None):
        if k0 is None:
            DMA(slot.ap.rearrange("p (k c) -> p k c", k=kk), scr[n], dbuf.chunk(n), [slot], "ldw_" + name)
            return slot.ap.rearrange("p (k c) -> p k c", k=kk)
        nk = k1 - k0
        v = slot.ap[:, 0:nk * 128].rearrange("p (k c) -> p k c", k=nk)
        DMA(v, scr[n][:, k0:k1, :], dbuf.chunk(n), [slot], "ldw_" + name)
        return v

    class WStream:
        def __init__(self, slots, tag, hold=1):
            self.slots = slots; self.tag = tag; self.NS = len(slots); self.hold = hold
            self.seq = []; self.issued = 0; self.views = {}; self.pos = 0

        def plan(self, items):
            self.seq += items

        def next(self):
            j = self.pos; self.pos += 1
            while self.issued < min(len(self.seq), j + self.NS - self.hold + 1):
                q = self.issued
                it_ = self.seq[q]
                key, n, kk = it_[0], it_[1], it_[2]
                slot = self.slots[q % self.NS]
                if len(it_) == 5:
                    v = load_w(WC[key], d_w[key], n, kk, slot, "%s%d" % (self.tag, q % self.NS), k0=it_[3], k1=it_[4])
                else:
                    v = load_w(WC[key], d_w[key], n, kk, slot, "%s%d" % (self.tag, q % self.NS))
                self.views[q] = (v, slot)
                self.issued += 1
            return self.views.pop(j)

    class Converter:
        KH = 4

        def __init__(self):
            self.queue = []
            self.pendq = []
            self.done = 0
            self.st = None
            self.it = 0
            self.rate = 0.0
            self.acc = 0.0

        def add(self, src, K, N, scr, dbuf):
            kk = K // 128
            sv = src.rearrange("(k p) n -> p k n", p=128)
            for n in range(N // 128):
                for k0 in range(0, kk, self.KH):
                    kh = min(self.KH, kk - k0)
                    self.queue.append((sv, scr, dbuf.get(n, k0), n, k0, kh))

        def attach(self, nsteps=0, eng="pool", frac=1.0, must=0, nst=4):
            self.NST = nst
            self.st = [(A.f32(self.KH * 128, "cvf"), A.bf(self.KH * 128, "cvb")) for _ in range(self.NST)]
            want = max(frac * len(self.queue), min(must, len(self.queue)))
            self.rate = (want / float(nsteps) * 1.1) if nsteps else 0.0
            self.acc = 0.0
            self.eng = eng
            self.pendq = []

        def _store(self, all_=True):
            while self.pendq and (all_ or len(self.pendq) >= self.NST - 1):
                (bt, scr, db, n, k0, kh, q) = self.pendq.pop(0)
                DMA(scr[n, :, k0:k0 + kh, :], bt.ap[:, 0:kh * 128].rearrange("p (k c) -> p k c", k=kh), [bt], [db], "cvs%d" % q, q="act")

        def one(self):
            self._store(all_=False)
            if not self.queue:
                self._store(all_=True)
                return
            (sv, scr, db, n, k0, kh) = self.queue.pop(0)
            self.done += 1
            q = self.it % self.NST; self.it += 1
            f, bt = self.st[q]
            DMA(f.ap[:, 0:kh * 128].rearrange("p (k c) -> p k c", k=kh), sv[:, k0:k0 + kh, n * 128:(n + 1) * 128], [], [f], "cvl%d" % q, q="act")
            CP(self.eng, bt.ap[:, 0:kh * 128], f.ap[:, 0:kh * 128], [f], [bt])
            self.pendq.append((bt, scr, db, n, k0, kh, q))

        def pump(self):
            self.acc += self.rate
            while self.acc >= 1.0:
                self.acc -= 1.0
                self.one()

        def flush(self, leave=0):
            while len(self.queue) > leave:
                self.one()
            self._store(all_=True)

    cv = Converter()
    cv_box.append(cv)

    def add_mix(l):
        i = l // 2; q = l % 2
        if l % 2 == 0:
            cv.add(ab_w_in[i], D, 3 * AW, WS[q]["mi"], DWS[q]["mi"]); cv.add(ab_w_out[i], D, D, WS[q]["mo"], DWS[q]["mo"])
        else:
            cv.add(cd_w_in[i], D, 3 * AW, WS[q]["mi"], DWS[q]["mi"]); cv.add(cd_w_out[i], D, D, WS[q]["mo"], DWS[q]["mo"])

    def add_ffn(l):
        q = l % 2
        cv.add(f_w_in[l], D, 2 * DFF, WS[q]["fi"], DWS[q]["fi"]); cv.add(f_w_out[l], DFF, D, WS[q]["fo"], DWS[q]["fo"])

    def out_res(xv, m, hh, ps_ap, psb, dstb, i, rd):
        ox = oxs[m % 2]
        TT("dve", ox.ap[:], xv[:, m, hh:hh + T], ps_ap, ALU.add, rd + [psb], [ox])
        DMA(xview(dstb)[:, m, H + T * i:H + T * i + T], ox.ap[:], [ox], [xtile[dstb][i]], "ox%d" % (m % 2))

    xin = [A.f32(4 * D, "xin") for _ in range(1)]
    xo = [A.f32(T, "xo") for _ in range(4)]
    add_mix(0)
    n_mix0 = len(cv.queue)
    add_ffn(0)
    cv.attach(NT * KD, eng="pool", frac=0.0, must=n_mix0 + (len(cv.queue) - n_mix0) // 4)
    for i in range(NT):
        xi = xin[0]
        xiv = xi.ap.rearrange("p (s d) -> p s d", s=4)
        DMA(xiv, x_in[T * i:T * (i + 1), :].rearrange("(s p) d -> p s d", p=128), [], [xi], "ld_xin")
        for k in range(KD):
            bk = bank()
            for s_ in range(4):
                S.op("pe", (lambda o, i_: lambda e: e.transpose(o, i_, ident.ap[:]))(bk.ap[:, s_ * 128:(s_ + 1) * 128], xiv[:, s_, k * 128:(k + 1) * 128]), [xi, ident], [bk])
            o = xo[k % 4]
            CP(alt(), o.ap[:], bk.ap[:], [bk], [o])
            DMA(xview(0)[:, k, H + T * i:H + T * (i + 1)], o.ap[:], [o], [xtile[0][i]], "st_xo%d" % (k % 4))
            cv.pump()
    phase_end()
    cur = 0

    def ffn_phase(l, src, dst):
        A.reset()
        acts = [A.bf(T, "act") for _ in range(KF)]
        w1s = WStream([A.bf(KD * 128, "w1") for _ in range(4)], "w1", hold=2)
        NWO = 3
        wos_ = WStream([A.bf(KF * 128, "wo") for _ in range(NWO)], "wo")
        for i in range(NT):
            for cc in range(KF):
                w1s.plan([("fi", cc, KD), ("fi", KF + cc, KD)])
            wos_.plan([("fo", m, KF) for m in range(KD)])
        cvs = [A.f32(T, "cv") for _ in range(2)]
        oxs[:] = [A.f32(T, "ox") for _ in range(2)]
        xts = [A.f32(KD * W, "xt") for _ in range(2)]
        nb = norm_bufs(W)
        cv.attach(NT * (KF + KD), eng="pool", frac=1.0, nst=3)
        fw = c.v_fw[l]; fb = c.v_fb[l]
        xt, xv = load_xt(src, 0, W, slot=xts[0], key="xt0")
        hs = rmsnorm(xt, xv, W, c.v_nffn[l], nb=nb)
        wi = 0
        for i in range(NT):
            nxt = None
            if DBG_NOPIPE and i > 0:
                xt, xv = load_xt(src, i, W, slot=xts[i % 2], key="xt%d" % (i % 2))
                hs = rmsnorm(xt, xv, W, c.v_nffn[l], nb=nb)
            for cc in range(KF):
                wa, wa_slot = w1s.next()
                wg, wg_slot = w1s.next()
                if cc == min(2, KF - 1) and i + 1 < NT and not DBG_NOPIPE:
                    nxt = load_xt(src, i + 1, W, slot=xts[(i + 1) % 2], key="xt%d" % ((i + 1) % 2))
                (b0, b1), pa = pair()
                pg = bank()
                for k in range(KD):
                    MM(pa[:, 0:512], wa[:, k, :], hs[k].ap[:, 0:512], k == 0, k == KD - 1, [wa_slot, hs[k]], [b0])
                    MM(pa[:, 512:W], wa[:, k, :], hs[k].ap[:, 512:W], k == 0, k == KD - 1, [wa_slot, hs[k]], [b1])
                for k in range(KD):
                    MM(pg.ap[:], wg[:, k, :], hs[k].ap[:, H:H + T], k == 0, k == KD - 1, [wg_slot, hs[k]], [pg])
                cvt_ = cvs[cc % 2]; gl = cvt_
                ACTF(cvt_.ap[:], pa[:, H:H + T], AF.Identity, [b0, b1, vecs_t], [cvt_], scale=vcol(fw + 1 * KF + cc), bias=vcol(fb + cc))
                STT(cvt_.ap[:], pa[:, H - 1:H - 1 + T], vcol(fw + 0 * KF + cc), cvt_.ap[:], ALU.mult, ALU.add, [b0, b1, cvt_, vecs_t], [cvt_])
                STT(cvt_.ap[:], pa[:, H + 1:H + 1 + T], vcol(fw + 2 * KF + cc), cvt_.ap[:], ALU.mult, ALU.add, [b0, b1, cvt_, vecs_t], [cvt_])
                ACTF(gl.ap[:], cvt_.ap[:], AF.Gelu_apprx_tanh, [cvt_], [gl])
                TT("dve", acts[cc].ap[:], gl.ap[:], pg.ap[:], ALU.mult, [gl, pg], [acts[cc]])
                cv.pump()
            wo_first = wos_.next()
            xt_cur, xv_cur = xt, xv
            if nxt is not None:
                xt, xv = nxt
                hs = rmsnorm(xt, xv, W, c.v_nffn[l], nb=nb)
            for m in range(KD):
                wo, wo_slot = wo_first if m == 0 else wos_.next()
                po = bank()
                for cc in range(KF):
                    MM(po.ap[:], wo[:, cc, :], acts[cc].ap[:], cc == 0, cc == KF - 1, [wo_slot, acts[cc]], [po])
                out_res(xv_cur, m, H, po.ap[:], po, dst, i, [xt_cur])
                cv.pump()
        phase_end()

    oxs = [None, None]

    def cvt_flush_phase(leave=0):
        if len(cv.queue) <= leave:
            return
        A.reset()
        cv.attach(0)
        cv.flush(leave)
        phase_end()

    def even_phase(l, src, dst):
        A.reset()
        i_ = l // 2
        GD = AW // 4
        NDC = GD // 128
        wsb = A.bf(4 * 128, "wsb")
        vg = A.f32(AW, "vg")
        bsb = A.bf(512, "bsb"); on2 = A.bf(128, "on2")
        wgb = A.bf(4 * NDC * GD, "wgb")
        m0 = A.o
        wsf = A.f32(4 * 128, "wsf")
        DMA(wsf.ap.rearrange("q (h p) -> q h p", h=4), a_w_sT[i_].rearrange("h q p -> q h p"), [], [wsf], "c_wsf")
        CP("dve", wsb.ap[:], wsf.ap[:], [wsf], [wsb])
        DMA(vg.ap[:], a_v_gain[i_, :].partition_broadcast(128), [], [vg], "c_vg")
        bsf = A.f32(512, "bsf"); bs2 = A.f32(512, "bs2")
        S.op("pool", lambda e: e.memset(bsf.ap[:], 0.0), [], [bsf])
        S.op("pool", lambda e: e.memset(bs2.ap[:], 0.0), [], [bs2])
        S.op("pool", lambda e: e.memset(on2.ap[:], 0.0), [], [on2])
        S.op("pool", lambda e: e.memset(on2.ap[0:2, :], 1.0), [], [on2])
        DMA(bsf.ap[0:1, :], a_b_s[i_:i_ + 1, :], [bsf], [bsf], "c_bsf")
        DMA(bsf.ap[1:2, :], a_b_s[i_:i_ + 1, :], [bsf], [bsf], "c_bsf")
        CP("dve", bsb.ap[:], bsf.ap[:], [bsf], [bsb])
        TT("dve", bs2.ap[:], bsf.ap[:], bsb.ap[:], ALU.subtract, [bsf, bsb], [bs2])
        bs3 = A.bf(512, "bs3")
        CP("dve", bs3.ap[:], bs2.ap[:], [bs2], [bs3])
        DMA(bsb.ap[1:2, :], bs3.ap[0:1, :], [bs3, bsb], [bsb], "c_bsb")
        wgf = A.f32(4 * NDC * GD, "wgf")
        DMA(wgf.ap.rearrange("p (g dc e) -> p g dc e", g=4, dc=NDC), b_w_g[i_].rearrange("g (dc p) e -> p g dc e", p=128), [], [wgf], "c_wgf")
        CP("pool", wgb.ap[:], wgf.ap[:], [wgf], [wgb])
        wgv = wgb.ap.rearrange("p (g dc e) -> p g dc e", g=4, dc=NDC)
        S.barrier()
        A.o = m0
        wv = A.bf(KA * KD * 128, "wv")
        wvv = wv.ap.rearrange("p (n k c) -> p n k c", n=KA, k=KD)
        for n in range(KA):
            DMA(wvv[:, n, :, :], WC["mi"][KA + n], d_w["mi"].chunk(KA + n), [wv], "c_wv")
        wst = WStream([A.bf(KD * 128, "ws") for _ in range(4)], "ws")
        for i in range(NT):
            wst.plan([("mi", ch, KD) for ch in range(KA)] + [("mi", 2 * KA + ch, KD) for ch in range(KA)] + [("mo", m, KD) for m in range(KD)])
        oxs[:] = [A.f32(T, "ox") for _ in range(2)]
        ic = [A.f32(T, "ic") for _ in range(2)]
        cv.attach(NT * (2 * KA + KD), eng="act", frac=0.35, must=must_box[0], nst=3)
        xrs = [A.f32(T, "xr") for _ in range(2)]
        xt = A.f32(KD * W, "xt")
        nb = norm_bufs(W)
        _, xv = load_xt(src, 0, W, slot=xt, key="xt")
        hs = rmsnorm(xt, xv, W, c.v_nmix[l], nb=nb)
        if NT > 1:
            load_xt(src, 1, W, slot=xt, key="xt")
        A.mark()
        wi = 0
        for i in range(NT):
            A.begin_iter(i == 0)
            us = []
            for ch in range(KA):
                wu, sl = wst.next()
                bk = bank()
                for k in range(KD):
                    MM(bk.ap[:], wu[:, k, :], hs[k].ap[:, H:H + T], k == 0, k == KD - 1, [sl, hs[k]], [bk])
                u = A.bf(T, "u")
                ACTF(u.ap[:], bk.ap[:], AF.Gelu_apprx_tanh, [bk], [u])
                us.append(u)
                cv.pump()
            pbs = []
            xbs_ = [A.f32(W, "xb") for _ in range(2)]
            t1s = [A.f32(W, "t1") for _ in range(2)]
            t2s = [A.f32(W, "t2") for _ in range(1)]
            for ch in range(KA):
                wx, sl = wst.next()
                (b0, b1), pp = pair()
                for k in range(KD):
                    MM(pp[:, 0:512], wx[:, k, :], hs[k].ap[:, 0:512], k == 0, k == KD - 1, [sl, hs[k]], [b0])
                    MM(pp[:, 512:W], wx[:, k, :], hs[k].ap[:, 512:W], k == 0, k == KD - 1, [sl, hs[k]], [b1])
                xb_ = xbs_[ch % 2]
                CP("act", xb_.ap[:], pp[:, 0:W], [b0, b1], [xb_])
                g = ch // (KA // 4)
                w_ = POOLW[g]
                if ch % (KA // 4) == 0:
                    DMA(ic[g % 2].ap[:], invcnt[g, T * i:T * (i + 1)].partition_broadcast(128), [], [ic[g % 2]], "ic%d" % (g % 2))
                t1 = t1s[ch % 2]; t2 = t2s[0]
                TT("pool", t1.ap[:, 0:W - 1], xb_.ap[:, 0:W - 1], xb_.ap[:, 1:W], ALU.add, [xb_], [t1])
                curb = t1; ln = W - 1; step = 2; other = t2
                while step < w_:
                    TT("pool", other.ap[:, 0:ln - step], curb.ap[:, 0:ln - step], curb.ap[:, step:ln], ALU.add, [curb], [other])
                    curb, other = other, curb
                    ln -= step; step *= 2
                st0 = H - w_ // 2
                TT("pool", other.ap[:, 0:T], curb.ap[:, st0:st0 + T], ic[g % 2].ap[:], ALU.mult, [curb, ic[g % 2]], [other])
                pb = A.bf(T, "pb")
                TT("pool", pb.ap[:], other.ap[:, 0:T], xb_.ap[:, H:H + T], ALU.subtract, [other, xb_], [pb])
                pbs.append(pb)
                cv.pump()
            vns = []
            NH = AW // 512
            vgels = [A.f32(AW, "vgel") for _ in range(2)]
            vsqs = [A.bf(AW, "vsq") for _ in range(1)]
            for sb_ in range(4):
                vgel = vgels[sb_ % 2]
                for hf in range(NH):
                    bk = bank()
                    for k in range(KD):
                        MM(bk.ap[:], hs[k].ap[:, H + 128 * sb_:H + 128 * (sb_ + 1)], wvv[:, hf * 4:(hf + 1) * 4, k, :], k == 0, k == KD - 1, [wv, hs[k]], [bk])
                    ACTF(vgel.ap[:, hf * 512:(hf + 1) * 512], bk.ap[:], AF.Gelu_apprx_tanh, [bk], [vgel])
                vsq = vsqs[0]
                ss = A.f32(2, "ss")
                ACTF(vsq.ap[:], vgel.ap[:], AF.Square, [vgel], [vsq], scale=float(AW) ** -0.5)
                S.op("dve", (lambda o, i_: lambda e: e.reduce_sum(out=o, in_=i_, axis=AX.X))(ss.ap[:, 0:1], vsq.ap[:]), [vsq], [ss])
                ACTF(ss.ap[:, 1:2], ss.ap[:, 0:1], AF.Sqrt, [ss, eps_t], [ss], bias=eps_t.ap[:, 0:1], scale=1.0)
                S.op("dve", (lambda o, i_: lambda e: e.reciprocal(out=o, in_=i_))(ss.ap[:, 0:1], ss.ap[:, 1:2]), [ss], [ss])
                vn = A.bf(AW, "vn")
                STT(vn.ap[:], vgel.ap[:], ss.ap[:, 0:1], vg.ap[:], ALU.mult, ALU.mult, [vgel, ss, vg], [vn])
                vns.append(vn)
            if i + 1 < NT:
                hs = rmsnorm(xt, xv, W, c.v_nmix[l], nb=nb)
                if i + 2 < NT:
                    load_xt(src, i + 2, W, slot=xt, key="xt")
            ys = []
            for ch in range(KA):
                hd = ch // (KA // 4)
                bk = bank()
                for sb_ in range(4):
                    o = bk.ap[:, 128 * sb_:128 * (sb_ + 1)]
                    MM(o, vns[sb_].ap[:, ch * 128:(ch + 1) * 128], wsb.ap[:, hd * 128:(hd + 1) * 128], True, False, [vns[sb_], wsb], [bk])
                    MM(o, on2.ap[:], bsb.ap[:, hd * 128:(hd + 1) * 128], False, True, [on2, bsb], [bk])
                ya = A.bf(T, "ya")
                TT("dve", ya.ap[:], us[ch].ap[:], bk.ap[:], ALU.mult, [us[ch], bk], [ya])
                ys.append(ya)
            for ech in range(KA):
                g = ech // NDC; eh = ech % NDC
                bk = bank()
                for dc in range(NDC):
                    MM(bk.ap[:], wgv[:, g, dc, eh * 128:(eh + 1) * 128], pbs[g * NDC + dc].ap[:], dc == 0, dc == NDC - 1, [wgb, pbs[g * NDC + dc]], [bk])
                yb = A.bf(T, "yb")
                ACTF(yb.ap[:], bk.ap[:], AF.Copy, [bk, vecs_t], [yb], scale=vcol(c.v_bscale[i_] + ech))
                ys.append(yb)
            for m in range(KD):
                wo, sl = wst.next()
                bk = bank()
                for k in range(KD):
                    MM(bk.ap[:], wo[:, k, :], ys[k].ap[:], k == 0, k == KD - 1, [sl, ys[k]], [bk])
                xr = xrs[m % 2]; ox = oxs[m % 2]
                DMA(xr.ap[:], xview(src)[:, m, H + T * i:H + T * i + T], [xtile[src][i]], [xr], "xr%d" % (m % 2))
                TT("dve", ox.ap[:], xr.ap[:], bk.ap[:], ALU.add, [xr, bk], [ox])
                DMA(xview(dst)[:, m, H + T * i:H + T * i + T], ox.ap[:], [ox], [xtile[dst][i]], "ox%d" % (m % 2))
                cv.pump()
        phase_end()

    def odd_phase1(l, src, dst):
        A.reset()
        i_ = l // 2
        ccw = c.v_ccw[i_]
        onesc = A.bf(128, "onesc")
        S.op("pool", lambda e: e.memset(onesc.ap[:], 1.0 / AW), [], [onesc])
        wd = A.bf(KA * KD * 128, "wd")
        wdv = wd.ap.rearrange("p (n k c) -> p n k c", n=KA, k=KD)
        for n in range(KA):
            DMA(wdv[:, n, :, :], WC["mi"][2 * KA + n], d_w["mi"].chunk(2 * KA + n), [wd], "c_wd")
        wst = WStream([A.bf(KD * 128, "ws") for _ in range(5)], "ws", hold=2)
        seqA = []
        for ch in range(KA):
            seqA += [("mi", ch, KD), ("mi", KA + ch, KD)]
        seqO = [("mo", m, KD, 0, KA) for m in range(KD)]
        HA = KA // 2
        wst.plan(seqA)
        for i in range(NT):
            if i + 1 < NT:
                wst.plan(seqA[:2 * HA] + seqA[2 * HA:])
            wst.plan(seqO)
        oxs[:] = [A.f32(T, "ox") for _ in range(2)]
        xrs = [A.f32(T, "xr") for _ in range(2)]
        cv.attach(NT * (KA + KD) + 16, eng="act", frac=0.3, must=must_box[0])
        xt = A.f32(KD * W, "xt")
        nb = norm_bufs(W)
        ygs = [A.bf(W, "yg") for _ in range(KA)]
        sgs = [A.f32(W, "sg") for _ in range(2)]
        xdts = [A.bf(AW, "xdt") for _ in range(2)]
        cvas = [A.f32(T, "cva") for _ in range(KA)]
        dgs = [A.bf(c.CK * 128, "dg") for _ in range(2)]
        cbs_ = [A.bf(T, "cvb") for _ in range(2)]; sqs_ = [A.bf(T, "csq") for _ in range(2)]
        mean = A.f32(T, "mean"); var = A.f32(T, "var"); msq = A.f32(T, "msq")
        ycs = [A.bf(T, "yc") for _ in range(KA)]
        NH = AW // 512

        def stage_A(hs, chs):
            for ch in chs:
                wa, sla = wst.next()
                wg, slg = wst.next()
                (a0, a1), pa = pair()
                (g0, g1), pg = pair()
                for k in range(KD):
                    MM(pa[:, 0:512], wa[:, k, :], hs[k].ap[:, 0:512], k == 0, k == KD - 1, [sla, hs[k]], [a0])
                    MM(pa[:, 512:W], wa[:, k, :], hs[k].ap[:, 512:W], k == 0, k == KD - 1, [sla, hs[k]], [a1])
                for k in range(KD):
                    MM(pg[:, 0:512], wg[:, k, :], hs[k].ap[:, 0:512], k == 0, k == KD - 1, [slg, hs[k]], [g0])
                    MM(pg[:, 512:W], wg[:, k, :], hs[k].ap[:, 512:W], k == 0, k == KD - 1, [slg, hs[k]], [g1])
                sg = sgs[ch % 2]
                ACTF(sg.ap[:], pg[:, 0:W], AF.Sigmoid, [g0, g1], [sg])
                TT("dve", ygs[ch].ap[:], pa[:, 0:W], sg.ap[:], ALU.mult, [a0, a1, sg], [ygs[ch]])
                cv.pump()

        _, xv = load_xt(src, 0, W, slot=xt, key="xt")
        hs = rmsnorm(xt, xv, W, c.v_nmix[l], nb=nb)
        if NT > 1:
            load_xt(src, 1, W, slot=xt, key="xt")
        stage_A(hs, range(KA))
        for i in range(NT):
            for sb_ in range(4):
                xdt = xdts[sb_ % 2]
                for hf in range(NH):
                    bk = bank()
                    for k in range(KD):
                        MM(bk.ap[:], hs[k].ap[:, H + 128 * sb_:H + 128 * (sb_ + 1)], wdv[:, hf * 4:(hf + 1) * 4, k, :], k == 0, k == KD - 1, [wd, hs[k]], [bk])
                    CP(alt(), xdt.ap[:, hf * 512:(hf + 1) * 512], bk.ap[:], [bk], [xdt])
                r0 = T * i + 128 * sb_
                DMA(xd_loc[r0:r0 + 128, :], xdt.ap[:], [xdt], [d_xd_loc], "st_xdt%d" % (sb_ % 2))
            bm = banks[0]; bq = banks[1]
            for ch in range(KA):
                dg = dgs[ch % 2]
                dgv = dg.ap.rearrange("p (k c) -> p k c", k=c.CK)
                wtap = vecs_t.ap[:, ccw + ch:ccw + ch + c.CK * KA:KA]
                in0_ = identb.ap[:, :].unsqueeze(1).to_broadcast([128, c.CK, 128])
                in1_ = wtap.unsqueeze(2).to_broadcast([128, c.CK, 128])
                TT("dve" if ch % 2 == 0 else "pool", dgv, in0_, in1_, ALU.mult, [identb, vecs_t], [dg])
                bk = bank()
                for kt in range(c.CK):
                    MM(bk.ap[:], dgv[:, kt, :], ygs[ch].ap[:, H - 15 + kt:H - 15 + kt + T], kt == 0, kt == c.CK - 1, [dg, ygs[ch]], [bk])
                cb_ = cbs_[ch % 2]; sq_ = sqs_[ch % 2]
                ACTF(cvas[ch].ap[:], bk.ap[:], AF.Identity, [bk, vecs_t], [cvas[ch]], bias=vcol(c.v_ccb[i_] + ch), scale=1.0)
                CP("dve", cb_.ap[:], cvas[ch].ap[:], [cvas[ch]], [cb_])
                ACTF(sq_.ap[:], cvas[ch].ap[:], AF.Square, [cvas[ch]], [sq_])
                MM(bm.ap[:], onesc.ap[:], cb_.ap[:], ch == 0, ch == KA - 1, [onesc, cb_], [bm])
                MM(bq.ap[:], onesc.ap[:], sq_.ap[:], ch == 0, ch == KA - 1, [onesc, sq_], [bq])
            CP("act", mean.ap[:], bm.ap[:], [bm], [mean])
            CP("dve", var.ap[:], bq.ap[:], [bq], [var])
            if i + 1 < NT:
                hs = rmsnorm(xt, xv, W, c.v_nmix[l], nb=nb)
                if i + 2 < NT:
                    load_xt(src, i + 2, W, slot=xt, key="xt")
                stage_A(hs, range(0, HA))
            TT("dve", msq.ap[:], mean.ap[:], mean.ap[:], ALU.mult, [mean], [msq])
            TT("dve", var.ap[:], var.ap[:], msq.ap[:], ALU.subtract, [var, msq], [var])
            S.op("dve", (lambda o: lambda e: e.tensor_scalar_max(out=o, in0=o, scalar1=0.0))(var.ap[:]), [var], [var])
            ACTF(var.ap[:], var.ap[:], AF.Sqrt, [var, eps_t], [var], bias=eps_t.ap[:, 0:1], scale=1.0)
            S.op("dve", (lambda o: lambda e: e.reciprocal(out=o, in_=o))(var.ap[:]), [var], [var])
            for ch in range(KA):
                TT("dve", cvas[ch].ap[:], cvas[ch].ap[:], mean.ap[:], ALU.subtract, [cvas[ch], mean], [cvas[ch]])
            for ch in range(KA):
                TT("dve", cvas[ch].ap[:], cvas[ch].ap[:], var.ap[:], ALU.mult, [cvas[ch], var], [cvas[ch]])
            for ch in range(KA):
                ACTF(ycs[ch].ap[:], cvas[ch].ap[:], AF.Silu, [cvas[ch], vecs_t], [ycs[ch]], scale=vcol(c.v_lng[i_] + ch), bias=vcol(c.v_lnb[i_] + ch))
            if i + 1 < NT:
                stage_A(hs, range(HA, KA))
            for m in range(KD):
                wo, sl = wst.next()
                bk = bank()
                for k in range(KA):
                    MM(bk.ap[:], wo[:, k, :], ycs[k].ap[:], k == 0, k == KA - 1, [sl, ycs[k]], [bk])
                xr = xrs[m % 2]; ox = oxs[m % 2]
                DMA(xr.ap[:], xview(src)[:, m, H + T * i:H + T * i + T], [xtile[src][i]], [xr], "xr%d" % (m % 2))
                TT("dve", ox.ap[:], xr.ap[:], bk.ap[:], ALU.add, [xr, bk], [ox])
                DMA(xview(dst)[:, m, H + T * i:H + T * i + T], ox.ap[:], [ox], [xtile[dst][i]], "ox%d" % (m % 2))
                cv.pump()
        phase_end()

    def dft_phase():
        A.reset()
        for j in range(NCH):
            S.op("pool", (lambda j: lambda e: e.collective_compute("AllGather", ALU.bypass, replica_groups=[[0, 1], [2, 3], [4, 5], [6, 7]],
                                                                    ins=[xd_loc[j * RCH:(j + 1) * RCH, :].opt()], outs=[xd_pair[j].opt()]))(j), [d_xd_loc], [d_xd_pair])
        cv.attach(N2 // min(8, N2) + 4 * 16, eng="pool", frac=0.2)
        gt = A.bf(N2 * 2 * 128, "gt")
        DMA(gt.ap[:], Gc[:, :], [], [gt], "c_gt")
        gtv = gt.ap.rearrange("p (s r k) -> p s r k", s=N2, r=2)
        f2 = A.bf(N2, "f2", parts=2 * N2)
        DMA(f2.ap[:], F2c[:, :], [], [f2], "c_f2")
        fc = A.bf(NCC * 2 * GDm, "fc")
        DMA(fc.ap[:], FCc[:, :], [], [fc], "c_fc")
        fcv = fc.ap.rearrange("p (cc r e) -> p cc r e", cc=NCC, r=2)
        SG = min(8, N2)
        xs_s = [A.bf(SG * AW, "xs") for _ in range(2)]
        bo_s = [A.bf(SG * 2 * AW, "bo") for _ in range(2)]
        PPC = RCH // N2
        NH = AW // 512
        for sg_ in range(N2 // SG):
            xs = xs_s[sg_ % 2]; bo = bo_s[sg_ % 2]
            xsv = xs.ap.rearrange("p (j c) -> p j c", j=SG)
            bov = bo.ap.rearrange("p (j r c) -> p j r c", j=SG, r=2)
            for rk in range(2):
                for j in range(NCH):
                    p0 = rk * (SC // N2) + j * PPC
                    srcv = xd_pair[j, rk * RCH:(rk + 1) * RCH, :].rearrange("(s1 s2) c -> s1 s2 c", s2=N2)
                    DMA(xsv[p0:p0 + PPC], srcv[:, sg_ * SG:(sg_ + 1) * SG, :], [d_xd_pair], [xs], "ld_xs%d" % (sg_ % 2))
            for j in range(SG):
                s2 = sg_ * SG + j
                for r in range(2):
                    for hf in range(NH):
                        bk = bank()
                        MM(bk.ap[:], gtv[:, s2, r, :], xsv[:, j, hf * 512:(hf + 1) * 512], True, True, [gt, xs], [bk])
                        CP(alt(), bov[:, j, r, hf * 512:(hf + 1) * 512], bk.ap[:], [bk], [bo])
            DMA(Bs[:, sg_ * SG:(sg_ + 1) * SG, :, :], bov, [bo], [d_Bs], "st_bo%d" % (sg_ % 2))
            cv.pump()
        KP = 2 * N2
        NK2 = N2 // 2
        bk_s = [A.bf(8 * GDm, "bkS", parts=KP) for q in range(2)]
        wgt = [A.bf(2 * SC, "wgt") for _ in range(NCC)]
        yd_s = [A.bf(T, "yd") for _ in range(2)]
        for g in range(4):
            for k1g in range(16):
                bks = bk_s[k1g % 2]
                bkv = bks.ap.rearrange("p (k c) -> p k c", k=8)
                DMA(bkv, Bs[k1g * 8:(k1g + 1) * 8, :, :, g * GDm:(g + 1) * GDm].rearrange("k s r c -> (s r) k c"), [d_Bs], [bks], "ld_bk%d" % (k1g % 2))
                for cc in range(NCC):
                    bk = bank()
                    for j in range(8):
                        MM(bk.ap[:, j * N2:(j + 1) * N2], bkv[:, j, cc * 128:(cc + 1) * 128], f2.ap[:], True, True, [bks, f2], [bk])
                    o = wgt[cc].ap.rearrange("p (r k2 k1) -> p r k2 k1", r=2, k2=NK2)[:, :, :, k1g * 8:(k1g + 1) * 8]
                    i_ap = bk.ap[:, 0:8 * N2].rearrange("p (j r k2) -> p r k2 j", j=8, r=2)
                    CP(alt(), o, i_ap, [bk], [wgt[cc]])
                cv.pump()
            for tt in range(NT):
                for eh in range(NCC):
                    bk = bank()
                    n_ = 0
                    for cc in range(NCC):
                        for r in range(2):
                            rhs = wgt[cc].ap.rearrange("p (r t) -> p r t", r=2)[:, r, tt * T:(tt + 1) * T]
                            MM(bk.ap[:], fcv[:, cc, r, eh * 128:(eh + 1) * 128], rhs, n_ == 0, n_ == 2 * NCC - 1, [fc, wgt[cc]], [bk])
                            n_ += 1
                    yd = yd_s[(tt * NCC + eh) % 2]
                    CP(alt(), yd.ap[:], bk.ap[:], [bk], [yd])
                    r0 = (g * NCC + eh) * 128
                    DMA(ydT[r0:r0 + 128, tt * T:(tt + 1) * T], yd.ap[:], [yd], [d_ydT], "st_yd%d" % ((tt * NCC + eh) % 2))
        phase_end()

    def odd_phase2(l, buf):
        A.reset()
        wst = WStream([A.bf(KD * 128, "ws") for _ in range(6)], "ws")
        for i in range(NT):
            wst.plan([("mo", m, KD, KA, KD) for m in range(KD)])
        oxs[:] = [A.f32(T, "ox") for _ in range(2)]
        cv.attach(NT * KD, eng="pool", frac=0.2)
        A.mark()
        wi = 0
        ydv = ydT.rearrange("(k p) t -> p k t", p=128)
        for i in range(NT):
            A.begin_iter(i == 0)
            xt, xv = load_xt(buf, i, T)
            yd = A.bf(KA * T, "ydl")
            ydlv = yd.ap.rearrange("p (k t) -> p k t", k=KA)
            DMA(ydlv, ydv[:, :, T * i:T * (i + 1)], [d_ydT], [yd], "ld_ydl")
            for m in range(KD):
                wo, sl = wst.next()
                bk = bank()
                for k in range(KA):
                    MM(bk.ap[:], wo[:, k, :], ydlv[:, k, :], k == 0, k == KA - 1, [sl, yd], [bk])
                out_res(xv, m, 0, bk.ap[:], bk, buf, i, [xt])
                cv.pump()
        phase_end()

    def halo_exchange(b):
        if not use_cc:
            return
        A.reset()
        xvw = xview(b)
        DMA(exin.rearrange("(k p) w -> p k w", p=128)[:, :, 0:H], xvw[:, :, H:2 * H], [xtile[b][0]], [d_ex_in], "ex_a")
        DMA(exin.rearrange("(k p) w -> p k w", p=128)[:, :, H:2 * H], xvw[:, :, SC:SC + H], [xtile[b][NT - 1]], [d_ex_in], "ex_a")
        S.op("pool", lambda e: e.collective_compute("AllGather", ALU.bypass, replica_groups=[[0, 1], [2, 3], [4, 5], [6, 7]],
                                                     ins=[exin.opt()], outs=[exout.opt()]), [d_ex_in], [d_ex_out])
        hl = A.f32(KD * H, "hl"); hr = A.f32(KD * H, "hr")
        exv = exout.rearrange("(r k p) w -> r p k w", r=2, p=128)
        DMA(hl.ap.rearrange("p (k h) -> p k h", k=KD), exv[0][:, :, H:2 * H], [d_ex_out], [hl], "ex_hl")
        DMA(hr.ap.rearrange("p (k h) -> p k h", k=KD), exv[1][:, :, 0:H], [d_ex_out], [hr], "ex_hr")
        TS("dve", hl.ap[:], hl.ap[:], vcol(c.v_mask + 0), None, ALU.mult, None, [hl, vecs_t], [hl])
        TS("dve", hr.ap[:], hr.ap[:], vcol(c.v_mask + 1), None, ALU.mult, None, [hr, vecs_t], [hr])
        DMA(xvw[:, :, 0:H], hl.ap.rearrange("p (k h) -> p k h", k=KD), [hl], [xhalo[b][0]], "ex_sl")
        DMA(xvw[:, :, H + SC:H + SC + H], hr.ap.rearrange("p (k h) -> p k h", k=KD), [hr], [xhalo[b][1]], "ex_sr")
        phase_end()

    def epilogue(src):
        A.reset()
        yo = [A.f32(D, "yo") for _ in range(2)]
        xts = [A.f32(KD * T, "xt") for _ in range(2)]
        nbs = [norm_bufs(T, F32, "hf") for _ in range(2)]
        nxt = load_xt(src, 0, T, slot=xts[0], key="xt0")
        for i in range(NT):
            xt, xv = nxt
            if i + 1 < NT:
                nxt = load_xt(src, i + 1, T, slot=xts[(i + 1) % 2], key="xt%d" % ((i + 1) % 2))
            hs = rmsnorm(xt, xv, T, c.v_nfin, out_dt=F32, name="hf", nb=nbs[i % 2])
            for sb_ in range(4):
                y_ = yo[sb_ % 2]
                for kg in range(KD // 4):
                    bk = bank()
                    for kk in range(4):
                        k = kg * 4 + kk
                        S.op("pe", (lambda o, i_: lambda e: e.transpose(o, i_, ident.ap[:]))(bk.ap[:, kk * 128:(kk + 1) * 128], hs[k].ap[:, sb_ * 128:(sb_ + 1) * 128]), [hs[k], ident], [bk])
                    CP(alt(), y_.ap[:, kg * 512:(kg + 1) * 512], bk.ap[:], [bk], [y_])
                r0 = T * i + 128 * sb_
                DMA(y_out[r0:r0 + 128, :], y_.ap[:], [y_], [], "st_y%d" % (sb_ % 2))
        phase_end()

    halo_exchange(cur)
    cvt_flush_phase(leave=len(cv.queue) - max(0, n_mix0 - cv.done))
    n_ffn0 = len(cv.queue)
    for l in range(c.depth):
        set_layer_weights(l)
        if l + 1 < c.depth:
            add_mix(l + 1); add_ffn(l + 1)
        n_next = len(cv.queue) - (n_ffn0 if l == 0 else 0)
        must_box[0] = n_ffn0 if l == 0 else 0
        if l % 2 == 0:
            even_phase(l, cur, 1 - cur)
            cur = 1 - cur
        else:
            odd_phase1(l, cur, 1 - cur)
            cur = 1 - cur
            dft_phase()
            odd_phase2(l, cur)
        if l == 0:
            cvt_flush_phase(leave=n_next)
        halo_exchange(cur)
        ffn_phase(l, cur, 1 - cur)
        cur = 1 - cur
        cvt_flush_phase()
        if l < c.depth - 1:
            halo_exchange(cur)
    epilogue(cur)
    S.run()
    return nc, S


def _col(v):
    v = np.asarray(v, np.float32)
    return v.reshape(-1, 128).T


def dft_consts(cfg, is_prompt, h):
    N2 = cfg.N2; SC = cfg.SC; NK2 = N2 // 2
    GD = cfg.AW // 4; NCC = GD // 128
    s1 = np.arange(128, dtype=np.float64)[:, None]; k1 = np.arange(128, dtype=np.float64)[None, :]
    G = np.zeros((128, N2, 2, 128), np.float64)
    RPC = SC // N2
    for s2 in range(N2):
        if is_prompt:
            Sq = 2 * SC
            th = 2 * np.pi * (k1 * s1 / 128.0 + k1 * s2 / Sq)
            valid = np.ones_like(th)
        else:
            Sq = SC
            s1p = s1 - RPC * h
            valid = ((s1p >= 0) & (s1p < RPC)).astype(np.float64) * np.ones_like(k1)
            th = 2 * np.pi * (k1 * s1p / RPC + k1 * s2 / SC)
        G[:, s2, 0, :] = np.cos(th) * valid
        G[:, s2, 1, :] = -np.sin(th) * valid
    s2 = np.arange(N2, dtype=np.float64)[:, None]; k2 = np.arange(NK2, dtype=np.float64)[None, :]
    if is_prompt:
        ph = 2 * np.pi * (k2 + NK2 * h) * s2 / N2
    else:
        ph = 2 * np.pi * 2 * k2 * s2 / N2
    Fr = np.cos(ph); Fi = -np.sin(ph)
    F2 = np.zeros((2 * N2, N2), np.float64)
    F2[0::2, 0:NK2] = Fr; F2[1::2, 0:NK2] = -Fi
    F2[0::2, NK2:] = Fi; F2[1::2, NK2:] = Fr
    cc_ = np.arange(GD, dtype=np.float64)[:, None]; cp = np.arange(GD, dtype=np.float64)[None, :]
    ps = 2 * np.pi * cc_ * cp / GD
    nrm = 1.0 / np.sqrt(Sq * GD)
    C = np.cos(ps) * nrm; Sn = np.sin(ps) * nrm
    FC = np.zeros((128, NCC, 2, GD), np.float64)
    for q in range(NCC):
        FC[:, q, 0, :] = C[q * 128:(q + 1) * 128, :]
        FC[:, q, 1, :] = Sn[q * 128:(q + 1) * 128, :]
    bf = ml_dtypes.bfloat16
    return (G.reshape(128, -1).astype(np.float32).astype(bf), F2.astype(np.float32).astype(bf),
            FC.reshape(128, -1).astype(np.float32).astype(bf))


def make_in_maps(cfg, inp, n_prompt_seq=2, n_sample_seq=4):
    c = cfg
    depth = c.depth
    vecs = np.zeros((128, c.NV), np.float32)
    for l in range(depth):
        vecs[:, c.v_nmix[l]:c.v_nmix[l] + c.KD] = _col(inp["norm_mix"][l])
        vecs[:, c.v_nffn[l]:c.v_nffn[l] + c.KD] = _col(inp["norm_ffn"][l])
        fw = np.asarray(inp["f_conv_w"][l], np.float32)
        for k in range(3):
            vecs[:, c.v_fw[l] + k * c.KF:c.v_fw[l] + (k + 1) * c.KF] = _col(fw[k])
        vecs[:, c.v_fb[l]:c.v_fb[l] + c.KF] = _col(inp["f_conv_b"][l])
    vecs[:, c.v_nfin:c.v_nfin + c.KD] = _col(inp["norm_final"])
    for i in range(c.NE):
        vecs[:, c.v_bscale[i]:c.v_bscale[i] + c.KA] = _col(inp["b_scale"][i])
    for i in range(c.NO):
        cw = np.asarray(inp["c_conv_w"][i], np.float32)
        for k in range(c.CK):
            vecs[:, c.v_ccw[i] + k * c.KA:c.v_ccw[i] + (k + 1) * c.KA] = _col(cw[k])
        vecs[:, c.v_ccb[i]:c.v_ccb[i] + c.KA] = _col(inp["c_conv_b"][i])
        vecs[:, c.v_lng[i]:c.v_lng[i] + c.KA] = _col(inp["c_ln_g"][i])
        vecs[:, c.v_lnb[i]:c.v_lnb[i] + c.KA] = _col(inp["c_ln_b"][i])
    shared = {
        "ident": np.eye(128, dtype=np.float32),
        "ab_w_in": np.ascontiguousarray(inp["ab_w_in"], np.float32), "ab_w_out": np.ascontiguousarray(inp["ab_w_out"], np.float32),
        "cd_w_in": np.ascontiguousarray(inp["cd_w_in"], np.float32), "cd_w_out": np.ascontiguousarray(inp["cd_w_out"], np.float32),
        "f_w_in": np.ascontiguousarray(inp["f_w_in"], np.float32), "f_w_out": np.ascontiguousarray(inp["f_w_out"], np.float32),
        "a_v_gain": np.ascontiguousarray(inp["a_v_gain"], np.float32),
        "a_w_sT": np.ascontiguousarray(np.transpose(np.asarray(inp["a_w_s"], np.float32), (0, 1, 3, 2))),
        "a_b_s": np.ascontiguousarray(np.asarray(inp["a_b_s"], np.float32).reshape(c.NE, 512)),
        "b_w_g": np.ascontiguousarray(inp["b_w_g"], np.float32),
    }
    xp = np.asarray(inp["x_prompt"], np.float32); xs = np.asarray(inp["x_sample"], np.float32)
    maps = []
    core = 0
    plan = []
    for b in range(n_prompt_seq):
        for h in range(2):
            plan.append((True, b, h))
    for b in range(n_sample_seq):
        plan.append((False, b, len(plan) % 2))
    for (is_p, b, h) in plan:
        m = dict(shared)
        if is_p:
            m["x"] = np.ascontiguousarray(xp[b, h * c.SC:(h + 1) * c.SC, :]); Sq = 2 * c.SC; off = h * c.SC
        else:
            m["x"] = np.ascontiguousarray(xs[b]); Sq = c.SC; off = 0
        v = vecs.copy()
        v[:, c.v_mask + 0] = 1.0 if (is_p and h == 1) else 0.0
        v[:, c.v_mask + 1] = 1.0 if (is_p and h == 0) else 0.0
        m["vecs"] = v
        t = off + np.arange(c.SC)
        ic = np.zeros((4, c.SC), np.float32)
        for g, w in enumerate(POOLW):
            lo = np.maximum(t - w // 2, 0); hi = np.minimum(t + w // 2, Sq)
            ic[g] = 1.0 / (hi - lo).astype(np.float32)
        m["invcnt"] = ic
        G, F2, FC = dft_consts(c, is_p, h)
        m["Gc"] = G; m["F2c"] = F2; m["FCc"] = FC
        maps.append(m)
    return maps, plan


def assemble(cfg, results, plan, n_prompt_seq=2, n_sample_seq=4):
    c = cfg
    yp = np.zeros((n_prompt_seq, 2 * c.SC, c.D), np.float32)
    ys = np.zeros((n_sample_seq, c.SC, c.D), np.float32)
    for r, (is_p, b, h) in zip(results, plan):
        if is_p:
            yp[b, h * c.SC:(h + 1) * c.SC] = r["y"]
        else:
            ys[b] = r["y"]
    return yp, ys


_CFG = Cfg()
_CACHE = {}


def kernel(**inputs):
    cfg = _CFG
    if "nc" not in _CACHE:
        _CACHE["nc"] = build(cfg)[0]
    nc = _CACHE["nc"]
    maps, plan = make_in_maps(cfg, inputs)
    res = run_bass_kernel_spmd(nc, maps, core_ids=list(range(8)))
    yp, ys = assemble(cfg, res.results, plan)
    return (yp, ys)
```
